# Optimizing a Trainium2 kernel written in Bass

```python
import math
import jax
import jax.numpy as jnp
from jax import lax
import numpy as np


D_MODEL = 2048
BATCH = 4
SEQ = 4096
DEPTH = 2
DEC_BATCH = 16
DEC_SEQ = 2048
PAST_LEN = 128

GRID_W = 64
Q_BLOCK = 128
HEAD_DIM = 128
SSM_WIDTH = D_MODEL // 2
SSM_GROUP = 16
SSM_GROUPS = SSM_WIDTH // SSM_GROUP
SSM_STATE = 64
ATT_WIDTH = D_MODEL - SSM_WIDTH
B_HEADS = ATT_WIDTH // HEAD_DIM
B_KV_HEADS = max(1, B_HEADS // 4)
B_GROUP = B_HEADS // B_KV_HEADS
AXIAL_THETA = 10000.0
KV_WIDTH = B_KV_HEADS * HEAD_DIM
EVEN_SPLITS = (SSM_WIDTH, 2 * SSM_WIDTH, 2 * SSM_WIDTH + ATT_WIDTH, 2 * SSM_WIDTH + ATT_WIDTH + KV_WIDTH, 2 * SSM_WIDTH + ATT_WIDTH + 2 * KV_WIDTH)
EVEN_IN = 2 * SSM_WIDTH + 2 * ATT_WIDTH + 2 * KV_WIDTH
EVEN_MIX = SSM_WIDTH + ATT_WIDTH
C_HEADS = D_MODEL // HEAD_DIM
C_WIDTH = C_HEADS * HEAD_DIM
C_WINDOWS = (128, 512, 2048)
C_DILATIONS = (1, 4, 16)
C_PAD = max(C_WINDOWS) // 2
ODD_IN = 4 * C_WIDTH
ROPE_THETA = 500000.0
ROPE_DIMS = HEAD_DIM // 4
N_EVEN = (DEPTH + 1) // 2
N_ODD = DEPTH // 2
DN_ALPHA = (2 * DEPTH) ** 0.25
DN_BETA = (8 * DEPTH) ** -0.25
LN_EPS = 1e-5
QK_EPS = 1e-6

kernel_name = "hybrid_s5_axialgqa_dilated_encoder"


def _layernorm(x, g, b):
    xf = x.astype(jnp.float32)
    mu = jnp.mean(xf, axis=-1, keepdims=True)
    var = jnp.mean(jnp.square(xf - mu), axis=-1, keepdims=True)
    return ((xf - mu) * lax.rsqrt(var + LN_EPS) * g.astype(jnp.float32) + b.astype(jnp.float32)).astype(x.dtype)


def _rmsnorm(x, g):
    xf = x.astype(jnp.float32)
    ms = jnp.mean(jnp.square(xf), axis=-1, keepdims=True)
    return (xf * lax.rsqrt(ms + QK_EPS) * g.astype(jnp.float32)).astype(x.dtype)


def _rotary(x, pos, theta):
    n = x.shape[-1]
    inv = theta ** (-jnp.arange(0, n, 2, dtype=jnp.float32) / n)
    ang = pos[:, None] * inv[None, :]
    shp = (ang.shape[0],) + (1,) * (x.ndim - 3) + (n // 2,)
    cos = jnp.cos(ang).reshape(shp)
    sin = jnp.sin(ang).reshape(shp)
    xf = x.astype(jnp.float32)
    x1, x2 = xf[..., : n // 2], xf[..., n // 2 :]
    return jnp.concatenate([x1 * cos - x2 * sin, x2 * cos + x1 * sin], axis=-1).astype(x.dtype)


def _complex_combine(e1, e2):
    a1r, a1i, b1r, b1i = e1
    a2r, a2i, b2r, b2i = e2
    ar = a1r * a2r - a1i * a2i
    ai = a1r * a2i + a1i * a2r
    br = a2r * b1r - a2i * b1i + b2r
    bi = a2r * b1i + a2i * b1r + b2i
    return (ar, ai, br, bi)


def _s5_mixer(u, a_re, a_im, log_dt, b_re, b_im, c_re, c_im, d_skip, glu_w, glu_b):
    bsz, t_len, _ = u.shape
    uf = u.astype(jnp.float32).reshape(bsz, t_len, SSM_GROUPS, SSM_GROUP)
    ar_c = a_re.astype(jnp.float32)
    ai_c = a_im.astype(jnp.float32)
    dt = jnp.exp(log_dt.astype(jnp.float32))[..., None]
    mag = jnp.exp(ar_c * dt)
    lb_re = mag * jnp.cos(ai_c * dt)
    lb_im = mag * jnp.sin(ai_c * dt)
    nr = lb_re - 1.0
    den = jnp.square(ar_c) + jnp.square(ai_c)
    f_re = (nr * ar_c + lb_im * ai_c) / den
    f_im = (lb_im * ar_c - nr * ai_c) / den
    br = b_re.astype(jnp.float32)
    bi = b_im.astype(jnp.float32)
    bb_re = f_re[..., None] * br - f_im[..., None] * bi
    bb_im = f_re[..., None] * bi + f_im[..., None] * br
    y = d_skip.astype(jnp.float32).reshape(SSM_GROUPS, SSM_GROUP) * uf
    for direction, rev in ((0, False), (1, True)):
        xr_in = jnp.einsum('btgc,gpc->btgp', uf, bb_re[direction])
        xi_in = jnp.einsum('btgc,gpc->btgp', uf, bb_im[direction])
        ar = jnp.broadcast_to(lb_re[direction][None, None], (1, t_len, SSM_GROUPS, SSM_STATE))
        ai = jnp.broadcast_to(lb_im[direction][None, None], (1, t_len, SSM_GROUPS, SSM_STATE))
        _, _, xr, xi = lax.associative_scan(_complex_combine, (ar, ai, xr_in, xi_in), reverse=rev, axis=1)
        y = y + jnp.einsum('btgp,gcp->btgc', xr, c_re[direction].astype(jnp.float32)) - jnp.einsum('btgp,gcp->btgc', xi, c_im[direction].astype(jnp.float32))
    y = jax.nn.gelu(y.reshape(bsz, t_len, SSM_WIDTH))
    y = y * jax.nn.sigmoid(y @ glu_w.astype(jnp.float32) + glu_b.astype(jnp.float32))
    return y.astype(u.dtype)


def _axial_gqa(q, k, v, qn_g, kn_g, row, col):
    bsz, t_len, _ = q.shape
    q = _rmsnorm(q.reshape(bsz, t_len, B_KV_HEADS, B_GROUP, HEAD_DIM), qn_g)
    k = _rmsnorm(k.reshape(bsz, t_len, B_KV_HEADS, HEAD_DIM), kn_g)
    v = v.reshape(bsz, t_len, B_KV_HEADS, HEAD_DIM)
    half = HEAD_DIM // 2
    q = jnp.concatenate([_rotary(q[..., :half], row, AXIAL_THETA), _rotary(q[..., half:], col, AXIAL_THETA)], axis=-1)
    k = jnp.concatenate([_rotary(k[..., :half], row, AXIAL_THETA), _rotary(k[..., half:], col, AXIAL_THETA)], axis=-1)
    nb = t_len // Q_BLOCK
    qb = q.reshape(bsz, nb, Q_BLOCK, B_KV_HEADS, B_GROUP, HEAD_DIM).transpose(1, 0, 2, 3, 4, 5)
    kf = k.astype(jnp.float32)
    vf = v.astype(jnp.float32)
    scale = HEAD_DIM ** -0.5

    def block(qblk):
        s = jnp.einsum('bqhge,bkhe->bhgqk', qblk.astype(jnp.float32), kf) * scale
        p = jax.nn.softmax(s, axis=-1)
        return jnp.einsum('bhgqk,bkhe->bqhge', p, vf)

    o = lax.map(block, qb)
    return o.transpose(1, 0, 2, 3, 4, 5).reshape(bsz, t_len, ATT_WIDTH).astype(v.dtype)


def _dilated_mixture(q, k, v, pos):
    bsz, t_len, _ = q.shape
    q = q.reshape(bsz, t_len, C_HEADS, HEAD_DIM)
    k = k.reshape(bsz, t_len, C_HEADS, HEAD_DIM)
    v = v.reshape(bsz, t_len, C_HEADS, HEAD_DIM)
    q = jnp.concatenate([_rotary(q[..., :ROPE_DIMS], pos, ROPE_THETA), q[..., ROPE_DIMS:]], axis=-1)
    k = jnp.concatenate([_rotary(k[..., :ROPE_DIMS], pos, ROPE_THETA), k[..., ROPE_DIMS:]], axis=-1)
    kp = jnp.pad(k.astype(jnp.float32), ((0, 0), (C_PAD, C_PAD), (0, 0), (0, 0)))
    vp = jnp.pad(v.astype(jnp.float32), ((0, 0), (C_PAD, C_PAD), (0, 0), (0, 0)))
    nb = t_len // Q_BLOCK
    qb = q.astype(jnp.float32).reshape(bsz, nb, Q_BLOCK, C_HEADS, HEAD_DIM).transpose(1, 0, 2, 3, 4)
    q0s = jnp.arange(nb, dtype=jnp.int32) * Q_BLOCK
    scale = HEAD_DIM ** -0.5

    def block(args):
        qblk, q0 = args
        outs = []
        lses = []
        for w, d in zip(C_WINDOWS, C_DILATIONS):
            hw = w // 2
            seg = Q_BLOCK + 2 * hw
            na = Q_BLOCK // d
            nc = seg // d
            n_span = 2 * hw // d
            start = q0 + (C_PAD - hw)
            ks = lax.dynamic_slice_in_dim(kp, start, seg, axis=1).reshape(bsz, nc, d, C_HEADS, HEAD_DIM)
            vs = lax.dynamic_slice_in_dim(vp, start, seg, axis=1).reshape(bsz, nc, d, C_HEADS, HEAD_DIM)
            qs = qblk.reshape(bsz, na, d, C_HEADS, HEAD_DIM)
            s = jnp.einsum('barne,bcrne->brnac', qs, ks) * scale
            a_i = jnp.arange(na)[:, None]
            c_i = jnp.arange(nc)[None, :]
            rel = c_i - a_i
            in_win = (rel >= 0) & (rel <= n_span)
            kpos = q0 - hw + c_i[None] * d + jnp.arange(d)[:, None, None]
            valid = (kpos >= 0) & (kpos < t_len)
            mask = (in_win[None] & valid)[None, :, None]
            s = jnp.where(mask, s, -jnp.inf)
            m = jnp.max(s, axis=-1, keepdims=True)
            p = jnp.exp(s - m)
            l = jnp.sum(p, axis=-1)
            o = jnp.einsum('brnac,bcrne->barne', p, vs) / l.transpose(0, 3, 1, 2)[..., None]
            lse = (m[..., 0] + jnp.log(l)).transpose(0, 3, 1, 2)
            outs.append(o.reshape(bsz, Q_BLOCK, C_HEADS, HEAD_DIM))
            lses.append(lse.reshape(bsz, Q_BLOCK, C_HEADS))
        wts = jax.nn.softmax(jnp.stack(lses, axis=0), axis=0)
        return jnp.sum(wts[..., None] * jnp.stack(outs, axis=0), axis=0)

    o = lax.map(block, (qb, q0s))
    return o.transpose(1, 0, 2, 3, 4).reshape(bsz, t_len, C_WIDTH).astype(v.dtype)


def setup_inputs(seed: int = 0) -> dict:
    key = jax.random.key(seed)
    ks = jax.random.split(key, 24)
    f32 = jnp.float32
    x_prompt = jax.random.normal(ks[0], (BATCH, SEQ, D_MODEL), f32)
    x_sample = jax.random.normal(ks[1], (DEC_BATCH, DEC_SEQ, D_MODEL), f32)
    even_w_in = jax.random.normal(ks[2], (N_EVEN, D_MODEL, EVEN_IN), f32) * D_MODEL ** -0.5
    even_w_out = jax.random.normal(ks[3], (N_EVEN, EVEN_MIX, D_MODEL), f32) * (EVEN_MIX ** -0.5 * DN_BETA)
    ssm_a_re = -0.5 + 0.01 * jax.random.normal(ks[4], (N_EVEN, 2, SSM_GROUPS, SSM_STATE), f32)
    ssm_a_im = jnp.pi * jnp.arange(SSM_STATE, dtype=f32) + 0.01 * jax.random.normal(ks[5], (N_EVEN, 2, SSM_GROUPS, SSM_STATE), f32)
    ssm_log_dt = jax.random.uniform(ks[6], (N_EVEN, 2, SSM_GROUPS), f32, minval=math.log(1e-3), maxval=math.log(1e-1))
    ssm_b_re = jax.random.normal(ks[7], (N_EVEN, SSM_GROUPS, SSM_STATE, SSM_GROUP), f32) * (2 * SSM_GROUP) ** -0.5
    ssm_b_im = jax.random.normal(ks[8], (N_EVEN, SSM_GROUPS, SSM_STATE, SSM_GROUP), f32) * (2 * SSM_GROUP) ** -0.5
    ssm_c_re = jax.random.normal(ks[9], (N_EVEN, 2, SSM_GROUPS, SSM_GROUP, SSM_STATE), f32) * SSM_STATE ** -0.5
    ssm_c_im = jax.random.normal(ks[10], (N_EVEN, 2, SSM_GROUPS, SSM_GROUP, SSM_STATE), f32) * SSM_STATE ** -0.5
    ssm_d = jax.random.normal(ks[11], (N_EVEN, SSM_WIDTH), f32)
    ssm_glu_w = jax.random.normal(ks[12], (N_EVEN, SSM_WIDTH, SSM_WIDTH), f32) * SSM_WIDTH ** -0.5
    ssm_glu_b = 0.01 * jax.random.normal(ks[13], (N_EVEN, SSM_WIDTH), f32)
    attn_q_norm = 1.0 + 0.01 * jax.random.normal(ks[14], (N_EVEN, HEAD_DIM), f32)
    attn_k_norm = 1.0 + 0.01 * jax.random.normal(ks[15], (N_EVEN, HEAD_DIM), f32)
    odd_w_in = jax.random.normal(ks[16], (N_ODD, D_MODEL, ODD_IN), f32) * D_MODEL ** -0.5
    odd_w_out = jax.random.normal(ks[17], (N_ODD, C_WIDTH, D_MODEL), f32) * (C_WIDTH ** -0.5 * DN_BETA)
    ln_g = 1.0 + 0.01 * jax.random.normal(ks[18], (DEPTH, D_MODEL), f32)
    ln_b = 0.01 * jax.random.normal(ks[19], (DEPTH, D_MODEL), f32)
    return {"x_prompt": x_prompt, "x_sample": x_sample, "even_w_in": even_w_in, "even_w_out": even_w_out, "ssm_a_re": ssm_a_re, "ssm_a_im": ssm_a_im, "ssm_log_dt": ssm_log_dt, "ssm_b_re": ssm_b_re, "ssm_b_im": ssm_b_im, "ssm_c_re": ssm_c_re, "ssm_c_im": ssm_c_im, "ssm_d": ssm_d, "ssm_glu_w": ssm_glu_w, "ssm_glu_b": ssm_glu_b, "attn_q_norm": attn_q_norm, "attn_k_norm": attn_k_norm, "odd_w_in": odd_w_in, "odd_w_out": odd_w_out, "ln_g": ln_g, "ln_b": ln_b}


def reference(x_prompt, x_sample, even_w_in, even_w_out, ssm_a_re, ssm_a_im, ssm_log_dt, ssm_b_re, ssm_b_im, ssm_c_re, ssm_c_im, ssm_d, ssm_glu_w, ssm_glu_b, attn_q_norm, attn_k_norm, odd_w_in, odd_w_out, ln_g, ln_b):
    def trunk(x):
        t_len = x.shape[1]
        rows = t_len // GRID_W
        grid_r, grid_c = jnp.meshgrid(jnp.arange(rows, dtype=jnp.float32), jnp.arange(GRID_W, dtype=jnp.float32), indexing='ij')
        row = grid_r.reshape(-1)
        col = grid_c.reshape(-1)
        pos = jnp.arange(t_len, dtype=jnp.float32)
        for layer in range(DEPTH):
            i = layer // 2
            if layer % 2 == 0:
                h = x @ even_w_in[i]
                a_u, a_g, q, k, v, b_g = jnp.split(h, EVEN_SPLITS, axis=-1)
                ya = _s5_mixer(a_u, ssm_a_re[i], ssm_a_im[i], ssm_log_dt[i], ssm_b_re[i], ssm_b_im[i], ssm_c_re[i], ssm_c_im[i], ssm_d[i], ssm_glu_w[i], ssm_glu_b[i])
                yb = _axial_gqa(q, k, v, attn_q_norm[i], attn_k_norm[i], row, col)
                y = jnp.concatenate([ya * jax.nn.silu(a_g), yb * jax.nn.silu(b_g)], axis=-1) @ even_w_out[i]
            else:
                h = x @ odd_w_in[i]
                q, k, v, g = jnp.split(h, 4, axis=-1)
                y = (_dilated_mixture(q, k, v, pos) * jax.nn.silu(g)) @ odd_w_out[i]
            x = _layernorm(DN_ALPHA * x + y, ln_g[layer], ln_b[layer])
        return x

    y_prompt = trunk(x_prompt)
    y_sample = trunk(x_sample)
    return (y_prompt, y_sample)
```

```python
import contextlib
import math
import numpy as np
import ml_dtypes
import concourse.bass as bass
import concourse.mybir as mybir
from concourse.bass_utils import run_bass_kernel_spmd
from concourse.ap import AP

F32 = mybir.dt.float32
BF16 = mybir.dt.bfloat16
AF = mybir.ActivationFunctionType
ALU = mybir.AluOpType

D = 2048
T = 2048
NSEG = 3
KC = 16
TT = 512
EVEN_IN = 4608
ODD_IN = 8192
G = 64
GB = 16
GBP = 8
ALPHA = 4 ** 0.25
LN_EPS = 1e-5
QK_EPS = 1e-6
NEG = -30000.0
SEM_WRAP = 24000


class Tok:
    __slots__ = ("sid", "sem", "val")

    def __init__(self, sid, sem, val):
        self.sid, self.sem, self.val = sid, sem, val


class Buf:
    __slots__ = ("name", "w", "r", "excl")

    def __init__(self, name="", excl=False):
        self.name = name
        self.w = None
        self.r = {}
        self.excl = excl


class Eng:
    def __init__(self, kb, name, h):
        self.kb, self.name, self.h = kb, name, h
        self.sem = None
        self.sid = None
        self.cnt = 0
        self.seen = {}
        self.last = None


class KB:
    def __init__(self, nc):
        self.nc = nc
        self.es = contextlib.ExitStack()
        self.nsem = 0
        self.e = {
            "pe": Eng(self, "pe", nc.tensor),
            "act": Eng(self, "act", nc.scalar),
            "dve": Eng(self, "dve", nc.vector),
            "pool": Eng(self, "pool", nc.gpsimd),
            "sp": Eng(self, "sp", nc.sync),
        }
        self.dsems = []
        self.dnext = 0
        self.NDS = 40
        self.all_dma_toks = {}

    def new_sem(self):
        s = self.es.enter_context(self.nc.semaphore("s%d" % self.nsem))
        self.nsem += 1
        return (self.nsem, s)

    def wait(self, en, tok):
        if tok is None:
            return
        e = self.e[en]
        if e.seen.get(tok.sid, 0) >= tok.val:
            return
        e.h.wait_ge(tok.sem, tok.val)
        e.seen[tok.sid] = tok.val

    def signal(self, en, inst):
        e = self.e[en]
        if e.sem is None or e.cnt >= SEM_WRAP:
            e.sid, e.sem = self.new_sem()
            e.cnt = 0
        inst.then_inc(e.sem, 1)
        e.cnt += 1
        t = Tok(e.sid, e.sem, e.cnt)
        e.last = t
        return t

    def _deps(self, en, reads, writes):
        for b in reads:
            if b.w is not None and not (en == "pe" and b.w.sid == self.e["pe"].sid):
                self.wait(en, b.w)
            if b.excl:
                for t in b.r.values():
                    if t.sid != self.e[en].sid:
                        self.wait(en, t)
        for b in writes:
            if b.w is not None and not (en == "pe" and b.w.sid == self.e["pe"].sid):
                self.wait(en, b.w)
            for t in b.r.values():
                if not (en == "pe" and t.sid == self.e["pe"].sid):
                    self.wait(en, t)

    def _mark(self, tok, reads, writes):
        for b in reads:
            b.r[tok.sid] = tok
        for b in writes:
            b.w = tok
            b.r = {}

    def op(self, en, fn, reads=(), writes=()):
        self._deps(en, reads, writes)
        inst = fn(self.e[en].h)
        tok = self.signal(en, inst)
        self._mark(tok, reads, writes)
        return tok

    def group(self, en, fns, reads=(), writes=()):
        self._deps(en, reads, writes)
        inst = None
        for fn in fns:
            inst = fn(self.e[en].h)
        tok = self.signal(en, inst)
        self._mark(tok, reads, writes)
        return tok

    def dma(self, out, in_, reads=(), writes=(), q="sp", slow=False):
        self._deps(q, reads, writes)
        if len(self.dsems) < self.NDS:
            sid, sem = self.new_sem()
            ent = [sid, sem, 0, None]
            self.dsems.append(ent)
        else:
            ent = self.dsems[self.dnext % self.NDS]
        self.dnext += 1
        if ent[3] is not None:
            self.wait(q, ent[3])
        if slow:
            self.e[q].h.dma_start(out=out, in_=in_, allow_slow_non_contiguous=True).then_inc(ent[1], 16)
        else:
            self.e[q].h.dma_start(out=out, in_=in_).then_inc(ent[1], 16)
        ent[2] += 16
        tok = Tok(ent[0], ent[1], ent[2])
        ent[3] = tok
        self.all_dma_toks[ent[0]] = tok
        self._mark(tok, reads, writes)
        return tok

    def barrier(self):
        toks = [e.last for e in self.e.values() if e.last is not None] + list(self.all_dma_toks.values())
        for en in self.e:
            for t in toks:
                if en == "pe" and t.sid == self.e["pe"].sid:
                    continue
                self.wait(en, t)

    def finish(self):
        toks = [e.last for e in self.e.values() if e.last is not None] + list(self.all_dma_toks.values())
        for t in toks:
            self.wait("sp", t)


def dap(base, off, dims):
    return AP(base.tensor, base.offset + off, [list(d) for d in dims])


def rev_last(ap, n):
    dims = [list(d) for d in ap.ap]
    assert dims[-1][0] == 1
    dims[-1] = [-1, n]
    return AP(ap.tensor, ap.offset + (n - 1), dims)


class Prog:
    def __init__(self, T_=T, dbg=()):
        self.T = T_
        self.NTT = T_ // TT
        self.NB = T_ // 128
        self.NCH = T_ // 8
        self.dbg = set(dbg)
        nc = bass.Bass("TRN2", target_bir_lowering=False)
        self.nc = nc
        self.kb = KB(nc)
        Tn = T_

        def din(name, shape, dt=F32):
            return nc.dram_tensor(name, list(shape), dt, kind="ExternalInput").ap()

        def dscr(name, shape, dt=BF16):
            kind = "ExternalOutput" if name in self.dbg else "Internal"
            return nc.dram_tensor(name, list(shape), dt, kind=kind).ap()

        self.x = din("x", [NSEG, Tn, D])
        self.y = nc.dram_tensor("y", [NSEG, Tn, D], F32, kind="ExternalOutput").ap()
        self.w_in_e = din("w_in_e", [D, EVEN_IN])
        self.w_out_e = din("w_out_e", [D, D])
        self.w_glu = din("w_glu", [1024, 1024])
        self.b_glu = din("b_glu", [1024])
        self.w_in_o = din("w_in_o", [D, ODD_IN])
        self.w_out_o = din("w_out_o", [D, D])
        self.ln_g = din("ln_g", [2, D])
        self.ln_b = din("ln_b", [2, D])
        self.a_re = din("a_re", [2, G, 64])
        self.a_im = din("a_im", [2, G, 64])
        self.log_dt = din("log_dt", [2, G])
        self.b_re = din("b_re", [G, 64, 16])
        self.b_im = din("b_im", [G, 64, 16])
        self.c_re = din("c_re", [2, G, 16, 64])
        self.c_im = din("c_im", [2, G, 16, 64])
        self.d_skip = din("d_skip", [1024])
        self.gq = din("gq", [128])
        self.gk = din("gk", [128])
        self.cosE = din("cosE", [NSEG, 128, Tn])
        self.sinE = din("sinE", [NSEG, 128, Tn])
        self.cosO = din("cosO", [NSEG, 128, Tn])
        self.sinO = din("sinO", [NSEG, 128, Tn])
        self.c_identf = din("identf", [128, 128])
        self.c_identb = din("identb", [128, 128], BF16)
        self.c_rotE = din("rotE", [128, 128], BF16)
        self.c_rotO = din("rotO", [128, 128], BF16)
        self.c_dmask = din("dmask", [20, 128, 512], BF16)
        self.c_mmask = din("mmask", [2, 128, 128])
        self.c_flag = din("flag", [128, 2])
        self.UT = dscr("UT", [NSEG, 1024, Tn])
        self.AG = dscr("AG", [NSEG, 1024, Tn])
        self.QT = dscr("QT", [NSEG, 8, 128, Tn])
        self.KT = dscr("KT", [NSEG, 2, 128, Tn])
        self.VV = dscr("VV", [NSEG, Tn, 256])
        self.BG = dscr("BG", [NSEG, 1024, Tn])
        self.YT = dscr("YT", [NSEG, 1024, Tn])
        self.ZT = dscr("ZT", [NSEG, D, Tn])
        self.X1 = dscr("X1", [NSEG, Tn, D], F32)
        self.X1T = dscr("X1T", [NSEG, D, Tn])
        self.Q2T = dscr("Q2T", [NSEG, 16, 128, Tn])
        self.K2T = dscr("K2T", [NSEG, 16, 128, Tn])
        self.V2 = dscr("V2", [NSEG, Tn, D])
        self.G2T = dscr("G2T", [NSEG, D, Tn])
        self.MG = dscr("MG", [G, 128, 128])
        self.BIN = dscr("BIN", [G, 128, 2, 128])
        self.COUT = dscr("COUT", [G, 128, 4, 128])
        self.scrbuf = {}
        self.ges = contextlib.ExitStack()
        self.ps = [self.ges.enter_context(nc.psum_tensor("ps%d" % i, [128, 512], F32)) for i in range(8)]
        self.psb = [Buf("ps%d" % i, excl=True) for i in range(8)]
        self.gtiles = {}

    def sb(self, es, name, shape, dt=F32):
        self._nsb = getattr(self, "_nsb", 0) + 1
        return es.enter_context(self.nc.sbuf_tensor("%s_%d" % (name, self._nsb), list(shape), dt))

    def dbuf(self, key):
        b = self.scrbuf.get(key)
        if b is None:
            b = Buf(str(key))
            self.scrbuf[key] = b
        return b

    def load_consts(self):
        kb, nc, es = self.kb, self.nc, self.ges
        g = self.gtiles

        def ld(name, src, shape, dt, slow=False):
            t = self.sb(es, "c_" + name, shape, dt)
            b = Buf(name)
            kb.dma(t[:], src, writes=[b], slow=slow)
            g[name] = (t, b)

        ld("identf", self.c_identf[:, :], [128, 128], F32)
        ld("identb", self.c_identb[:, :], [128, 128], BF16)
        ld("rotE", self.c_rotE[:, :], [128, 128], BF16)
        ld("rotO", self.c_rotO[:, :], [128, 128], BF16)
        ld("flag", self.c_flag[:, :], [128, 2], F32)
        ld("gq", dap(self.gq, 0, [[1, 128], [1, 1]]), [128, 1], F32)
        ld("gk", dap(self.gk, 0, [[1, 128], [1, 1]]), [128, 1], F32)
        ld("gqb", dap(self.gq, 0, [[0, 128], [1, 128]]), [128, 128], F32)
        ld("gkb", dap(self.gk, 0, [[0, 128], [1, 128]]), [128, 128], F32)
        ld("bglu", dap(self.b_glu, 0, [[1, 128], [128, 8]]), [128, 8], F32, slow=True)
        g["mu"] = (self.sb(es, "c_mu", [128, 2, G], F32), Buf("mu"))
        g["mup"] = (self.sb(es, "c_mup", [128, 2, G, 16], F32), Buf("mup"))
        t = self.sb(es, "c_misc", [128, 8], F32)
        b = Buf("misc")
        g["misc"] = (t, b)
        kb.op("dve", lambda h: h.memset(t[:, 0:1], QK_EPS), writes=[b])
        kb.op("dve", lambda h: h.memset(t[:, 1:2], LN_EPS), writes=[b])
        kb.op("dve", lambda h: h.memset(t[:, 2:3], 0.0), writes=[b])
        kb.op("dve", lambda h: h.memset(t[:, 3:4], math.pi / 2), writes=[b])
        o1 = self.sb(es, "c_ones", [128, 128], BF16)
        o2 = self.sb(es, "c_ones128", [128, 128], BF16)
        b1 = Buf("ones")
        kb.op("dve", lambda h: h.memset(o1[:], 1.0), writes=[b1])
        kb.op("dve", lambda h: h.memset(o2[:], 1.0 / 128), writes=[b1])
        g["ones"] = (o1, b1)
        g["ones128"] = (o2, b1)
        mq = t[:, 4:5]
        mk = t[:, 5:6]
        gqb, gqbb = g["gqb"]
        gkb, gkbb = g["gkb"]
        kb.op("dve", lambda h: h.tensor_reduce(out=mq, in_=gqb[:], axis=mybir.AxisListType.X, op=ALU.max,
                                                apply_absolute_value=True), reads=[gqbb], writes=[b])
        kb.op("dve", lambda h: h.tensor_reduce(out=mk, in_=gkb[:], axis=mybir.AxisListType.X, op=ALU.max,
                                                apply_absolute_value=True), reads=[gkbb], writes=[b])
        kb.op("dve", lambda h: h.scalar_tensor_tensor(out=t[:, 6:7], in0=mq, scalar=-math.sqrt(128.0), in1=mk,
                                                       op0=ALU.mult, op1=ALU.mult), reads=[b], writes=[b])
        fl, flb = g["flag"]
        kb.op("dve", lambda h: h.tensor_tensor(out=t[:, 7:8], in0=t[:, 6:7], in1=fl[:, 1:2], op=ALU.add),
              reads=[b, flb], writes=[b])

    def load_xT(self, es, src, xT, xTb, pfx):
        kb = self.kb
        xin = [self.sb(es, pfx + "xin%d" % i, [128, D], F32) for i in range(2)]
        xbf = [self.sb(es, pfx + "xbf%d" % i, [128, D], BF16) for i in range(2)]
        xinb = [Buf(), Buf()]
        xbfb = [Buf(), Buf()]
        ident, identb = self.gtiles["identb"]
        for b in range(self.NB):
            sl = b % 2
            kb.dma(xin[sl][:], src[b * 128:(b + 1) * 128, :], writes=[xinb[sl]])
            kb.op("pool", lambda h: h.tensor_copy(xbf[sl][:], xin[sl][:]), reads=[xinb[sl]], writes=[xbfb[sl]])
            self.transpose_block(xbf[sl], xbfb[sl], xT, xTb[b], b, "act" if b % 2 == 0 else "dve")

    def transpose_block(self, xbf, xbfb, xT, xTblk, b, evac_eng):
        kb = self.kb
        ident, identb = self.gtiles["identb"]
        for q in range(4):
            pi = 6 + (q % 2)
            psv = self.ps[pi][:].bitcast(BF16)
            fns = []
            for i in range(4):
                kc = 4 * q + i
                fns.append(lambda h, i=i, kc=kc: h.transpose(psv[:, i * 128:(i + 1) * 128], xbf[:, kc * 128:(kc + 1) * 128], ident[:]))
            kb.group("pe", fns, reads=[xbfb, identb], writes=[self.psb[pi]])
            outv = xT[:, 4 * q:4 * q + 4, b * 128:(b + 1) * 128]
            inv = psv[:, 0:512].rearrange("p (a n) -> p a n", a=4)
            if evac_eng == "act":
                kb.op("act", lambda h: h.copy(outv, inv), reads=[self.psb[pi]], writes=[xTblk])
            else:
                kb.op("dve", lambda h: h.tensor_copy(outv, inv), reads=[self.psb[pi]], writes=[xTblk])

    def linear(self, es, W, ncols, xT, xTb, specs, pfx, maxw=256):
        kb = self.kb
        wst = [self.sb(es, pfx + "wst%d" % i, [128, KC, maxw], F32) for i in range(2)]
        wbf = [self.sb(es, pfx + "wbf%d" % i, [128, KC, maxw], BF16) for i in range(2)]
        wstb = [Buf(), Buf()]
        wbfb = [Buf(), Buf()]

        def fetch(i):
            c0, wd, _ = specs[i]
            sl = i % 2
            src = dap(W, c0, [[ncols, 128], [128 * ncols, KC], [1, wd]])
            kb.dma(wst[sl][:, :, 0:wd], src, writes=[wstb[sl]])
            kb.op("pool", lambda h: h.tensor_copy(wbf[sl][:, :, 0:wd], wst[sl][:, :, 0:wd]), reads=[wstb[sl]], writes=[wbfb[sl]])

        fetch(0)
        for i in range(len(specs)):
            if i + 1 < len(specs):
                fetch(i + 1)
            specs[i][2](wbf[i % 2], wbfb[i % 2])

    def phase_even_in(self, s):
        kb, nc = self.kb, self.nc
        Tn, NTT, NB = self.T, self.NTT, self.NB
        g = self.gtiles
        with contextlib.ExitStack() as es:
            xT = self.sb(es, "e_xT", [128, KC, Tn], BF16)
            xTb = [Buf("xT%d" % b) for b in range(NB)]
            self.load_xT(es, self.x[s], xT, xTb, "e_")
            cs = self.sb(es, "e_cs", [128, 2, Tn], F32)
            csb = Buf("cs")
            kb.dma(cs[:, 0, :], self.cosE[s], writes=[csb])
            kb.dma(cs[:, 1, :], self.sinE[s], writes=[csb])
            stg = [self.sb(es, "e_stg%d" % i, [128, Tn], BF16) for i in range(2)]
            stgb = [Buf(), Buf()]
            vst = self.sb(es, "e_vst", [128, NB, 256], BF16)
            vstb = Buf()
            sqb = [self.sb(es, "e_sq%d" % i, [128, TT], BF16) for i in range(3)]
            sd = [self.sb(es, "e_sd%d" % i, [128, TT], F32) for i in range(3)]
            qn = [self.sb(es, "e_qn%d" % i, [128, TT], F32) for i in range(3)]
            qnb = [self.sb(es, "e_qnb%d" % i, [128, TT], BF16) for i in range(3)]
            t1 = [self.sb(es, "e_t1%d" % i, [128, TT], F32) for i in range(3)]
            t2 = [self.sb(es, "e_t2%d" % i, [128, TT], F32) for i in range(3)]
            tb = {k: [Buf(), Buf(), Buf()] for k in ("sq", "sd", "qn", "qnb", "t1", "t2")}
            misc, miscb = g["misc"]
            ones128, onesb = g["ones128"]
            rotE, rotEb = g["rotE"]
            cnt = {"mm": 0, "ep": 0}

            def mm_tile(wbf, wbfb, tt):
                pi = cnt["mm"] % 4
                cnt["mm"] += 1
                fns = [(lambda h, kc=kc: h.matmul(self.ps[pi][:, 0:TT], wbf[:, kc, 0:128], xT[:, kc, tt * TT:(tt + 1) * TT],
                                                  start=(kc == 0), stop=(kc == KC - 1))) for kc in range(KC)]
                kb.group("pe", fns, reads=[wbfb] + xTb[4 * tt:4 * tt + 4], writes=[self.psb[pi]])
                return pi

            def chunk_simple(fo, kind, dst_ap):
                def run(wbf, wbfb):
                    run_due(flush=True)
                    sl = fo % 2
                    for tt in range(NTT):
                        pi = mm_tile(wbf, wbfb, tt)
                        if kind in ("u", "ag"):
                            outv = stg[sl][:].rearrange("p (j m) -> p m j", j=8)[:, 64 * tt:64 * tt + 64, :]
                            inv = self.ps[pi][:, 0:TT].rearrange("p (m j) -> p m j", j=8)
                        else:
                            outv = stg[sl][:, tt * TT:(tt + 1) * TT]
                            inv = self.ps[pi][:, 0:TT]
                        fn = AF.Copy if kind == "u" else AF.Silu
                        kb.op("act", lambda h: h.activation(out=outv, in_=inv, func=fn), reads=[self.psb[pi]], writes=[stgb[sl]])
                    kb.dma(dst_ap, stg[sl][:], reads=[stgb[sl]], writes=[self.dbuf(("E", s, fo))])
                return run

            pend = []
            clock = {"k": 0}

            def run_due(flush=False):
                while pend and (flush or pend[0][0] <= clock["k"]):
                    pend.pop(0)[1]()

            def chunk_qk(fo, gname, dst_ap):
                gt, gtb = g[gname]

                def run(wbf, wbfb):
                    sl = fo % 2
                    for tt in range(NTT):
                        k = clock["k"]
                        pi = mm_tile(wbf, wbfb, tt)
                        e = k % 3
                        pm, pr = 4, 5
                        kb.op("act", lambda h: h.activation(out=sqb[e][:], in_=self.ps[pi][:, 0:TT], func=AF.Square),
                              reads=[self.psb[pi]], writes=[tb["sq"][e]])

                        def stage_b(pi=pi, e=e):
                            kb.group("pe", [lambda h: h.matmul(self.ps[pm][:, 0:TT], ones128[:], sqb[e][:], start=True, stop=True)],
                                     reads=[tb["sq"][e], onesb], writes=[self.psb[pm]])
                            kb.op("act", lambda h: h.activation(out=sd[e][:], in_=self.ps[pm][:, 0:TT], func=AF.Ln, bias=misc[:, 0:1]),
                                  reads=[self.psb[pm], miscb], writes=[tb["sd"][e]])
                            kb.op("act", lambda h: h.activation(out=sd[e][:], in_=sd[e][:], func=AF.Exp, scale=-0.5), reads=[tb["sd"][e]], writes=[tb["sd"][e]])
                            kb.op("dve", lambda h: h.scalar_tensor_tensor(out=qn[e][:], in0=self.ps[pi][:, 0:TT], scalar=gt[:, 0:1], in1=sd[e][:],
                                                                           op0=ALU.mult, op1=ALU.mult),
                                  reads=[self.psb[pi], tb["sd"][e], gtb], writes=[tb["qn"][e]])
                            kb.op("act", lambda h: h.copy(qnb[e][:], qn[e][:]), reads=[tb["qn"][e]], writes=[tb["qnb"][e]])

                        def stage_c(e=e, tt=tt, sl=sl):
                            kb.group("pe", [lambda h: h.matmul(self.ps[pr][:, 0:TT], rotE[:], qnb[e][:], start=True, stop=True)],
                                     reads=[tb["qnb"][e], rotEb], writes=[self.psb[pr]])
                            kb.op("dve", lambda h: h.tensor_tensor(out=t1[e][:], in0=qn[e][:], in1=cs[:, 0, tt * TT:(tt + 1) * TT], op=ALU.mult),
                                  reads=[tb["qn"][e], csb], writes=[tb["t1"][e]])
                            kb.op("dve", lambda h: h.tensor_tensor(out=t2[e][:], in0=self.ps[pr][:, 0:TT], in1=cs[:, 1, tt * TT:(tt + 1) * TT], op=ALU.mult),
                                  reads=[self.psb[pr], csb], writes=[tb["t2"][e]])
                            kb.op("dve", lambda h: h.tensor_tensor(out=stg[sl][:, tt * TT:(tt + 1) * TT], in0=t1[e][:], in1=t2[e][:], op=ALU.add),
                                  reads=[tb["t1"][e], tb["t2"][e]], writes=[stgb[sl]])
                        pend.append((k + 1, stage_b))
                        pend.append((k + 2, stage_c))
                        if tt == NTT - 1:
                            pend.append((k + 2, lambda sl=sl: kb.dma(dst_ap, stg[sl][:], reads=[stgb[sl]], writes=[self.dbuf(("E", s, fo))])))
                        pend.sort(key=lambda t: t[0])
                        clock["k"] += 1
                        run_due()
                return run

            def chunk_v(half):
                def run(wbf, wbfb):
                    run_due(flush=True)
                    for b in range(NB):
                        pi = cnt["mm"] % 4
                        cnt["mm"] += 1
                        fns = [(lambda h, kc=kc: h.matmul(self.ps[pi][:, 0:128], xT[:, kc, b * 128:(b + 1) * 128], wbf[:, kc, 0:128],
                                                          start=(kc == 0), stop=(kc == KC - 1))) for kc in range(KC)]
                        kb.group("pe", fns, reads=[wbfb, xTb[b]], writes=[self.psb[pi]])
                        kb.op("act", lambda h: h.copy(vst[:, b, half * 128:(half + 1) * 128], self.ps[pi][:, 0:128]), reads=[self.psb[pi]], writes=[vstb])
                    if half == 1:
                        kb.dma(self.VV[s].rearrange("(b p) e -> p b e", p=128), vst[:], reads=[vstb], writes=[self.dbuf(("E", s, "v"))])
                return run

            specs = []
            for fo in range(36):
                c0 = fo * 128
                if fo < 8:
                    specs.append((c0, 128, chunk_simple(fo, "u", self.UT[s, fo * 128:(fo + 1) * 128, :])))
                elif fo < 16:
                    specs.append((c0, 128, chunk_simple(fo, "ag", self.AG[s, (fo - 8) * 128:(fo - 7) * 128, :])))
                elif fo < 24:
                    specs.append((c0, 128, chunk_qk(fo, "gq", self.QT[s, fo - 16])))
                elif fo < 26:
                    specs.append((c0, 128, chunk_qk(fo, "gk", self.KT[s, fo - 24])))
                elif fo < 28:
                    specs.append((c0, 128, chunk_v(fo - 26)))
                else:
                    specs.append((c0, 128, chunk_simple(fo, "bg", self.BG[s, (fo - 28) * 128:(fo - 27) * 128, :])))
            self.linear(es, self.w_in_e, EVEN_IN, xT, xTb, specs, "e_", maxw=128)
            run_due(flush=True)
        kb.barrier()

    def sview(self, tile, part0, nparts, off, dims):
        base = tile[:]
        pst = base.ap[0][0]
        return AP(base.tensor, base.offset + part0 * pst + off, [[pst, nparts]] + [list(d) for d in dims])

    def phase_ssm_pre(self):
        kb, nc = self.kb, self.nc
        g = self.gtiles
        identf, identfb = g["identf"]
        misc, miscb = g["misc"]
        MAGIC = 12582912.0
        TWO_PI = 2.0 * math.pi
        with contextlib.ExitStack() as es:
            def tl(name, shape, dt=F32):
                return self.sb(es, "p_" + name, shape, dt), Buf(name)
            Are, Areb = tl("Are", [128, G]); Aim, Aimb = tl("Aim", [128, G]); LDT, LDTb = tl("LDT", [128, G])
            Bre, Breb = tl("Bre", [128, G, 16]); Bim, Bimb = tl("Bim", [128, G, 16])
            Cre, Creb = tl("Cre", [128, G, 16]); Cim, Cimb = tl("Cim", [128, G, 16])
            Dsk, Dskb = tl("Dsk", [128, G])
            mmk, mmkb = tl("mmk", [128, 2, 128])
            kb.dma(mmk[:], self.c_mmask.rearrange("a p n -> p a n"), writes=[mmkb])
            for d in range(2):
                kb.dma(Are[64 * d:64 * d + 64, :], dap(self.a_re, d * G * 64, [[1, 64], [64, G]]), writes=[Areb], slow=True)
                kb.dma(Aim[64 * d:64 * d + 64, :], dap(self.a_im, d * G * 64, [[1, 64], [64, G]]), writes=[Aimb], slow=True)
                kb.dma(LDT[64 * d:64 * d + 64, :], dap(self.log_dt, d * G, [[0, 64], [1, G]]), writes=[LDTb])
                kb.dma(Bre[64 * d:64 * d + 64, :, :], dap(self.b_re, 0, [[16, 64], [1024, G], [1, 16]]), writes=[Breb])
                kb.dma(Bim[64 * d:64 * d + 64, :, :], dap(self.b_im, 0, [[16, 64], [1024, G], [1, 16]]), writes=[Bimb])
            for j in range(8):
                kb.dma(Dsk[16 * j:16 * j + 16, :], dap(self.d_skip, 0, [[1, 16], [16, G]]), writes=[Dskb], slow=True)
            ct = [tl("ct%d" % i, [128, 128]) for i in range(2)]
            n = 0
            for src, dst, dstb in ((self.c_re, Cre, Creb), (self.c_im, Cim, Cimb)):
                for blk in range(8):
                    c_t, c_b = ct[n % 2]
                    pi = n % 2
                    n += 1
                    kb.dma(c_t[:], dap(src, blk * 8 * 16 * 64, [[64, 128], [G * 16 * 64, 2], [1, 64]]), writes=[c_b])
                    kb.group("pe", [lambda h: h.transpose(self.ps[pi][:, 0:128], c_t[:], identf[:])], reads=[c_b, identfb], writes=[self.psb[pi]])
                    kb.op("act", lambda h: h.copy(dst[:, blk * 8:(blk + 1) * 8, :].rearrange("p a c -> p (a c)"), self.ps[pi][:, 0:128]),
                          reads=[self.psb[pi]], writes=[dstb])
            dt_, dtb = tl("dt", [128, G]); ard, ardb = tl("ard", [128, G]); aid, aidb = tl("aid", [128, G])
            kb.op("act", lambda h: h.activation(out=dt_[:], in_=LDT[:], func=AF.Exp), reads=[LDTb], writes=[dtb])
            kb.op("dve", lambda h: h.tensor_tensor(out=ard[:], in0=Are[:], in1=dt_[:], op=ALU.mult), reads=[Areb, dtb], writes=[ardb])
            kb.op("dve", lambda h: h.tensor_tensor(out=aid[:], in0=Aim[:], in1=dt_[:], op=ALU.mult), reads=[Aimb, dtb], writes=[aidb])
            EAr, EArb = tl("EAr", [128, 16, G]); EAi, EAib = tl("EAi", [128, 16, G])
            mag, magb = tl("mag", [128, 16, G]); ang, angb = tl("ang", [128, 16, G]); rr, rrb = tl("rr", [128, 16, G])
            sn, snb = tl("sn", [128, 16, G]); cs_, csb_ = tl("cs", [128, 16, G])
            for k in range(-7, 9):
                i = k + 7
                kb.op("act", lambda h: h.activation(out=mag[:, i, :], in_=ard[:], func=AF.Exp, scale=float(k)), reads=[ardb], writes=[magb])
                kb.op("dve", lambda h: h.tensor_scalar(out=rr[:, i, :], in0=aid[:], scalar1=float(k) / TWO_PI, scalar2=MAGIC, op0=ALU.mult, op1=ALU.add),
                      reads=[aidb], writes=[rrb])
                kb.op("dve", lambda h: h.tensor_scalar(out=rr[:, i, :], in0=rr[:, i, :], scalar1=-MAGIC, scalar2=-TWO_PI, op0=ALU.add, op1=ALU.mult),
                      reads=[rrb], writes=[rrb])
                kb.op("dve", lambda h: h.scalar_tensor_tensor(out=ang[:, i, :], in0=aid[:], scalar=float(k), in1=rr[:, i, :], op0=ALU.mult, op1=ALU.add),
                      reads=[aidb, rrb], writes=[angb])
            kb.op("dve", lambda h: h.tensor_scalar(out=ang[:], in0=ang[:], scalar1=math.pi, scalar2=-math.pi, op0=ALU.min, op1=ALU.max),
                  reads=[angb], writes=[angb])
            kb.op("act", lambda h: h.activation(out=sn[:], in_=ang[:], func=AF.Sin), reads=[angb], writes=[snb])
            kb.op("act", lambda h: h.activation(out=rr[:], in_=ang[:], func=AF.Abs), reads=[angb], writes=[rrb])
            kb.op("act", lambda h: h.activation(out=cs_[:], in_=rr[:], func=AF.Sin, scale=-1.0, bias=misc[:, 3:4]), reads=[rrb, miscb], writes=[csb_])
            kb.op("dve", lambda h: h.tensor_tensor(out=EAr[:], in0=mag[:], in1=cs_[:], op=ALU.mult), reads=[magb, csb_], writes=[EArb])
            kb.op("dve", lambda h: h.tensor_tensor(out=EAi[:], in0=mag[:], in1=sn[:], op=ALU.mult), reads=[magb, snb], writes=[EAib])
            mu_t, mu_b = g["mu"]
            kb.op("dve", lambda h: h.tensor_copy(mu_t[:, 0, :], EAr[:, 15, :]), reads=[EArb], writes=[mu_b])
            kb.op("dve", lambda h: h.tensor_copy(mu_t[:, 1, :], EAi[:, 15, :]), reads=[EAib], writes=[mu_b])
            mup_t, mup_b = g["mup"]
            pw1, pw1b = tl("pw1", [128, G]); pw2, pw2b = tl("pw2", [128, G])
            kb.op("dve", lambda h: h.tensor_copy(mup_t[:, 0, :, 0], EAr[:, 15, :]), reads=[EArb], writes=[mup_b])
            kb.op("dve", lambda h: h.tensor_copy(mup_t[:, 1, :, 0], EAi[:, 15, :]), reads=[EAib], writes=[mup_b])
            for k in range(1, 16):
                pr, pi_ = mup_t[:, 0, :, k - 1], mup_t[:, 1, :, k - 1]
                kb.op("dve", lambda h: h.tensor_tensor(out=pw1[:], in0=pr, in1=mu_t[:, 0, :], op=ALU.mult), reads=[mup_b, mu_b], writes=[pw1b])
                kb.op("dve", lambda h: h.tensor_tensor(out=pw2[:], in0=pi_, in1=mu_t[:, 1, :], op=ALU.mult), reads=[mup_b, mu_b], writes=[pw2b])
                kb.op("dve", lambda h: h.tensor_tensor(out=mup_t[:, 0, :, k], in0=pw1[:], in1=pw2[:], op=ALU.subtract), reads=[pw1b, pw2b], writes=[mup_b])
                kb.op("dve", lambda h: h.tensor_tensor(out=pw1[:], in0=pr, in1=mu_t[:, 1, :], op=ALU.mult), reads=[mup_b, mu_b], writes=[pw1b])
                kb.op("dve", lambda h: h.tensor_tensor(out=pw2[:], in0=pi_, in1=mu_t[:, 0, :], op=ALU.mult), reads=[mup_b, mu_b], writes=[pw2b])
                kb.op("dve", lambda h: h.tensor_tensor(out=mup_t[:, 1, :, k], in0=pw1[:], in1=pw2[:], op=ALU.add), reads=[pw1b, pw2b], writes=[mup_b])
            nr, nrb = tl("nr", [128, G]); den, denb = tl("den", [128, G]); tq, tqb = tl("tq", [128, G])
            fre, freb = tl("fre", [128, G]); fim, fimb = tl("fim", [128, G])
            kb.op("dve", lambda h: h.tensor_scalar_add(out=nr[:], in0=EAr[:, 8, :], scalar1=-1.0), reads=[EArb], writes=[nrb])
            kb.op("dve", lambda h: h.tensor_tensor(out=den[:], in0=Are[:], in1=Are[:], op=ALU.mult), reads=[Areb], writes=[denb])
            kb.op("dve", lambda h: h.tensor_tensor(out=tq[:], in0=Aim[:], in1=Aim[:], op=ALU.mult), reads=[Aimb], writes=[tqb])
            kb.op("dve", lambda h: h.tensor_tensor(out=den[:], in0=den[:], in1=tq[:], op=ALU.add), reads=[denb, tqb], writes=[denb])
            kb.op("dve", lambda h: h.reciprocal(den[:], den[:]), reads=[denb], writes=[denb])
            kb.op("dve", lambda h: h.tensor_tensor(out=fre[:], in0=nr[:], in1=Are[:], op=ALU.mult), reads=[nrb, Areb], writes=[freb])
            kb.op("dve", lambda h: h.tensor_tensor(out=tq[:], in0=EAi[:, 8, :], in1=Aim[:], op=ALU.mult), reads=[EAib, Aimb], writes=[tqb])
            kb.op("dve", lambda h: h.tensor_tensor(out=fre[:], in0=fre[:], in1=tq[:], op=ALU.add), reads=[freb, tqb], writes=[freb])
            kb.op("dve", lambda h: h.tensor_tensor(out=fre[:], in0=fre[:], in1=den[:], op=ALU.mult), reads=[freb, denb], writes=[freb])
            kb.op("dve", lambda h: h.tensor_tensor(out=fim[:], in0=EAi[:, 8, :], in1=Are[:], op=ALU.mult), reads=[EAib, Areb], writes=[fimb])
            kb.op("dve", lambda h: h.tensor_tensor(out=tq[:], in0=nr[:], in1=Aim[:], op=ALU.mult), reads=[nrb, Aimb], writes=[tqb])
            kb.op("dve", lambda h: h.tensor_tensor(out=fim[:], in0=fim[:], in1=tq[:], op=ALU.subtract), reads=[fimb, tqb], writes=[fimb])
            kb.op("dve", lambda h: h.tensor_tensor(out=fim[:], in0=fim[:], in1=den[:], op=ALU.mult), reads=[fimb, denb], writes=[fimb])
            Bbr, Bbrb = tl("Bbr", [128, G, 16]); Bbi, Bbib = tl("Bbi", [128, G, 16]); tb1, tb1b = tl("tb1", [128, G, 16])
            frb = fre[:].unsqueeze(2).broadcast_to([128, G, 16])
            fib = fim[:].unsqueeze(2).broadcast_to([128, G, 16])
            kb.op("dve", lambda h: h.tensor_tensor(out=Bbr[:], in0=Bre[:], in1=frb, op=ALU.mult), reads=[Breb, freb], writes=[Bbrb])
            kb.op("dve", lambda h: h.tensor_tensor(out=tb1[:], in0=Bim[:], in1=fib, op=ALU.mult), reads=[Bimb, fimb], writes=[tb1b])
            kb.op("dve", lambda h: h.tensor_tensor(out=Bbr[:], in0=Bbr[:], in1=tb1[:], op=ALU.subtract), reads=[Bbrb, tb1b], writes=[Bbrb])
            kb.op("dve", lambda h: h.tensor_tensor(out=Bbi[:], in0=Bim[:], in1=frb, op=ALU.mult), reads=[Bimb, freb], writes=[Bbib])
            kb.op("dve", lambda h: h.tensor_tensor(out=tb1[:], in0=Bre[:], in1=fib, op=ALU.mult), reads=[Breb, fimb], writes=[tb1b])
            kb.op("dve", lambda h: h.tensor_tensor(out=Bbi[:], in0=Bbi[:], in1=tb1[:], op=ALU.add), reads=[Bbib, tb1b], writes=[Bbib])
            fam = {}
            for nm in ("BTr", "BTi", "COr", "COi", "P1r", "P1i", "P2rF", "P2iF", "P2rB", "P2iB"):
                fam[nm] = tl(nm, [128, GBP, 128])
            for nm in ("P2rF", "P2iF", "P2rB", "P2iB"):
                kb.op("dve", lambda h: h.memset(fam[nm][0][:], 0.0), writes=[fam[nm][1]])
            tmp = [tl("tmp%d" % i, [128, GBP, 128]) for i in range(2)]
            mgs, mgsb = tl("mgs", [128, GBP, 128], BF16)
            bins, binsb = tl("bins", [128, GBP, 2, 128], BF16)
            cos_, cosb_ = tl("cos", [128, GBP, 4, 128], BF16)
            kb.op("dve", lambda h: h.memset(cos_[:], 0.0), writes=[cosb_])
            m1, m1b = tl("m1", [128, 128]); m2, m2b = tl("m2", [128, 128])

            def cprod(part, g0, k0, ks, Yr, Yrb, Yi, Yib, outr, outi, neg_im):
                en = "dve"
                o_r, o_rb = fam[outr]
                o_i, o_ib = fam[outi]
                (ta, tab), (tc, tcb) = tmp
                dims_e = [[1, GBP], [ks * G, 8], [0, 16]]
                Xr = self.sview(EAr, part, 64, (k0 + 7) * G + g0, dims_e)
                Xi = self.sview(EAi, part, 64, (k0 + 7) * G + g0, dims_e)
                dims_y = [[16, GBP], [0, 8], [1, 16]]
                yr = self.sview(Yr, part, 64, g0 * 16, dims_y)
                yi = self.sview(Yi, part, 64, g0 * 16, dims_y)
                d4 = [[128, GBP], [16, 8], [1, 16]]
                v = lambda t: self.sview(t, part, 64, 0, d4)
                kb.op(en, lambda h: h.tensor_tensor(out=v(ta), in0=Xr, in1=yr, op=ALU.mult), reads=[EArb, Yrb], writes=[tab])
                kb.op(en, lambda h: h.tensor_tensor(out=v(tc), in0=Xi, in1=yi, op=ALU.mult), reads=[EAib, Yib], writes=[tcb])
                kb.op(en, lambda h: h.tensor_tensor(out=v(o_r), in0=v(ta), in1=v(tc), op=ALU.subtract), reads=[tab, tcb], writes=[o_rb])
                kb.op(en, lambda h: h.tensor_tensor(out=v(ta), in0=Xr, in1=yi, op=ALU.mult), reads=[EArb, Yib], writes=[tab])
                kb.op(en, lambda h: h.tensor_tensor(out=v(tc), in0=Xi, in1=yr, op=ALU.mult), reads=[EAib, Yrb], writes=[tcb])
                kb.op(en, lambda h: h.tensor_tensor(out=v(o_i), in0=v(ta), in1=v(tc), op=ALU.add), reads=[tab, tcb], writes=[o_ib])
                if neg_im:
                    kb.op(en, lambda h: h.tensor_scalar_mul(out=v(o_i), in0=v(o_i), scalar1=-1.0), reads=[o_ib], writes=[o_ib])

            for bi in range(G // GBP):
                g0 = bi * GBP
                for part, d in ((0, 0), (64, 1)):
                    kB = (7, -1) if d == 0 else (0, 1)
                    kC = (1, 1) if d == 0 else (8, -1)
                    k1 = (0, 1) if d == 0 else (0, -1)
                    k2 = (0, -1) if d == 0 else (0, 1)
                    sfx = "F" if d == 0 else "B"
                    cprod(part, g0, kB[0], kB[1], Bbr, Bbrb, Bbi, Bbib, "BTr", "BTi", False)
                    cprod(part, g0, kC[0], kC[1], Cre, Creb, Cim, Cimb, "COr", "COi", True)
                    cprod(part, g0, k1[0], k1[1], Cre, Creb, Cim, Cimb, "P1r", "P1i", False)
                    cprod(part, g0, k2[0], k2[1], Bbr, Bbrb, Bbi, Bbib, "P2r" + sfx, "P2i" + sfx, True)
                BTr, BTrb = fam["BTr"]; BTi, BTib = fam["BTi"]
                P1r, P1rb = fam["P1r"]; P1i, P1ib = fam["P1i"]
                COr, COrb = fam["COr"]; COi, COib = fam["COi"]
                kb.op("act", lambda h: h.copy(cos_[0:64, :, 0, :], COr[0:64]), reads=[COrb], writes=[cosb_])
                kb.op("act", lambda h: h.copy(cos_[0:64, :, 1, :], COi[0:64]), reads=[COib], writes=[cosb_])
                kb.op("act", lambda h: h.copy(cos_[64:128, :, 2, :], COr[64:128]), reads=[COrb], writes=[cosb_])
                kb.op("act", lambda h: h.copy(cos_[64:128, :, 3, :], COi[64:128]), reads=[COib], writes=[cosb_])
                kb.dma(self.COUT[g0:g0 + GBP].rearrange("g p r n -> p g r n"), cos_[:], reads=[cosb_], writes=[self.dbuf("COUT")])
                for gi in range(GBP):
                    pa, pb = 2 + (gi % 2), 4 + (gi % 2)
                    kb.group("pe", [lambda h: h.transpose(self.ps[pa][:, 0:128], BTr[:, gi, :], identf[:]),
                                    lambda h: h.transpose(self.ps[pa][:, 128:256], BTi[:, gi, :], identf[:])],
                             reads=[BTrb, BTib, identfb], writes=[self.psb[pa]])
                    kb.op("act", lambda h: h.copy(bins[:, gi, :, :], self.ps[pa][:, 0:256].rearrange("p (r n) -> p r n", r=2)),
                          reads=[self.psb[pa]], writes=[binsb])
                    fns = []
                    rds = [P1rb, P1ib]
                    for d, sfx in ((0, "F"), (1, "B")):
                        p2r, p2rb = fam["P2r" + sfx]
                        p2i, p2ib = fam["P2i" + sfx]
                        rds += [p2rb, p2ib]
                        fns.append(lambda h, d=d, p2r=p2r: h.matmul(self.ps[pb][:, 128 * d:128 * d + 128], p2r[:, gi, :], P1r[:, gi, :], start=True, stop=False))
                        fns.append(lambda h, d=d, p2i=p2i: h.matmul(self.ps[pb][:, 128 * d:128 * d + 128], p2i[:, gi, :], P1i[:, gi, :], start=False, stop=True))
                    kb.group("pe", fns, reads=rds, writes=[self.psb[pb]])
                    kb.op("dve", lambda h: h.tensor_tensor(out=m1[:], in0=self.ps[pb][:, 0:128], in1=mmk[:, 0, :], op=ALU.mult),
                          reads=[self.psb[pb], mmkb], writes=[m1b])
                    kb.op("dve", lambda h: h.tensor_tensor(out=m2[:], in0=self.ps[pb][:, 128:256], in1=mmk[:, 1, :], op=ALU.mult),
                          reads=[self.psb[pb], mmkb], writes=[m2b])
                    kb.op("dve", lambda h: h.tensor_tensor(out=m1[:], in0=m1[:], in1=m2[:], op=ALU.add), reads=[m1b, m2b], writes=[m1b])
                    kb.op("dve", lambda h: h.scalar_tensor_tensor(out=mgs[:, gi, :], in0=identf[:], scalar=Dsk[:, g0 + gi:g0 + gi + 1], in1=m1[:],
                                                                   op0=ALU.mult, op1=ALU.add), reads=[identfb, Dskb, m1b], writes=[mgsb])
                kb.dma(self.BIN[g0:g0 + GBP].rearrange("g p r n -> p g r n"), bins[:], reads=[binsb], writes=[self.dbuf("BIN")])
                kb.dma(self.MG[g0:g0 + GBP].rearrange("g p n -> p g n"), mgs[:], reads=[mgsb], writes=[self.dbuf("MG")])
        kb.barrier()

    def phase_ssm(self):
        kb = self.kb
        Tn, NCH = self.T, self.NCH
        g = self.gtiles
        mu, mub = g["mu"]
        fl, flb = g["flag"]
        NX = NCH + 1
        with contextlib.ExitStack() as es:
            def tl(name, shape, dt=F32):
                return self.sb(es, "s_" + name, shape, dt), Buf(name)
            mg, mgb = tl("mg", [128, GB, 128], BF16)
            bn, bnb = tl("bin", [128, GB, 2, 128], BF16)
            co, cob = tl("cout", [128, GB, 4, 128], BF16)
            U8 = [tl("u8_%d" % s, [128, GB, NCH], BF16) for s in range(NSEG)]
            XS, XSb = tl("XS", [128, 2, GB, NX])
            XP = {k: tl("XP" + k, [128, 2, GB, NX], BF16) for k in "ABC"}
            MUA, MUAb = tl("MUA", [128, 2, GB])
            MUB, MUBb = tl("MUB", [128, 2, GB])
            T1, T1b = tl("T1", [128, 2, GB])
            T2, T2b = tl("T2", [128, 2, GB])
            FIN, FINb = tl("FIN", [128, 2, GB])
            LB = 16
            NBK = NCH // LB
            W1, W1b = tl("W1", [128, 2, GB, NBK])
            W2, W2b = tl("W2", [128, 2, GB, NBK])
            W3, W3b = tl("W3", [128, GB, NBK, LB - 1])
            MLA, MLAb = tl("MLA", [128, 2, GB])
            MLB, MLBb = tl("MLB", [128, 2, GB])
            mup, mupb = g["mup"]
            y8 = [tl("y8_%d" % i, [128, NCH], BF16) for i in range(2)]
            sq = [tl("sq%d" % i, [128, NCH]) for i in range(2)]
            uu = [tl("uu%d" % i, [128, NCH]) for i in range(2)]
            cnt = {"ps": 0, "ev": 0}

            def xcol(t, k):
                return t[:, :, :, k]

            def run_pass(sf, sbw, init_from, key, g0):
                xp, xpb = XP[key]
                for gi in range(GB):
                    pi = cnt["ps"] % 3
                    cnt["ps"] += 1
                    uf, ufb = U8[sf]
                    ub, ubb = U8[sbw]
                    fns = []
                    for r in range(2):
                        fns.append(lambda h, r=r: h.matmul(self.ps[pi][0:64, r * NCH:(r + 1) * NCH], bn[:, gi, r, 0:64], uf[:, gi, :], start=True, stop=True))
                        fns.append(lambda h, r=r: h.matmul(self.ps[pi][64:128, r * NCH:(r + 1) * NCH], bn[:, gi, r, 64:128], rev_last(ub[:, gi, :], NCH), start=True, stop=True))
                    kb.group("pe", fns, reads=[bnb, ufb, ubb], writes=[self.psb[pi]])
                    kb.op("act", lambda h: h.copy(XS[:, :, gi, 1:NX], self.ps[pi][:, 0:2 * NCH].rearrange("p (r n) -> p r n", r=2)),
                          reads=[self.psb[pi]], writes=[XSb])
                if init_from is None:
                    kb.op("dve", lambda h: h.memset(xcol(XS, 0), 0.0), writes=[XSb])
                else:
                    kb.op("dve", lambda h: h.tensor_scalar_mul(out=xcol(XS, 0), in0=FIN[:], scalar1=fl[:, 0:1]), reads=[FINb, flb], writes=[XSb])
                base = XS[:]
                pst = base.ap[0][0]
                RS = GB * NX

                def xv(off, dims):
                    return AP(base.tensor, base.offset + off, [[pst, 128]] + [list(d) for d in dims])
                mua_b = AP(MUA[:].tensor, MUA[:].offset, [[MUA[:].ap[0][0], 128], [GB, 2], [1, GB], [0, NBK]])
                mub_b = AP(MUB[:].tensor, MUB[:].offset, [[MUB[:].ap[0][0], 128], [GB, 2], [1, GB], [0, NBK]])
                for j in range(1, LB):
                    prev = xv(j, [[RS, 2], [NX, GB], [LB, NBK]])
                    prev_sw = xv(RS + j, [[-RS, 2], [NX, GB], [LB, NBK]])
                    cur = xv(1 + j, [[RS, 2], [NX, GB], [LB, NBK]])
                    kb.op("dve", lambda h: h.tensor_tensor(out=W1[:], in0=mua_b, in1=prev, op=ALU.mult), reads=[MUAb, XSb], writes=[W1b])
                    kb.op("dve", lambda h: h.tensor_tensor(out=W2[:], in0=mub_b, in1=prev_sw, op=ALU.mult), reads=[MUBb, XSb], writes=[W2b])
                    kb.op("dve", lambda h: h.tensor_tensor(out=W1[:], in0=W1[:], in1=W2[:], op=ALU.add), reads=[W1b, W2b], writes=[W1b])
                    kb.op("dve", lambda h: h.tensor_tensor(out=cur, in0=cur, in1=W1[:], op=ALU.add), reads=[W1b, XSb], writes=[XSb])
                for b in range(NBK):
                    pc = xv(b * LB, [[RS, 2], [NX, GB]])
                    pc_sw = xv(RS + b * LB, [[-RS, 2], [NX, GB]])
                    cur = xv((b + 1) * LB, [[RS, 2], [NX, GB]])
                    kb.op("dve", lambda h: h.tensor_tensor(out=T1[:], in0=MLA[:], in1=pc, op=ALU.mult), reads=[MLAb, XSb], writes=[T1b])
                    kb.op("dve", lambda h: h.tensor_tensor(out=T2[:], in0=MLB[:], in1=pc_sw, op=ALU.mult), reads=[MLBb, XSb], writes=[T2b])
                    kb.op("dve", lambda h: h.tensor_tensor(out=T1[:], in0=T1[:], in1=T2[:], op=ALU.add), reads=[T1b, T2b], writes=[T1b])
                    kb.op("dve", lambda h: h.tensor_tensor(out=cur, in0=cur, in1=T1[:], op=ALU.add), reads=[T1b, XSb], writes=[XSb])
                mp = mup[:]
                mps = mp.ap[0][0]

                def pv(r):
                    return AP(mp.tensor, mp.offset + (r * G + g0) * 16, [[mps, 128], [16, GB], [0, NBK], [1, LB - 1]])
                Cr = xv(0, [[NX, GB], [LB, NBK], [0, LB - 1]])
                Ci = xv(RS, [[NX, GB], [LB, NBK], [0, LB - 1]])
                Xr = xv(1, [[NX, GB], [LB, NBK], [1, LB - 1]])
                Xi = xv(RS + 1, [[NX, GB], [LB, NBK], [1, LB - 1]])
                for (pa, ca, tgt, op) in ((pv(0), Cr, Xr, ALU.add), (pv(1), Ci, Xr, ALU.subtract), (pv(0), Ci, Xi, ALU.add), (pv(1), Cr, Xi, ALU.add)):
                    kb.op("dve", lambda h: h.tensor_tensor(out=W3[:], in0=pa, in1=ca, op=ALU.mult), reads=[mupb, XSb], writes=[W3b])
                    kb.op("dve", lambda h: h.tensor_tensor(out=tgt, in0=tgt, in1=W3[:], op=op), reads=[W3b, XSb], writes=[XSb])
                kb.op("dve", lambda h: h.tensor_copy(FIN[:], xcol(XS, NCH)), reads=[XSb], writes=[FINb])
                kb.op("act", lambda h: h.copy(xp[:], XS[:]), reads=[XSb], writes=[xpb])

            def emit_y(s, kf, kbk, g0):
                xf, xfb = XP[kf]
                xb, xbb = XP[kbk]
                us, usb = U8[s]
                for gi in range(GB):
                    pi = cnt["ps"] % 3
                    cnt["ps"] += 1
                    e = cnt["ev"] % 2
                    cnt["ev"] += 1
                    o = self.ps[pi][:, 0:NCH]
                    fns = [lambda h: h.matmul(o, mg[:, gi, :], us[:, gi, :], start=True, stop=False),
                           lambda h: h.matmul(o, co[:, gi, 0, :], xf[:, 0, gi, 0:NCH], start=False, stop=False),
                           lambda h: h.matmul(o, co[:, gi, 1, :], xf[:, 1, gi, 0:NCH], start=False, stop=False),
                           lambda h: h.matmul(o, co[:, gi, 2, :], rev_last(xb[:, 0, gi, 0:NCH], NCH), start=False, stop=False),
                           lambda h: h.matmul(o, co[:, gi, 3, :], rev_last(xb[:, 1, gi, 0:NCH], NCH), start=False, stop=True)]
                    kb.group("pe", fns, reads=[mgb, cob, usb, xfb, xbb], writes=[self.psb[pi]])
                    (sq_t, sq_b), (uu_t, uu_b), (y_t, y_b) = sq[e], uu[e], y8[e]
                    kb.op("act", lambda h: h.activation(out=sq_t[:], in_=o, func=AF.Square), reads=[self.psb[pi]], writes=[sq_b])
                    kb.op("dve", lambda h: h.tensor_scalar(out=sq_t[:], in0=sq_t[:], scalar1=0.044715, scalar2=1.0, op0=ALU.mult, op1=ALU.add),
                          reads=[sq_b], writes=[sq_b])
                    kb.op("dve", lambda h: h.tensor_tensor(out=uu_t[:], in0=sq_t[:], in1=o, op=ALU.mult), reads=[sq_b, self.psb[pi]], writes=[uu_b])
                    kb.op("act", lambda h: h.activation(out=uu_t[:], in_=uu_t[:], func=AF.Sigmoid, scale=1.5957691216057308), reads=[uu_b], writes=[uu_b])
                    kb.op("dve", lambda h: h.tensor_tensor(out=y_t[:], in0=uu_t[:], in1=o, op=ALU.mult), reads=[uu_b, self.psb[pi]], writes=[y_b])
                    dst = dap(self.YT, s * 1024 * Tn + (g0 + gi) * 16 * Tn, [[NCH, 8], [Tn, 16], [1, NCH]])
                    kb.dma(dst, y_t[:], reads=[y_b], writes=[self.dbuf(("YT", s))])

            for bi in range(G // GB):
                g0 = bi * GB
                kb.dma(mg[:], self.MG[g0:g0 + GB].rearrange("g p n -> p g n"), reads=[self.dbuf("MG")], writes=[mgb])
                kb.dma(bn[:], self.BIN[g0:g0 + GB].rearrange("g p r n -> p g r n"), reads=[self.dbuf("BIN")], writes=[bnb])
                kb.dma(co[:], self.COUT[g0:g0 + GB].rearrange("g p r n -> p g r n"), reads=[self.dbuf("COUT")], writes=[cob])
                for s in range(NSEG):
                    ut, utb = U8[s]
                    for gi in range(GB):
                        src = dap(self.UT, s * 1024 * Tn + (g0 + gi) * 16 * Tn, [[NCH, 8], [Tn, 16], [1, NCH]])
                        kb.dma(ut[:, gi, :], src, reads=[self.dbuf(("E", s, (g0 + gi) // 8))], writes=[utb])
                kb.op("pool", lambda h: h.tensor_copy(MUA[:, 0, :], mu[:, 0, g0:g0 + GB]), reads=[mub], writes=[MUAb])
                kb.op("pool", lambda h: h.tensor_copy(MUA[:, 1, :], mu[:, 0, g0:g0 + GB]), reads=[mub], writes=[MUAb])
                kb.op("pool", lambda h: h.tensor_scalar_mul(out=MUB[:, 0, :], in0=mu[:, 1, g0:g0 + GB], scalar1=-1.0), reads=[mub], writes=[MUBb])
                kb.op("pool", lambda h: h.tensor_copy(MUB[:, 1, :], mu[:, 1, g0:g0 + GB]), reads=[mub], writes=[MUBb])
                kb.op("pool", lambda h: h.tensor_copy(MLA[:, 0, :], mup[:, 0, g0:g0 + GB, LB - 1]), reads=[mupb], writes=[MLAb])
                kb.op("pool", lambda h: h.tensor_copy(MLA[:, 1, :], mup[:, 0, g0:g0 + GB, LB - 1]), reads=[mupb], writes=[MLAb])
                kb.op("pool", lambda h: h.tensor_scalar_mul(out=MLB[:, 0, :], in0=mup[:, 1, g0:g0 + GB, LB - 1], scalar1=-1.0), reads=[mupb], writes=[MLBb])
                kb.op("pool", lambda h: h.tensor_copy(MLB[:, 1, :], mup[:, 1, g0:g0 + GB, LB - 1]), reads=[mupb], writes=[MLBb])
                run_pass(0, 1, None, "A", g0)
                run_pass(1, 0, "A", "B", g0)
                emit_y(0, "A", "B", g0)
                emit_y(1, "B", "A", g0)
                run_pass(2, 2, None, "C", g0)
                emit_y(2, "C", "C", g0)
        kb.barrier()

    def phase_glu(self):
        kb = self.kb
        Tn, NCH = self.T, self.NCH
        g = self.gtiles
        bglu, bglub = g["bglu"]
        with contextlib.ExitStack() as es:
            wg = self.sb(es, "g_w", [128, 8, 1024], BF16)
            wgb = Buf()
            wst = [self.sb(es, "g_wst%d" % i, [128, 1024], F32) for i in range(2)]
            wstb = [Buf(), Buf()]
            for kc in range(8):
                kb.dma(wst[kc % 2][:], self.w_glu[kc * 128:(kc + 1) * 128, :], writes=[wstb[kc % 2]])
                kb.op("pool", lambda h: h.tensor_copy(wg[:, kc, :], wst[kc % 2][:]), reads=[wstb[kc % 2]], writes=[wgb])
            yt = [self.sb(es, "g_yt%d" % i, [128, 8, TT], BF16) for i in range(2)]
            at = [self.sb(es, "g_at%d" % i, [128, 8, TT], BF16) for i in range(2)]
            ytb = [Buf(), Buf()]
            atb = [Buf(), Buf()]
            zst = [self.sb(es, "g_z%d" % i, [128, Tn], BF16) for i in range(8)]
            zstb = [Buf() for _ in range(8)]
            sg = [self.sb(es, "g_sg%d" % i, [128, TT], F32) for i in range(2)]
            sgb = [Buf(), Buf()]
            JT = TT // NCH
            n = 0
            it = 0
            for s in range(NSEG):
                for ct in range(Tn // TT):
                    sl = it % 2
                    it += 1
                    kb.dma(yt[sl][:], self.YT[s, :, ct * TT:(ct + 1) * TT].rearrange("(k p) n -> p k n", p=128), writes=[ytb[sl]])
                    kb.dma(at[sl][:], self.AG[s, :, ct * TT:(ct + 1) * TT].rearrange("(k p) n -> p k n", p=128), writes=[atb[sl]])
                    for fo in range(8):
                        pi = n % 3
                        e = n % 2
                        n += 1
                        fns = [(lambda h, kc=kc: h.matmul(self.ps[pi][:, 0:TT], wg[:, kc, fo * 128:(fo + 1) * 128], yt[sl][:, kc, :],
                                                          start=(kc == 0), stop=(kc == 7))) for kc in range(8)]
                        kb.group("pe", fns, reads=[wgb, ytb[sl]], writes=[self.psb[pi]])
                        kb.op("act", lambda h: h.activation(out=sg[e][:], in_=self.ps[pi][:, 0:TT], func=AF.Sigmoid, bias=bglu[:, fo:fo + 1]),
                              reads=[self.psb[pi], bglub], writes=[sgb[e]])
                        kb.op("dve", lambda h: h.tensor_tensor(out=sg[e][:], in0=sg[e][:], in1=yt[sl][:, fo, :], op=ALU.mult),
                              reads=[sgb[e], ytb[sl]], writes=[sgb[e]])
                        outv = zst[fo][:].rearrange("p (m j) -> p j m", j=8)[:, ct * JT:(ct + 1) * JT, :]
                        kb.op("dve", lambda h: h.tensor_tensor(out=outv, in0=sg[e][:].rearrange("p (j m) -> p j m", j=JT),
                                                               in1=at[sl][:, fo, :].rearrange("p (j m) -> p j m", j=JT), op=ALU.mult),
                              reads=[sgb[e], atb[sl]], writes=[zstb[fo]])
                for fo in range(8):
                    kb.dma(self.ZT[s, fo * 128:(fo + 1) * 128, :], zst[fo][:], reads=[zstb[fo]])
        kb.barrier()

    def attn_setup(self, es, pfx):
        st = {
            "PT": [self.sb(es, pfx + "PT%d" % i, [128, TT], BF16) for i in range(5)],
            "PTb": [Buf() for _ in range(5)],
            "RL": [self.sb(es, pfx + "RL%d" % i, [128, TT], F32) for i in range(2)],
            "RLb": [Buf(), Buf()],
            "AC": [self.sb(es, pfx + "AC%d" % i, [128, TT], BF16) for i in range(2)],
            "ACb": [Buf(), Buf()],
            "acn": 0,
            "n": 0, "u": 0,
        }
        return st

    def attn_unit(self, st, q_ap, qb, blocks, scale, gate_ap, gateb, out_ap, outb, after=None):
        kb = self.kb
        ones, onesb = self.gtiles["ones"]
        u = st["u"] % 2
        st["u"] += 1
        pO, pL = 4 + u, 6 + u
        nb = len(blocks)
        LOOK = 3
        slots = {}

        def emit_s(i):
            k_ap, kbuf, v_ap, vbuf, b_ap, bbuf, m_ap, mbuf = blocks[i]
            pS = st["n"] % 4
            sl = st["n"] % 5
            st["n"] += 1
            slots[i] = sl
            PT, PTb = st["PT"][sl], st["PTb"][sl]
            kb.group("pe", [lambda h: h.matmul(self.ps[pS][:, 0:TT], k_ap, q_ap, start=True, stop=True)], reads=[kbuf, qb], writes=[self.psb[pS]])
            kb.op("act", lambda h: h.activation(out=PT[:], in_=self.ps[pS][:, 0:TT], func=AF.Exp, scale=scale, bias=b_ap),
                  reads=[self.psb[pS], bbuf], writes=[PTb])
            if m_ap is not None:
                en = "pool" if (i % 3 == 2) else "dve"
                kb.op(en, lambda h: h.tensor_tensor(out=PT[:], in0=PT[:], in1=m_ap, op=ALU.mult), reads=[PTb, mbuf], writes=[PTb])

        def emit_pv(i):
            k_ap, kbuf, v_ap, vbuf, b_ap, bbuf, m_ap, mbuf = blocks[i]
            sl = slots[i]
            PT, PTb = st["PT"][sl], st["PTb"][sl]
            fns = [lambda h: h.matmul(self.ps[pO][:, 0:TT], v_ap, PT[:], start=(i == 0), stop=(i == nb - 1))]
            rd = [vbuf, PTb, onesb]
            wr = [self.psb[pO]]
            if i % 2 == 1:
                a = st["acn"] % 2
                st["acn"] += 1
                AC, ACb = st["AC"][a], st["ACb"][a]
                P0, P0b = st["PT"][slots[i - 1]], st["PTb"][slots[i - 1]]
                kb.op("dve", lambda h: h.tensor_tensor(out=AC[:], in0=P0[:], in1=PT[:], op=ALU.add), reads=[P0b, PTb], writes=[ACb])
                fns.append(lambda h: h.matmul(self.ps[pL][:, 0:TT], ones[:], AC[:], start=(i == 1), stop=(i == nb - 1)))
                rd.append(ACb)
                wr.append(self.psb[pL])
            elif i == nb - 1:
                fns.append(lambda h: h.matmul(self.ps[pL][:, 0:TT], ones[:], PT[:], start=(i == 0), stop=True))
                wr.append(self.psb[pL])
            kb.group("pe", fns, reads=rd, writes=wr)

        for i in range(nb + LOOK):
            if i < nb:
                emit_s(i)
            if i >= LOOK:
                emit_pv(i - LOOK)
            fin_prev = st.get("fin")
            if fin_prev is not None:
                if nb >= 8:
                    if 2 <= i < 6:
                        fin_prev(i - 2)
                    if i == 6:
                        fin_prev(4)
                        st.pop("fin")
                elif i == nb - 1:
                    for c in range(5):
                        fin_prev(c)
                    st.pop("fin")
        RL, RLb = st["RL"][u], st["RLb"][u]

        def fin(c):
            if c < 4:
                cs_ = slice(c * 128, (c + 1) * 128)
                kb.op("dve", lambda h: h.reciprocal(RL[:, cs_], self.ps[pL][:, cs_]), reads=[self.psb[pL]], writes=[RLb])
                kb.op("dve", lambda h: h.tensor_tensor(out=RL[:, cs_], in0=RL[:, cs_], in1=self.ps[pO][:, cs_], op=ALU.mult), reads=[RLb, self.psb[pO]], writes=[RLb])
            else:
                kb.op("dve", lambda h: h.tensor_tensor(out=out_ap, in0=RL[:], in1=gate_ap, op=ALU.mult), reads=[RLb, gateb], writes=[outb])
                if after is not None:
                    after()
        st["fin"] = fin

    def run_units(self, units):
        n = len(units)
        if n:
            units[0][0]()
        for i in range(n):
            if i + 1 < n:
                units[i + 1][0]()
            units[i][1]()
            units[i][2]()

    def phase_gqa(self):
        kb = self.kb
        Tn, NB, NTT = self.T, self.NB, self.NTT
        g = self.gtiles
        misc, miscb = g["misc"]
        scale = 128.0 ** -0.5
        with contextlib.ExitStack() as es:
            st = self.attn_setup(es, "a_")
            kt = [self.sb(es, "a_kt%d" % i, [128, 2, Tn], BF16) for i in range(2)]
            vt = [self.sb(es, "a_vt%d" % i, [128, 2, NB, 128], BF16) for i in range(2)]
            ktb, vtb = [Buf(), Buf()], [Buf(), Buf()]
            qt_ = [self.sb(es, "a_q%d" % i, [128, TT], BF16) for i in range(3)]
            gt_ = [self.sb(es, "a_g%d" % i, [128, TT], BF16) for i in range(3)]
            zt_ = [self.sb(es, "a_z%d" % i, [128, TT], BF16) for i in range(3)]
            qtb, gtb, ztb = [Buf() for _ in range(3)], [Buf() for _ in range(3)], [Buf() for _ in range(3)]
            units = []
            it = 0
            hn = 0
            for segs in ((0, 1), (2,)):
                for kh in range(2):
                    hs = hn % 2
                    hn += 1
                    first = True
                    for li, s in enumerate(segs):
                        for h in range(4 * kh, 4 * kh + 4):
                            for tq in range(NTT):
                                sl3 = it % 3
                                sl2 = it % 3
                                it += 1

                                def load(first=first, hs=hs, segs=segs, kh=kh, s=s, h=h, tq=tq, sl3=sl3):
                                    if first:
                                        for lj, s2 in enumerate(segs):
                                            kb.dma(kt[hs][:, lj, :], self.KT[s2, kh], writes=[ktb[hs]])
                                            kb.dma(vt[hs][:, lj, :, :], dap(self.VV, s2 * Tn * 256 + kh * 128, [[256, 128], [128 * 256, NB], [1, 128]]), writes=[vtb[hs]])
                                    kb.dma(qt_[sl3][:], self.QT[s, h, :, tq * TT:(tq + 1) * TT], writes=[qtb[sl3]])
                                    kb.dma(gt_[sl3][:], self.BG[s, h * 128:(h + 1) * 128, tq * TT:(tq + 1) * TT], writes=[gtb[sl3]])

                                def compute(hs=hs, segs=segs, li=li, sl3=sl3, sl2=sl2, s=s, h=h, tq=tq):
                                    blocks = []
                                    for lj in range(len(segs)):
                                        b_ap = misc[:, 6:7] if lj == li else misc[:, 7:8]
                                        for b in range(NB):
                                            blocks.append((kt[hs][:, lj, b * 128:(b + 1) * 128], ktb[hs], vt[hs][:, lj, b, :], vtb[hs], b_ap, miscb, None, None))
                                    self.attn_unit(st, qt_[sl3][:], qtb[sl3], blocks, scale, gt_[sl3][:], gtb[sl3], zt_[sl2][:], ztb[sl2],
                                                   after=lambda: kb.dma(self.ZT[s, 1024 + h * 128:1024 + (h + 1) * 128, tq * TT:(tq + 1) * TT], zt_[sl2][:], reads=[ztb[sl2]]))
                                units.append((load, compute, lambda: None))
                                first = False
            self.run_units(units)
            for c in range(5):
                st["fin"](c)
            st.pop("fin")
        kb.barrier()

    def phase_dil(self):
        kb = self.kb
        Tn, NB, NTT = self.T, self.NB, self.NTT
        g = self.gtiles
        misc, miscb = g["misc"]
        fl, flb = g["flag"]
        scale = 128.0 ** -0.5
        with contextlib.ExitStack() as es:
            st = self.attn_setup(es, "d_")
            dm = self.sb(es, "d_mask", [128, 20, TT], BF16)
            dmb = Buf()
            kb.dma(dm[:], self.c_dmask.rearrange("i p n -> p i n"), writes=[dmb])
            kt = [self.sb(es, "d_kt%d" % i, [128, 2, Tn], BF16) for i in range(2)]
            vt = [self.sb(es, "d_vt%d" % i, [128, 2, NB, 128], BF16) for i in range(2)]
            ktb, vtb = [Buf(), Buf()], [Buf(), Buf()]
            qt_ = [self.sb(es, "d_q%d" % i, [128, TT], BF16) for i in range(3)]
            gt_ = [self.sb(es, "d_g%d" % i, [128, TT], BF16) for i in range(3)]
            zt_ = [self.sb(es, "d_z%d" % i, [128, TT], BF16) for i in range(3)]
            qtb, gtb, ztb = [Buf() for _ in range(3)], [Buf() for _ in range(3)], [Buf() for _ in range(3)]
            units = []
            it = 0
            hn = 0
            for segs in ((0, 1), (2,)):
                span = len(segs) * Tn
                for n in range(16):
                    hs = hn % 2
                    hn += 1
                    first = True
                    for li, s in enumerate(segs):
                        for tq in range(NTT):
                            sl3 = it % 3
                            sl2 = it % 3
                            it += 1

                            def load(first=first, hs=hs, segs=segs, n=n, s=s, tq=tq, sl3=sl3):
                                if first:
                                    for lj, s2 in enumerate(segs):
                                        kb.dma(kt[hs][:, lj, :], self.K2T[s2, n], writes=[ktb[hs]])
                                        kb.dma(vt[hs][:, lj, :, :], dap(self.V2, s2 * Tn * D + n * 128, [[D, 128], [128 * D, NB], [1, 128]]), writes=[vtb[hs]])
                                kb.dma(qt_[sl3][:], self.Q2T[s, n, :, tq * TT:(tq + 1) * TT], writes=[qtb[sl3]])
                                kb.dma(gt_[sl3][:], self.G2T[s, n * 128:(n + 1) * 128, tq * TT:(tq + 1) * TT], writes=[gtb[sl3]])

                            def compute(hs=hs, li=li, tq=tq, span=span, sl3=sl3, sl2=sl2, s=s, n=n):
                                Q0 = li * Tn + tq * TT
                                blocks = []
                                for mi in range(20):
                                    ks = Q0 - 1024 + mi * 128
                                    if ks < 0 or ks >= span:
                                        continue
                                    lj, b = ks // Tn, (ks % Tn) // 128
                                    b_ap = misc[:, 2:3] if lj == li else fl[:, 1:2]
                                    bb = miscb if lj == li else flb
                                    blocks.append((kt[hs][:, lj, b * 128:(b + 1) * 128], ktb[hs], vt[hs][:, lj, b, :], vtb[hs], b_ap, bb, dm[:, mi, :], dmb))
                                self.attn_unit(st, qt_[sl3][:], qtb[sl3], blocks, scale, gt_[sl3][:], gtb[sl3], zt_[sl2][:], ztb[sl2],
                                               after=lambda: kb.dma(self.ZT[s, n * 128:(n + 1) * 128, tq * TT:(tq + 1) * TT], zt_[sl2][:], reads=[ztb[sl2]]))
                            units.append((load, compute, lambda: None))
                            first = False
            self.run_units(units)
            for c in range(5):
                st["fin"](c)
            st.pop("fin")
        kb.barrier()

    def phase_out(self, layer):
        kb = self.kb
        Tn, NB = self.T, self.NB
        g = self.gtiles
        misc, miscb = g["misc"]
        W = self.w_out_e if layer == 0 else self.w_out_o
        with contextlib.ExitStack() as es:
            wo = self.sb(es, "o_w", [128, KC, D], BF16)
            wob = Buf()
            xr_ = [self.sb(es, "o_xr%d" % i, [128, D], F32) for i in range(2)]
            xrb = [Buf(), Buf()]
            for kc in range(KC):
                kb.dma(xr_[kc % 2][:], W[kc * 128:(kc + 1) * 128, :], writes=[xrb[kc % 2]])
                kb.op("pool", lambda h: h.tensor_copy(wo[:, kc, :], xr_[kc % 2][:]), reads=[xrb[kc % 2]], writes=[wob])
            lg = self.sb(es, "o_lg", [128, D], F32)
            lb = self.sb(es, "o_lb", [128, D], F32)
            lgb = Buf()
            kb.dma(lg[:], dap(self.ln_g, layer * D, [[0, 128], [1, D]]), writes=[lgb])
            kb.dma(lb[:], dap(self.ln_b, layer * D, [[0, 128], [1, D]]), writes=[lgb])
            zb_ = [self.sb(es, "o_zb%d" % i, [128, KC, TT], BF16) for i in range(2)]
            zbb = [Buf(), Buf()]
            v = self.sb(es, "o_v", [128, D], F32)
            vb = Buf()
            xo = [self.sb(es, "o_xo%d" % i, [128, D], F32) for i in range(2)]
            xob = [Buf(), Buf()]
            stt_ = self.sb(es, "o_st", [128, 8], F32)
            stb = Buf()
            xbf = [self.sb(es, "o_xbf%d" % i, [128, D], BF16) for i in range(3)]
            xbfb = [Buf(), Buf(), Buf()]
            xTb_ = [self.sb(es, "o_xT%d" % i, [128, KC, TT], BF16) for i in range(2)]
            xTbb = [Buf(), Buf()]
            units = []
            it = 0
            hcnt = [0]
            src = self.x if layer == 0 else self.X1
            for s in range(NSEG):
                for b in range(NB):
                    sl = it % 2
                    it += 1

                    zs = (it - 1) // 4 % 2
                    bq = b % 4

                    s3 = (it - 1) % 3

                    def load(s=s, b=b, sl=sl):
                        kb.dma(xr_[sl][:], src[s, b * 128:(b + 1) * 128, :], writes=[xrb[sl]])

                    def loadz(s=s, b=b, zs=zs):
                        kb.dma(zb_[zs][:], self.ZT[s, :, b * 128:b * 128 + TT].rearrange("(k p) n -> p k n", p=128), writes=[zbb[zs]])

                    def compute(s=s, b=b, sl=sl, zs=zs, bq=bq, s3=s3):
                        if layer == 1:
                            halves = [[4 * sl + i for i in range(4)]]
                        else:
                            halves = []
                            for hf in range(2):
                                pp = (2 * hcnt[0]) % 6
                                hcnt[0] += 1
                                halves.append([pp, pp + 1])
                        nt0 = 0
                        for banks in halves:
                            fns = []
                            for kc in range(KC):
                                for j, pbk in enumerate(banks):
                                    nt = nt0 + j
                                    fns.append(lambda h, kc=kc, nt=nt, pbk=pbk: h.matmul(self.ps[pbk][:, 0:512], zb_[zs][:, kc, bq * 128:(bq + 1) * 128], wo[:, kc, nt * 512:(nt + 1) * 512],
                                                                                     start=(kc == 0), stop=(kc == KC - 1)))
                            kb.group("pe", fns, reads=[zbb[zs], wob], writes=[self.psb[p_] for p_ in banks])
                            for j, pbk in enumerate(banks):
                                nt = nt0 + j
                                kb.op("dve", lambda h: h.scalar_tensor_tensor(out=v[:, nt * 512:(nt + 1) * 512], in0=xr_[sl][:, nt * 512:(nt + 1) * 512], scalar=ALPHA,
                                                                               in1=self.ps[pbk][:, 0:512], op0=ALU.mult, op1=ALU.add),
                                      reads=[xrb[sl], self.psb[pbk]], writes=[vb])
                            nt0 += len(banks)
                        kb.op("act", lambda h: h.activation(out=xbf[s3][:], in_=v[:], func=AF.Copy, accum_out=stt_[:, 0:1]), reads=[vb], writes=[xbfb[s3], stb])
                        kb.op("act", lambda h: h.activation(out=xbf[s3][:], in_=v[:], func=AF.Square, accum_out=stt_[:, 1:2]), reads=[vb], writes=[xbfb[s3], stb])
                        kb.op("dve", lambda h: h.tensor_scalar_mul(out=stt_[:, 2:3], in0=stt_[:, 0:1], scalar1=1.0 / D), reads=[stb], writes=[stb])
                        kb.op("dve", lambda h: h.tensor_tensor(out=stt_[:, 3:4], in0=stt_[:, 2:3], in1=stt_[:, 2:3], op=ALU.mult), reads=[stb], writes=[stb])
                        kb.op("dve", lambda h: h.scalar_tensor_tensor(out=stt_[:, 4:5], in0=stt_[:, 1:2], scalar=1.0 / D, in1=stt_[:, 3:4], op0=ALU.mult, op1=ALU.subtract),
                              reads=[stb], writes=[stb])
                        kb.op("act", lambda h: h.activation(out=stt_[:, 5:6], in_=stt_[:, 4:5], func=AF.Sqrt, bias=misc[:, 1:2]), reads=[stb, miscb], writes=[stb])
                        kb.op("dve", lambda h: h.reciprocal(stt_[:, 6:7], stt_[:, 5:6]), reads=[stb], writes=[stb])
                        xo_t, xo_b = xo[sl], xob[sl]
                        kb.op("dve", lambda h: h.scalar_tensor_tensor(out=xo_t[:], in0=v[:], scalar=stt_[:, 2:3], in1=lg[:], op0=ALU.subtract, op1=ALU.mult),
                              reads=[vb, stb, lgb], writes=[xo_b])
                        kb.op("dve", lambda h: h.scalar_tensor_tensor(out=xo_t[:], in0=xo_t[:], scalar=stt_[:, 6:7], in1=lb[:], op0=ALU.mult, op1=ALU.add),
                              reads=[xo_b, stb, lgb], writes=[xo_b])
                        if layer == 0:
                            kb.op("pool", lambda h: h.tensor_copy(xbf[s3][:], xo_t[:]), reads=[xo_b], writes=[xbfb[s3]])

                    def late(s=s, b=b, s3=s3, zs=zs, bq=bq):
                        if layer == 0:
                            self.transpose_block(xbf[s3], xbfb[s3], xTb_[zs], xTbb[zs], bq, "act" if b % 2 == 0 else "dve")
                            if bq == 3:
                                kb.dma(self.X1T[s, :, (b - 3) * 128:(b + 1) * 128].rearrange("(k p) n -> p k n", p=128), xTb_[zs][:], reads=[xTbb[zs]])

                    def store(s=s, b=b, sl=sl):
                        xo_t, xo_b = xo[sl], xob[sl]
                        if layer == 0:
                            kb.dma(self.X1[s, b * 128:(b + 1) * 128, :], xo_t[:], reads=[xo_b])
                        else:
                            kb.dma(self.y[s, b * 128:(b + 1) * 128, :], xo_t[:], reads=[xo_b])
                    units.append((load, compute, store, late, loadz))
            n = len(units)
            units[0][4]()
            units[0][0]()
            for i in range(n):
                if i + 1 < n:
                    units[i + 1][0]()
                if i % 4 == 0 and i + 4 < n:
                    units[i + 4][4]()
                units[i][1]()
                if i > 1:
                    units[i - 2][3]()
                units[i][2]()
            if n > 1:
                units[n - 2][3]()
            units[n - 1][3]()
        kb.barrier()

    def phase_odd_in(self, s):
        kb = self.kb
        Tn, NTT, NB = self.T, self.NTT, self.NB
        g = self.gtiles
        rotO, rotOb = g["rotO"]
        with contextlib.ExitStack() as es:
            xT = self.sb(es, "f_xT", [128, KC, Tn], BF16)
            xTb = [Buf() for _ in range(NB)]
            for tq in range(NTT):
                kb.dma(xT[:, :, tq * TT:(tq + 1) * TT], self.X1T[s, :, tq * TT:(tq + 1) * TT].rearrange("(k p) n -> p k n", p=128), writes=xTb[4 * tq:4 * tq + 4])
            cs = self.sb(es, "f_cs", [128, 2, Tn], F32)
            csb = Buf()
            kb.dma(cs[:, 0, :], self.cosO[s], writes=[csb])
            kb.dma(cs[:, 1, :], self.sinO[s], writes=[csb])
            stg = [self.sb(es, "f_stg%d" % i, [128, Tn], BF16) for i in range(2)]
            stgb = [Buf(), Buf()]
            vst = [self.sb(es, "f_vst%d" % i, [128, NB, 256], BF16) for i in range(2)]
            vstb = [Buf(), Buf()]
            qb_ = [self.sb(es, "f_qb%d" % i, [128, TT], BF16) for i in range(3)]
            t1 = [self.sb(es, "f_t1%d" % i, [128, TT], F32) for i in range(3)]
            t2 = [self.sb(es, "f_t2%d" % i, [128, TT], F32) for i in range(3)]
            tb = {k: [Buf(), Buf(), Buf()] for k in ("qb", "t1", "t2")}
            cnt = {"mm": 0, "ep": 0, "v": 0, "rc": 0}

            def mm_tile(wbf, wbfb, tt):
                pi = cnt["mm"] % 4
                cnt["mm"] += 1
                fns = [(lambda h, kc=kc: h.matmul(self.ps[pi][:, 0:TT], wbf[:, kc, 0:128], xT[:, kc, tt * TT:(tt + 1) * TT],
                                                  start=(kc == 0), stop=(kc == KC - 1))) for kc in range(KC)]
                kb.group("pe", fns, reads=[wbfb] + xTb[4 * tt:4 * tt + 4], writes=[self.psb[pi]])
                return pi

            def chunk_silu(fo, dst_ap):
                def run(wbf, wbfb):
                    run_due(flush=True)
                    sl = fo % 2
                    for tt in range(NTT):
                        pi = mm_tile(wbf, wbfb, tt)
                        kb.op("act", lambda h: h.activation(out=stg[sl][:, tt * TT:(tt + 1) * TT], in_=self.ps[pi][:, 0:TT], func=AF.Silu),
                              reads=[self.psb[pi]], writes=[stgb[sl]])
                    kb.dma(dst_ap, stg[sl][:], reads=[stgb[sl]])
                return run

            pend = []
            clock = {"k": 0}

            def run_due(flush=False):
                while pend and (flush or pend[0][0] <= clock["k"]):
                    pend.pop(0)[1]()

            def chunk_rope(fo, dst_ap):
                def run(wbf, wbfb):
                    sl = cnt["rc"] % 2
                    cnt["rc"] += 1
                    for tt in range(NTT):
                        k = clock["k"]
                        pi = mm_tile(wbf, wbfb, tt)
                        e = k % 3
                        pr = 4 + (k % 2)
                        kb.op("act", lambda h: h.copy(qb_[e][:], self.ps[pi][:, 0:TT]), reads=[self.psb[pi]], writes=[tb["qb"][e]])

                        def stage_b(pi=pi, e=e, pr=pr, tt=tt, sl=sl):
                            kb.group("pe", [lambda h: h.matmul(self.ps[pr][:, 0:TT], rotO[:], qb_[e][:], start=True, stop=True)],
                                     reads=[tb["qb"][e], rotOb], writes=[self.psb[pr]])
                            kb.op("dve", lambda h: h.tensor_tensor(out=t1[e][:], in0=self.ps[pi][:, 0:TT], in1=cs[:, 0, tt * TT:(tt + 1) * TT], op=ALU.mult),
                                  reads=[self.psb[pi], csb], writes=[tb["t1"][e]])
                            kb.op("dve", lambda h: h.tensor_tensor(out=t2[e][:], in0=self.ps[pr][:, 0:TT], in1=cs[:, 1, tt * TT:(tt + 1) * TT], op=ALU.mult),
                                  reads=[self.psb[pr], csb], writes=[tb["t2"][e]])
                            kb.op("dve", lambda h: h.tensor_tensor(out=stg[sl][:, tt * TT:(tt + 1) * TT], in0=t1[e][:], in1=t2[e][:], op=ALU.add),
                                  reads=[tb["t1"][e], tb["t2"][e]], writes=[stgb[sl]])
                        pend.append((k + 1, stage_b))
                        if tt == NTT - 1:
                            pend.append((k + 1, lambda sl=sl: kb.dma(dst_ap, stg[sl][:], reads=[stgb[sl]])))
                        clock["k"] += 1
                        run_due()
                return run

            def chunk_v(vc):
                def run(wbf, wbfb):
                    run_due(flush=True)
                    sl = vc % 2
                    for b in range(NB):
                        pi = cnt["mm"] % 4
                        cnt["mm"] += 1
                        fns = [(lambda h, kc=kc: h.matmul(self.ps[pi][:, 0:256], xT[:, kc, b * 128:(b + 1) * 128], wbf[:, kc, 0:256],
                                                          start=(kc == 0), stop=(kc == KC - 1))) for kc in range(KC)]
                        kb.group("pe", fns, reads=[wbfb, xTb[b]], writes=[self.psb[pi]])
                        kb.op("act", lambda h: h.copy(vst[sl][:, b, :], self.ps[pi][:, 0:256]), reads=[self.psb[pi]], writes=[vstb[sl]])
                    kb.dma(dap(self.V2, s * Tn * D + vc * 256, [[D, 128], [128 * D, NB], [1, 256]]), vst[sl][:], reads=[vstb[sl]])
                return run

            specs = []
            for fo in range(16):
                specs.append((fo * 128, 128, chunk_rope(fo, self.Q2T[s, fo])))
            for fo in range(16):
                specs.append((2048 + fo * 128, 128, chunk_rope(fo, self.K2T[s, fo])))
            for vc in range(8):
                specs.append((4096 + vc * 256, 256, chunk_v(vc)))
            for fo in range(16):
                specs.append((6144 + fo * 128, 128, chunk_silu(fo, self.G2T[s, fo * 128:(fo + 1) * 128, :])))
            import os
            sel = os.environ.get("ODD_SEL")
            if sel:
                a, b = [int(v) for v in sel.split(":")]
                specs = specs[a:b]
            self.linear(es, self.w_in_o, ODD_IN, xT, xTb, specs, "f_")
            run_due(flush=True)
        kb.barrier()

    def build(self):
        nc = self.nc

        def ph(name, fn, *a):
            with nc.named_scope(name):
                fn(*a)
        ph("consts", self.load_consts)
        ph("ssm_pre", self.phase_ssm_pre)
        for s in range(NSEG):
            ph("even_in%d" % s, self.phase_even_in, s)
        ph("ssm", self.phase_ssm)
        ph("glu", self.phase_glu)
        ph("gqa", self.phase_gqa)
        ph("out0", self.phase_out, 0)
        for s in range(NSEG):
            ph("odd_in%d" % s, self.phase_odd_in, s)
        ph("dil", self.phase_dil)
        ph("out1", self.phase_out, 1)
        self.kb.finish()
        return self.nc


def _rot_tables(Tn, pos0, kind):
    f32 = np.float32
    pos = (np.arange(Tn, dtype=f32) + f32(pos0))
    cos = np.ones((128, Tn), f32)
    sin = np.zeros((128, Tn), f32)
    if kind == "E":
        n = 64
        inv = (f32(10000.0) ** (-(np.arange(0, n, 2, dtype=f32) / f32(n)))).astype(f32)
        row = np.floor(pos / f32(64)).astype(f32)
        col = (pos - row * f32(64)).astype(f32)
        for half, pp in ((0, row), (1, col)):
            ang = (pp[None, :] * inv[:, None]).astype(f32)
            c, s_ = np.cos(ang).astype(f32), np.sin(ang).astype(f32)
            cos[half * 64:half * 64 + 32] = c
            cos[half * 64 + 32:half * 64 + 64] = c
            sin[half * 64:half * 64 + 32] = s_
            sin[half * 64 + 32:half * 64 + 64] = s_
    else:
        n = 32
        inv = (f32(500000.0) ** (-(np.arange(0, n, 2, dtype=f32) / f32(n)))).astype(f32)
        ang = (pos[None, :] * inv[:, None]).astype(f32)
        c, s_ = np.cos(ang).astype(f32), np.sin(ang).astype(f32)
        cos[0:16] = c
        cos[16:32] = c
        sin[0:16] = s_
        sin[16:32] = s_
    return cos, sin


def _rot_mats():
    bf = ml_dtypes.bfloat16
    rE = np.zeros((128, 128), np.float32)
    for m in range(128):
        if (m % 64) < 32:
            rE[m + 32, m] = -1.0
        else:
            rE[m - 32, m] = 1.0
    rO = np.zeros((128, 128), np.float32)
    for m in range(32):
        if m < 16:
            rO[m + 16, m] = -1.0
        else:
            rO[m - 16, m] = 1.0
    return rE.astype(bf), rO.astype(bf)


def _dil_masks():
    out = np.zeros((20, 128, 512), np.float32)
    k = np.arange(128)[:, None]
    q = np.arange(512)[None, :]
    for i in range(20):
        dl = (i * 128 - 1024) + k - q
        a = np.abs(dl)
        m = (a <= 64).astype(np.float32)
        m += ((a <= 256) & (dl % 4 == 0)).astype(np.float32)
        m += ((a <= 1024) & (dl % 16 == 0)).astype(np.float32)
        out[i] = m
    return out.astype(ml_dtypes.bfloat16)


def _mmasks():
    jc = np.arange(128) // 16
    mf = (jc[:, None] <= jc[None, :]).astype(np.float32)
    mb = (jc[:, None] >= jc[None, :]).astype(np.float32)
    return np.stack([mf, mb], 0)


def make_core_inputs(inputs, Tn=T):
    xp = np.asarray(inputs["x_prompt"])
    xs = np.asarray(inputs["x_sample"])
    rE, rO = _rot_mats()
    shared = {
        "w_in_e": np.ascontiguousarray(inputs["even_w_in"][0]), "w_out_e": np.ascontiguousarray(inputs["even_w_out"][0]),
        "w_glu": np.ascontiguousarray(inputs["ssm_glu_w"][0]), "b_glu": np.ascontiguousarray(inputs["ssm_glu_b"][0]),
        "w_in_o": np.ascontiguousarray(inputs["odd_w_in"][0]), "w_out_o": np.ascontiguousarray(inputs["odd_w_out"][0]),
        "ln_g": np.ascontiguousarray(inputs["ln_g"]), "ln_b": np.ascontiguousarray(inputs["ln_b"]),
        "a_re": np.ascontiguousarray(inputs["ssm_a_re"][0]), "a_im": np.ascontiguousarray(inputs["ssm_a_im"][0]),
        "log_dt": np.ascontiguousarray(inputs["ssm_log_dt"][0]),
        "b_re": np.ascontiguousarray(inputs["ssm_b_re"][0]), "b_im": np.ascontiguousarray(inputs["ssm_b_im"][0]),
        "c_re": np.ascontiguousarray(inputs["ssm_c_re"][0]), "c_im": np.ascontiguousarray(inputs["ssm_c_im"][0]),
        "d_skip": np.ascontiguousarray(inputs["ssm_d"][0]),
        "gq": np.ascontiguousarray(inputs["attn_q_norm"][0]), "gk": np.ascontiguousarray(inputs["attn_k_norm"][0]),
        "identf": np.eye(128, dtype=np.float32), "identb": np.eye(128, dtype=np.float32).astype(ml_dtypes.bfloat16),
        "rotE": rE, "rotO": rO, "dmask": _dil_masks(), "mmask": _mmasks(),
    }
    shared = {k: np.asarray(v, dtype=v.dtype if v.dtype != np.float64 else np.float32) for k, v in shared.items()}
    tabs = {}
    for linked in (0, 1):
        p0 = [0, Tn * linked, 0]
        cE, sE, cO, sO = [], [], [], []
        for sg in range(NSEG):
            c, s_ = _rot_tables(Tn, p0[sg], "E")
            cE.append(c); sE.append(s_)
            c, s_ = _rot_tables(Tn, p0[sg], "O")
            cO.append(c); sO.append(s_)
        fl = np.zeros((128, 2), np.float32)
        fl[:, 0] = float(linked)
        fl[:, 1] = 0.0 if linked else NEG
        tabs[linked] = {"cosE": np.stack(cE), "sinE": np.stack(sE), "cosO": np.stack(cO), "sinO": np.stack(sO), "flag": fl}
    maps = []
    for c in range(8):
        if c < 4:
            xx = np.stack([xp[c, 0:Tn], xp[c, Tn:2 * Tn], xs[c, 0:Tn]], 0)
            linked = 1
        else:
            b0 = 4 + 3 * (c - 4)
            xx = np.stack([xs[b0, 0:Tn], xs[b0 + 1, 0:Tn], xs[b0 + 2, 0:Tn]], 0)
            linked = 0
        m = dict(shared)
        m.update(tabs[linked])
        m["x"] = np.ascontiguousarray(xx, dtype=np.float32)
        maps.append(m)
    return maps


_PROG_CACHE = {}


def kernel(**inputs):
    maps = make_core_inputs(inputs, T)
    if "nc" not in _PROG_CACHE:
        _PROG_CACHE["nc"] = Prog(T_=T).build()
    nc = _PROG_CACHE["nc"]
    res = run_bass_kernel_spmd(nc, maps, core_ids=list(range(8)))
    yp = np.empty((4, 2 * T, D), np.float32)
    ys = np.empty((16, T, D), np.float32)
    for c in range(8):
        y = np.asarray(res.results[c]["y"])
        if c < 4:
            yp[c, 0:T] = y[0]
            yp[c, T:2 * T] = y[1]
            ys[c] = y[2]
        else:
            b0 = 4 + 3 * (c - 4)
            ys[b0], ys[b0 + 1], ys[b0 + 2] = y[0], y[1], y[2]
    return (yp, ys)
```

```python
import contextlib
import math
import numpy as np
import ml_dtypes
import concourse.bass as bass
import concourse.mybir as mybir
from concourse.bass_utils import run_bass_kernel_spmd
from concourse.ap import AP

F32 = mybir.dt.float32
BF16 = mybir.dt.bfloat16
AF = mybir.ActivationFunctionType
ALU = mybir.AluOpType

D = 2048
T = 2048
NSEG = 3
KC = 16
TT = 512
EVEN_IN = 4608
ODD_IN = 8192
G = 64
GB = 16
GBP = 8
ALPHA = 4 ** 0.25
LN_EPS = 1e-5
QK_EPS = 1e-6
NEG = -30000.0
SEM_WRAP = 24000


class Tok:
    __slots__ = ("sid", "sem", "val")

    def __init__(self, sid, sem, val):
        self.sid, self.sem, self.val = sid, sem, val


class Buf:
    __slots__ = ("name", "w", "r", "excl")

    def __init__(self, name="", excl=False):
        self.name = name
        self.w = None
        self.r = {}
        self.excl = excl


class Eng:
    def __init__(self, kb, name, h):
        self.kb, self.name, self.h = kb, name, h
        self.sem = None
        self.sid = None
        self.cnt = 0
        self.seen = {}
        self.last = None


class KB:
    def __init__(self, nc):
        self.nc = nc
        self.es = contextlib.ExitStack()
        self.nsem = 0
        self.e = {
            "pe": Eng(self, "pe", nc.tensor),
            "act": Eng(self, "act", nc.scalar),
            "dve": Eng(self, "dve", nc.vector),
            "pool": Eng(self, "pool", nc.gpsimd),
            "sp": Eng(self, "sp", nc.sync),
        }
        self.dsems = []
        self.dnext = 0
        self.NDS = 40
        self.all_dma_toks = {}

    def new_sem(self):
        s = self.es.enter_context(self.nc.semaphore("s%d" % self.nsem))
        self.nsem += 1
        return (self.nsem, s)

    def wait(self, en, tok):
        if tok is None:
            return
        e = self.e[en]
        if e.seen.get(tok.sid, 0) >= tok.val:
            return
        e.h.wait_ge(tok.sem, tok.val)
        e.seen[tok.sid] = tok.val

    def signal(self, en, inst):
        e = self.e[en]
        if e.sem is None or e.cnt >= SEM_WRAP:
            e.sid, e.sem = self.new_sem()
            e.cnt = 0
        inst.then_inc(e.sem, 1)
        e.cnt += 1
        t = Tok(e.sid, e.sem, e.cnt)
        e.last = t
        return t

    def _deps(self, en, reads, writes):
        for b in reads:
            if b.w is not None and not (en == "pe" and b.w.sid == self.e["pe"].sid):
                self.wait(en, b.w)
            if b.excl:
                for t in b.r.values():
                    if t.sid != self.e[en].sid:
                        self.wait(en, t)
        for b in writes:
            if b.w is not None and not (en == "pe" and b.w.sid == self.e["pe"].sid):
                self.wait(en, b.w)
            for t in b.r.values():
                if not (en == "pe" and t.sid == self.e["pe"].sid):
                    self.wait(en, t)

    def _mark(self, tok, reads, writes):
        for b in reads:
            b.r[tok.sid] = tok
        for b in writes:
            b.w = tok
            b.r = {}

    def op(self, en, fn, reads=(), writes=()):
        self._deps(en, reads, writes)
        inst = fn(self.e[en].h)
        tok = self.signal(en, inst)
        self._mark(tok, reads, writes)
        return tok

    def group(self, en, fns, reads=(), writes=()):
        self._deps(en, reads, writes)
        inst = None
        for fn in fns:
            inst = fn(self.e[en].h)
        tok = self.signal(en, inst)
        self._mark(tok, reads, writes)
        return tok

    def dma(self, out, in_, reads=(), writes=(), q="sp", slow=False):
        self._deps(q, reads, writes)
        if len(self.dsems) < self.NDS:
            sid, sem = self.new_sem()
            ent = [sid, sem, 0, None]
            self.dsems.append(ent)
        else:
            ent = self.dsems[self.dnext % self.NDS]
        self.dnext += 1
        if ent[3] is not None:
            self.wait(q, ent[3])
        if slow:
            self.e[q].h.dma_start(out=out, in_=in_, allow_slow_non_contiguous=True).then_inc(ent[1], 16)
        else:
            self.e[q].h.dma_start(out=out, in_=in_).then_inc(ent[1], 16)
        ent[2] += 16
        tok = Tok(ent[0], ent[1], ent[2])
        ent[3] = tok
        self.all_dma_toks[ent[0]] = tok
        self._mark(tok, reads, writes)
        return tok

    def barrier(self):
        toks = [e.last for e in self.e.values() if e.last is not None] + list(self.all_dma_toks.values())
        for en in self.e:
            for t in toks:
                if en == "pe" and t.sid == self.e["pe"].sid:
                    continue
                self.wait(en, t)

    def finish(self):
        toks = [e.last for e in self.e.values() if e.last is not None] + list(self.all_dma_toks.values())
        for t in toks:
            self.wait("sp", t)


def dap(base, off, dims):
    return AP(base.tensor, base.offset + off, [list(d) for d in dims])


def rev_last(ap, n):
    dims = [list(d) for d in ap.ap]
    assert dims[-1][0] == 1
    dims[-1] = [-1, n]
    return AP(ap.tensor, ap.offset + (n - 1), dims)


class Prog:
    def __init__(self, T_=T, dbg=()):
        self.T = T_
        self.NTT = T_ // TT
        self.NB = T_ // 128
        self.NCH = T_ // 8
        self.dbg = set(dbg)
        nc = bass.Bass("TRN2", target_bir_lowering=False)
        self.nc = nc
        self.kb = KB(nc)
        Tn = T_

        def din(name, shape, dt=F32):
            return nc.dram_tensor(name, list(shape), dt, kind="ExternalInput").ap()

        def dscr(name, shape, dt=BF16):
            kind = "ExternalOutput" if name in self.dbg else "Internal"
            return nc.dram_tensor(name, list(shape), dt, kind=kind).ap()

        self.x = din("x", [NSEG, Tn, D])
        self.y = nc.dram_tensor("y", [NSEG, Tn, D], F32, kind="ExternalOutput").ap()
        self.w_in_e = din("w_in_e", [D, EVEN_IN])
        self.w_out_e = din("w_out_e", [D, D])
        self.w_glu = din("w_glu", [1024, 1024])
        self.b_glu = din("b_glu", [1024])
        self.w_in_o = din("w_in_o", [D, ODD_IN])
        self.w_out_o = din("w_out_o", [D, D])
        self.ln_g = din("ln_g", [2, D])
        self.ln_b = din("ln_b", [2, D])
        self.a_re = din("a_re", [2, G, 64])
        self.a_im = din("a_im", [2, G, 64])
        self.log_dt = din("log_dt", [2, G])
        self.b_re = din("b_re", [G, 64, 16])
        self.b_im = din("b_im", [G, 64, 16])
        self.c_re = din("c_re", [2, G, 16, 64])
        self.c_im = din("c_im", [2, G, 16, 64])
        self.d_skip = din("d_skip", [1024])
        self.gq = din("gq", [128])
        self.gk = din("gk", [128])
        self.cosE = din("cosE", [NSEG, 128, Tn])
        self.sinE = din("sinE", [NSEG, 128, Tn])
        self.cosO = din("cosO", [NSEG, 128, Tn])
        self.sinO = din("sinO", [NSEG, 128, Tn])
        self.c_identf = din("identf", [128, 128])
        self.c_identb = din("identb", [128, 128], BF16)
        self.c_rotE = din("rotE", [128, 128], BF16)
        self.c_rotO = din("rotO", [128, 128], BF16)
        self.c_dmask = din("dmask", [20, 128, 512], BF16)
        self.c_mmask = din("mmask", [2, 128, 128])
        self.c_flag = din("flag", [128, 2])
        self.UT = dscr("UT", [NSEG, 1024, Tn])
        self.AG = dscr("AG", [NSEG, 1024, Tn])
        self.QT = dscr("QT", [NSEG, 8, 128, Tn])
        self.KT = dscr("KT", [NSEG, 2, 128, Tn])
        self.VV = dscr("VV", [NSEG, Tn, 256])
        self.BG = dscr("BG", [NSEG, 1024, Tn])
        self.YT = dscr("YT", [NSEG, 1024, Tn])
        self.ZT = dscr("ZT", [NSEG, D, Tn])
        self.X1 = dscr("X1", [NSEG, Tn, D], F32)
        self.X1T = dscr("X1T", [NSEG, D, Tn])
        self.Q2T = dscr("Q2T", [NSEG, 16, 128, Tn])
        self.K2T = dscr("K2T", [NSEG, 16, 128, Tn])
        self.V2 = dscr("V2", [NSEG, Tn, D])
        self.G2T = dscr("G2T", [NSEG, D, Tn])
        self.MG = dscr("MG", [G, 128, 128])
        self.BIN = dscr("BIN", [G, 128, 2, 128])
        self.COUT = dscr("COUT", [G, 128, 4, 128])
        self.scrbuf = {}
        self.ges = contextlib.ExitStack()
        self.ps = [self.ges.enter_context(nc.psum_tensor("ps%d" % i, [128, 512], F32)) for i in range(8)]
        self.psb = [Buf("ps%d" % i, excl=True) for i in range(8)]
        self.gtiles = {}

    def sb(self, es, name, shape, dt=F32):
        self._nsb = getattr(self, "_nsb", 0) + 1
        return es.enter_context(self.nc.sbuf_tensor("%s_%d" % (name, self._nsb), list(shape), dt))

    def dbuf(self, key):
        b = self.scrbuf.get(key)
        if b is None:
            b = Buf(str(key))
            self.scrbuf[key] = b
        return b

    def load_consts(self):
        kb, nc, es = self.kb, self.nc, self.ges
        g = self.gtiles

        def ld(name, src, shape, dt, slow=False):
            t = self.sb(es, "c_" + name, shape, dt)
            b = Buf(name)
            kb.dma(t[:], src, writes=[b], slow=slow)
            g[name] = (t, b)

        ld("identf", self.c_identf[:, :], [128, 128], F32)
        ld("identb", self.c_identb[:, :], [128, 128], BF16)
        ld("rotE", self.c_rotE[:, :], [128, 128], BF16)
        ld("rotO", self.c_rotO[:, :], [128, 128], BF16)
        ld("flag", self.c_flag[:, :], [128, 2], F32)
        ld("gq", dap(self.gq, 0, [[1, 128], [1, 1]]), [128, 1], F32)
        ld("gk", dap(self.gk, 0, [[1, 128], [1, 1]]), [128, 1], F32)
        ld("gqb", dap(self.gq, 0, [[0, 128], [1, 128]]), [128, 128], F32)
        ld("gkb", dap(self.gk, 0, [[0, 128], [1, 128]]), [128, 128], F32)
        ld("bglu", dap(self.b_glu, 0, [[1, 128], [128, 8]]), [128, 8], F32, slow=True)
        g["mu"] = (self.sb(es, "c_mu", [128, 2, G], F32), Buf("mu"))
        g["mup"] = (self.sb(es, "c_mup", [128, 2, G, 16], F32), Buf("mup"))
        t = self.sb(es, "c_misc", [128, 8], F32)
        b = Buf("misc")
        g["misc"] = (t, b)
        kb.op("dve", lambda h: h.memset(t[:, 0:1], QK_EPS), writes=[b])
        kb.op("dve", lambda h: h.memset(t[:, 1:2], LN_EPS), writes=[b])
        kb.op("dve", lambda h: h.memset(t[:, 2:3], 0.0), writes=[b])
        kb.op("dve", lambda h: h.memset(t[:, 3:4], math.pi / 2), writes=[b])
        o1 = self.sb(es, "c_ones", [128, 128], BF16)
        o2 = self.sb(es, "c_ones128", [128, 128], BF16)
        b1 = Buf("ones")
        kb.op("dve", lambda h: h.memset(o1[:], 1.0), writes=[b1])
        kb.op("dve", lambda h: h.memset(o2[:], 1.0 / 128), writes=[b1])
        g["ones"] = (o1, b1)
        g["ones128"] = (o2, b1)
        mq = t[:, 4:5]
        mk = t[:, 5:6]
        gqb, gqbb = g["gqb"]
        gkb, gkbb = g["gkb"]
        kb.op("dve", lambda h: h.tensor_reduce(out=mq, in_=gqb[:], axis=mybir.AxisListType.X, op=ALU.max,
                                                apply_absolute_value=True), reads=[gqbb], writes=[b])
        kb.op("dve", lambda h: h.tensor_reduce(out=mk, in_=gkb[:], axis=mybir.AxisListType.X, op=ALU.max,
                                                apply_absolute_value=True), reads=[gkbb], writes=[b])
        kb.op("dve", lambda h: h.scalar_tensor_tensor(out=t[:, 6:7], in0=mq, scalar=-math.sqrt(128.0), in1=mk,
                                                       op0=ALU.mult, op1=ALU.mult), reads=[b], writes=[b])
        fl, flb = g["flag"]
        kb.op("dve", lambda h: h.tensor_tensor(out=t[:, 7:8], in0=t[:, 6:7], in1=fl[:, 1:2], op=ALU.add),
              reads=[b, flb], writes=[b])

    def load_xT(self, es, src, xT, xTb, pfx):
        kb = self.kb
        xin = [self.sb(es, pfx + "xin%d" % i, [128, D], F32) for i in range(2)]
        xbf = [self.sb(es, pfx + "xbf%d" % i, [128, D], BF16) for i in range(2)]
        xinb = [Buf(), Buf()]
        xbfb = [Buf(), Buf()]
        ident, identb = self.gtiles["identb"]
        for b in range(self.NB):
            sl = b % 2
            kb.dma(xin[sl][:], src[b * 128:(b + 1) * 128, :], writes=[xinb[sl]])
            kb.op("pool", lambda h: h.tensor_copy(xbf[sl][:], xin[sl][:]), reads=[xinb[sl]], writes=[xbfb[sl]])
            self.transpose_block(xbf[sl], xbfb[sl], xT, xTb[b], b, "act" if b % 2 == 0 else "dve")

    def transpose_block(self, xbf, xbfb, xT, xTblk, b, evac_eng):
        kb = self.kb
        ident, identb = self.gtiles["identb"]
        for q in range(4):
            pi = 6 + (q % 2)
            psv = self.ps[pi][:].bitcast(BF16)
            fns = []
            for i in range(4):
                kc = 4 * q + i
                fns.append(lambda h, i=i, kc=kc: h.transpose(psv[:, i * 128:(i + 1) * 128], xbf[:, kc * 128:(kc + 1) * 128], ident[:]))
            kb.group("pe", fns, reads=[xbfb, identb], writes=[self.psb[pi]])
            outv = xT[:, 4 * q:4 * q + 4, b * 128:(b + 1) * 128]
            inv = psv[:, 0:512].rearrange("p (a n) -> p a n", a=4)
            if evac_eng == "act":
                kb.op("act", lambda h: h.copy(outv, inv), reads=[self.psb[pi]], writes=[xTblk])
            else:
                kb.op("dve", lambda h: h.tensor_copy(outv, inv), reads=[self.psb[pi]], writes=[xTblk])

    def linear(self, es, W, ncols, xT, xTb, specs, pfx, maxw=256):
        kb = self.kb
        wst = [self.sb(es, pfx + "wst%d" % i, [128, KC, maxw], F32) for i in range(2)]
        wbf = [self.sb(es, pfx + "wbf%d" % i, [128, KC, maxw], BF16) for i in range(2)]
        wstb = [Buf(), Buf()]
        wbfb = [Buf(), Buf()]

        def fetch(i):
            c0, wd, _ = specs[i]
            sl = i % 2
            src = dap(W, c0, [[ncols, 128], [128 * ncols, KC], [1, wd]])
            kb.dma(wst[sl][:, :, 0:wd], src, writes=[wstb[sl]])
            kb.op("pool", lambda h: h.tensor_copy(wbf[sl][:, :, 0:wd], wst[sl][:, :, 0:wd]), reads=[wstb[sl]], writes=[wbfb[sl]])

        fetch(0)
        for i in range(len(specs)):
            if i + 1 < len(specs):
                fetch(i + 1)
            specs[i][2](wbf[i % 2], wbfb[i % 2])

    def phase_even_in(self, s):
        kb, nc = self.kb, self.nc
        Tn, NTT, NB = self.T, self.NTT, self.NB
        g = self.gtiles
        with contextlib.ExitStack() as es:
            xT = self.sb(es, "e_xT", [128, KC, Tn], BF16)
            xTb = [Buf("xT%d" % b) for b in range(NB)]
            self.load_xT(es, self.x[s], xT, xTb, "e_")
            cs = self.sb(es, "e_cs", [128, 2, Tn], F32)
            csb = Buf("cs")
            kb.dma(cs[:, 0, :], self.cosE[s], writes=[csb])
            kb.dma(cs[:, 1, :], self.sinE[s], writes=[csb])
            stg = [self.sb(es, "e_stg%d" % i, [128, Tn], BF16) for i in range(2)]
            stgb = [Buf(), Buf()]
            vst = self.sb(es, "e_vst", [128, NB, 256], BF16)
            vstb = Buf()
            sqb = [self.sb(es, "e_sq%d" % i, [128, TT], BF16) for i in range(3)]
            sd = [self.sb(es, "e_sd%d" % i, [128, TT], F32) for i in range(3)]
            qn = [self.sb(es, "e_qn%d" % i, [128, TT], F32) for i in range(3)]
            qnb = [self.sb(es, "e_qnb%d" % i, [128, TT], BF16) for i in range(3)]
            t1 = [self.sb(es, "e_t1%d" % i, [128, TT], F32) for i in range(3)]
            t2 = [self.sb(es, "e_t2%d" % i, [128, TT], F32) for i in range(3)]
            tb = {k: [Buf(), Buf(), Buf()] for k in ("sq", "sd", "qn", "qnb", "t1", "t2")}
            misc, miscb = g["misc"]
            ones128, onesb = g["ones128"]
            rotE, rotEb = g["rotE"]
            cnt = {"mm": 0, "ep": 0}

            def mm_tile(wbf, wbfb, tt):
                pi = cnt["mm"] % 4
                cnt["mm"] += 1
                fns = [(lambda h, kc=kc: h.matmul(self.ps[pi][:, 0:TT], wbf[:, kc, 0:128], xT[:, kc, tt * TT:(tt + 1) * TT],
                                                  start=(kc == 0), stop=(kc == KC - 1))) for kc in range(KC)]
                kb.group("pe", fns, reads=[wbfb] + xTb[4 * tt:4 * tt + 4], writes=[self.psb[pi]])
                return pi

            def chunk_simple(fo, kind, dst_ap):
                def run(wbf, wbfb):
                    run_due(flush=True)
                    sl = fo % 2
                    for tt in range(NTT):
                        pi = mm_tile(wbf, wbfb, tt)
                        if kind in ("u", "ag"):
                            outv = stg[sl][:].rearrange("p (j m) -> p m j", j=8)[:, 64 * tt:64 * tt + 64, :]
                            inv = self.ps[pi][:, 0:TT].rearrange("p (m j) -> p m j", j=8)
                        else:
                            outv = stg[sl][:, tt * TT:(tt + 1) * TT]
                            inv = self.ps[pi][:, 0:TT]
                        fn = AF.Copy if kind == "u" else AF.Silu
                        kb.op("act", lambda h: h.activation(out=outv, in_=inv, func=fn), reads=[self.psb[pi]], writes=[stgb[sl]])
                    kb.dma(dst_ap, stg[sl][:], reads=[stgb[sl]], writes=[self.dbuf(("E", s, fo))])
                return run

            pend = []
            clock = {"k": 0}

            def run_due(flush=False):
                while pend and (flush or pend[0][0] <= clock["k"]):
                    pend.pop(0)[1]()

            def chunk_qk(fo, gname, dst_ap):
                gt, gtb = g[gname]

                def run(wbf, wbfb):
                    sl = fo % 2
                    for tt in range(NTT):
                        k = clock["k"]
                        pi = mm_tile(wbf, wbfb, tt)
                        e = k % 3
                        pm, pr = 4, 5
                        kb.op("act", lambda h: h.activation(out=sqb[e][:], in_=self.ps[pi][:, 0:TT], func=AF.Square),
                              reads=[self.psb[pi]], writes=[tb["sq"][e]])

                        def stage_b(pi=pi, e=e):
                            kb.group("pe", [lambda h: h.matmul(self.ps[pm][:, 0:TT], ones128[:], sqb[e][:], start=True, stop=True)],
                                     reads=[tb["sq"][e], onesb], writes=[self.psb[pm]])
                            kb.op("act", lambda h: h.activation(out=sd[e][:], in_=self.ps[pm][:, 0:TT], func=AF.Ln, bias=misc[:, 0:1]),
                                  reads=[self.psb[pm], miscb], writes=[tb["sd"][e]])
                            kb.op("act", lambda h: h.activation(out=sd[e][:], in_=sd[e][:], func=AF.Exp, scale=-0.5), reads=[tb["sd"][e]], writes=[tb["sd"][e]])
                            kb.op("dve", lambda h: h.scalar_tensor_tensor(out=qn[e][:], in0=self.ps[pi][:, 0:TT], scalar=gt[:, 0:1], in1=sd[e][:],
                                                                           op0=ALU.mult, op1=ALU.mult),
                                  reads=[self.psb[pi], tb["sd"][e], gtb], writes=[tb["qn"][e]])
                            kb.op("act", lambda h: h.copy(qnb[e][:], qn[e][:]), reads=[tb["qn"][e]], writes=[tb["qnb"][e]])

                        def stage_c(e=e, tt=tt, sl=sl):
                            kb.group("pe", [lambda h: h.matmul(self.ps[pr][:, 0:TT], rotE[:], qnb[e][:], start=True, stop=True)],
                                     reads=[tb["qnb"][e], rotEb], writes=[self.psb[pr]])
                            kb.op("dve", lambda h: h.tensor_tensor(out=t1[e][:], in0=qn[e][:], in1=cs[:, 0, tt * TT:(tt + 1) * TT], op=ALU.mult),
                                  reads=[tb["qn"][e], csb], writes=[tb["t1"][e]])
                            kb.op("dve", lambda h: h.tensor_tensor(out=t2[e][:], in0=self.ps[pr][:, 0:TT], in1=cs[:, 1, tt * TT:(tt + 1) * TT], op=ALU.mult),
                                  reads=[self.psb[pr], csb], writes=[tb["t2"][e]])
                            kb.op("dve", lambda h: h.tensor_tensor(out=stg[sl][:, tt * TT:(tt + 1) * TT], in0=t1[e][:], in1=t2[e][:], op=ALU.add),
                                  reads=[tb["t1"][e], tb["t2"][e]], writes=[stgb[sl]])
                        pend.append((k + 1, stage_b))
                        pend.append((k + 2, stage_c))
                        if tt == NTT - 1:
                            pend.append((k + 2, lambda sl=sl: kb.dma(dst_ap, stg[sl][:], reads=[stgb[sl]], writes=[self.dbuf(("E", s, fo))])))
                        pend.sort(key=lambda t: t[0])
                        clock["k"] += 1
                        run_due()
                return run

            def chunk_v(half):
                def run(wbf, wbfb):
                    run_due(flush=True)
                    for b in range(NB):
                        pi = cnt["mm"] % 4
                        cnt["mm"] += 1
                        fns = [(lambda h, kc=kc: h.matmul(self.ps[pi][:, 0:128], xT[:, kc, b * 128:(b + 1) * 128], wbf[:, kc, 0:128],
                                                          start=(kc == 0), stop=(kc == KC - 1))) for kc in range(KC)]
                        kb.group("pe", fns, reads=[wbfb, xTb[b]], writes=[self.psb[pi]])
                        kb.op("act", lambda h: h.copy(vst[:, b, half * 128:(half + 1) * 128], self.ps[pi][:, 0:128]), reads=[self.psb[pi]], writes=[vstb])
                    if half == 1:
                        kb.dma(self.VV[s].rearrange("(b p) e -> p b e", p=128), vst[:], reads=[vstb], writes=[self.dbuf(("E", s, "v"))])
                return run

            specs = []
            for fo in range(36):
                c0 = fo * 128
                if fo < 8:
                    specs.append((c0, 128, chunk_simple(fo, "u", self.UT[s, fo * 128:(fo + 1) * 128, :])))
                elif fo < 16:
                    specs.append((c0, 128, chunk_simple(fo, "ag", self.AG[s, (fo - 8) * 128:(fo - 7) * 128, :])))
                elif fo < 24:
                    specs.append((c0, 128, chunk_qk(fo, "gq", self.QT[s, fo - 16])))
                elif fo < 26:
                    specs.append((c0, 128, chunk_qk(fo, "gk", self.KT[s, fo - 24])))
                elif fo < 28:
                    specs.append((c0, 128, chunk_v(fo - 26)))
                else:
                    specs.append((c0, 128, chunk_simple(fo, "bg", self.BG[s, (fo - 28) * 128:(fo - 27) * 128, :])))
            self.linear(es, self.w_in_e, EVEN_IN, xT, xTb, specs, "e_", maxw=128)
            run_due(flush=True)
        kb.barrier()

    def sview(self, tile, part0, nparts, off, dims):
        base = tile[:]
        pst = base.ap[0][0]
        return AP(base.tensor, base.offset + part0 * pst + off, [[pst, nparts]] + [list(d) for d in dims])

    def phase_ssm_pre(self):
        kb, nc = self.kb, self.nc
        g = self.gtiles
        identf, identfb = g["identf"]
        misc, miscb = g["misc"]
        MAGIC = 12582912.0
        TWO_PI = 2.0 * math.pi
        with contextlib.ExitStack() as es:
            def tl(name, shape, dt=F32):
                return self.sb(es, "p_" + name, shape, dt), Buf(name)
            Are, Areb = tl("Are", [128, G]); Aim, Aimb = tl("Aim", [128, G]); LDT, LDTb = tl("LDT", [128, G])
            Bre, Breb = tl("Bre", [128, G, 16]); Bim, Bimb = tl("Bim", [128, G, 16])
            Cre, Creb = tl("Cre", [128, G, 16]); Cim, Cimb = tl("Cim", [128, G, 16])
            Dsk, Dskb = tl("Dsk", [128, G])
            mmk, mmkb = tl("mmk", [128, 2, 128])
            kb.dma(mmk[:], self.c_mmask.rearrange("a p n -> p a n"), writes=[mmkb])
            for d in range(2):
                kb.dma(Are[64 * d:64 * d + 64, :], dap(self.a_re, d * G * 64, [[1, 64], [64, G]]), writes=[Areb], slow=True)
                kb.dma(Aim[64 * d:64 * d + 64, :], dap(self.a_im, d * G * 64, [[1, 64], [64, G]]), writes=[Aimb], slow=True)
                kb.dma(LDT[64 * d:64 * d + 64, :], dap(self.log_dt, d * G, [[0, 64], [1, G]]), writes=[LDTb])
                kb.dma(Bre[64 * d:64 * d + 64, :, :], dap(self.b_re, 0, [[16, 64], [1024, G], [1, 16]]), writes=[Breb])
                kb.dma(Bim[64 * d:64 * d + 64, :, :], dap(self.b_im, 0, [[16, 64], [1024, G], [1, 16]]), writes=[Bimb])
            for j in range(8):
                kb.dma(Dsk[16 * j:16 * j + 16, :], dap(self.d_skip, 0, [[1, 16], [16, G]]), writes=[Dskb], slow=True)
            ct = [tl("ct%d" % i, [128, 128]) for i in range(2)]
            n = 0
            for src, dst, dstb in ((self.c_re, Cre, Creb), (self.c_im, Cim, Cimb)):
                for blk in range(8):
                    c_t, c_b = ct[n % 2]
                    pi = n % 2
                    n += 1
                    kb.dma(c_t[:], dap(src, blk * 8 * 16 * 64, [[64, 128], [G * 16 * 64, 2], [1, 64]]), writes=[c_b])
                    kb.group("pe", [lambda h: h.transpose(self.ps[pi][:, 0:128], c_t[:], identf[:])], reads=[c_b, identfb], writes=[self.psb[pi]])
                    kb.op("act", lambda h: h.copy(dst[:, blk * 8:(blk + 1) * 8, :].rearrange("p a c -> p (a c)"), self.ps[pi][:, 0:128]),
                          reads=[self.psb[pi]], writes=[dstb])
            dt_, dtb = tl("dt", [128, G]); ard, ardb = tl("ard", [128, G]); aid, aidb = tl("aid", [128, G])
            kb.op("act", lambda h: h.activation(out=dt_[:], in_=LDT[:], func=AF.Exp), reads=[LDTb], writes=[dtb])
            kb.op("dve", lambda h: h.tensor_tensor(out=ard[:], in0=Are[:], in1=dt_[:], op=ALU.mult), reads=[Areb, dtb], writes=[ardb])
            kb.op("dve", lambda h: h.tensor_tensor(out=aid[:], in0=Aim[:], in1=dt_[:], op=ALU.mult), reads=[Aimb, dtb], writes=[aidb])
            EAr, EArb = tl("EAr", [128, 16, G]); EAi, EAib = tl("EAi", [128, 16, G])
            mag, magb = tl("mag", [128, 16, G]); ang, angb = tl("ang", [128, 16, G]); rr, rrb = tl("rr", [128, 16, G])
            sn, snb = tl("sn", [128, 16, G]); cs_, csb_ = tl("cs", [128, 16, G])
            for k in range(-7, 9):
                i = k + 7
                kb.op("act", lambda h: h.activation(out=mag[:, i, :], in_=ard[:], func=AF.Exp, scale=float(k)), reads=[ardb], writes=[magb])
                kb.op("dve", lambda h: h.tensor_scalar(out=rr[:, i, :], in0=aid[:], scalar1=float(k) / TWO_PI, scalar2=MAGIC, op0=ALU.mult, op1=ALU.add),
                      reads=[aidb], writes=[rrb])
                kb.op("dve", lambda h: h.tensor_scalar(out=rr[:, i, :], in0=rr[:, i, :], scalar1=-MAGIC, scalar2=-TWO_PI, op0=ALU.add, op1=ALU.mult),
                      reads=[rrb], writes=[rrb])
                kb.op("dve", lambda h: h.scalar_tensor_tensor(out=ang[:, i, :], in0=aid[:], scalar=float(k), in1=rr[:, i, :], op0=ALU.mult, op1=ALU.add),
                      reads=[aidb, rrb], writes=[angb])
            kb.op("dve", lambda h: h.tensor_scalar(out=ang[:], in0=ang[:], scalar1=math.pi, scalar2=-math.pi, op0=ALU.min, op1=ALU.max),
                  reads=[angb], writes=[angb])
            kb.op("act", lambda h: h.activation(out=sn[:], in_=ang[:], func=AF.Sin), reads=[angb], writes=[snb])
            kb.op("act", lambda h: h.activation(out=rr[:], in_=ang[:], func=AF.Abs), reads=[angb], writes=[rrb])
            kb.op("act", lambda h: h.activation(out=cs_[:], in_=rr[:], func=AF.Sin, scale=-1.0, bias=misc[:, 3:4]), reads=[rrb, miscb], writes=[csb_])
            kb.op("dve", lambda h: h.tensor_tensor(out=EAr[:], in0=mag[:], in1=cs_[:], op=ALU.mult), reads=[magb, csb_], writes=[EArb])
            kb.op("dve", lambda h: h.tensor_tensor(out=EAi[:], in0=mag[:], in1=sn[:], op=ALU.mult), reads=[magb, snb], writes=[EAib])
            mu_t, mu_b = g["mu"]
            kb.op("dve", lambda h: h.tensor_copy(mu_t[:, 0, :], EAr[:, 15, :]), reads=[EArb], writes=[mu_b])
            kb.op("dve", lambda h: h.tensor_copy(mu_t[:, 1, :], EAi[:, 15, :]), reads=[EAib], writes=[mu_b])
            mup_t, mup_b = g["mup"]
            pw1, pw1b = tl("pw1", [128, G]); pw2, pw2b = tl("pw2", [128, G])
            kb.op("dve", lambda h: h.tensor_copy(mup_t[:, 0, :, 0], EAr[:, 15, :]), reads=[EArb], writes=[mup_b])
            kb.op("dve", lambda h: h.tensor_copy(mup_t[:, 1, :, 0], EAi[:, 15, :]), reads=[EAib], writes=[mup_b])
            for k in range(1, 16):
                pr, pi_ = mup_t[:, 0, :, k - 1], mup_t[:, 1, :, k - 1]
                kb.op("dve", lambda h: h.tensor_tensor(out=pw1[:], in0=pr, in1=mu_t[:, 0, :], op=ALU.mult), reads=[mup_b, mu_b], writes=[pw1b])
                kb.op("dve", lambda h: h.tensor_tensor(out=pw2[:], in0=pi_, in1=mu_t[:, 1, :], op=ALU.mult), reads=[mup_b, mu_b], writes=[pw2b])
                kb.op("dve", lambda h: h.tensor_tensor(out=mup_t[:, 0, :, k], in0=pw1[:], in1=pw2[:], op=ALU.subtract), reads=[pw1b, pw2b], writes=[mup_b])
                kb.op("dve", lambda h: h.tensor_tensor(out=pw1[:], in0=pr, in1=mu_t[:, 1, :], op=ALU.mult), reads=[mup_b, mu_b], writes=[pw1b])
                kb.op("dve", lambda h: h.tensor_tensor(out=pw2[:], in0=pi_, in1=mu_t[:, 0, :], op=ALU.mult), reads=[mup_b, mu_b], writes=[pw2b])
                kb.op("dve", lambda h: h.tensor_tensor(out=mup_t[:, 1, :, k], in0=pw1[:], in1=pw2[:], op=ALU.add), reads=[pw1b, pw2b], writes=[mup_b])
            nr, nrb = tl("nr", [128, G]); den, denb = tl("den", [128, G]); tq, tqb = tl("tq", [128, G])
            fre, freb = tl("fre", [128, G]); fim, fimb = tl("fim", [128, G])
            kb.op("dve", lambda h: h.tensor_scalar_add(out=nr[:], in0=EAr[:, 8, :], scalar1=-1.0), reads=[EArb], writes=[nrb])
            kb.op("dve", lambda h: h.tensor_tensor(out=den[:], in0=Are[:], in1=Are[:], op=ALU.mult), reads=[Areb], writes=[denb])
            kb.op("dve", lambda h: h.tensor_tensor(out=tq[:], in0=Aim[:], in1=Aim[:], op=ALU.mult), reads=[Aimb], writes=[tqb])
            kb.op("dve", lambda h: h.tensor_tensor(out=den[:], in0=den[:], in1=tq[:], op=ALU.add), reads=[denb, tqb], writes=[denb])
            kb.op("dve", lambda h: h.reciprocal(den[:], den[:]), reads=[denb], writes=[denb])
            kb.op("dve", lambda h: h.tensor_tensor(out=fre[:], in0=nr[:], in1=Are[:], op=ALU.mult), reads=[nrb, Areb], writes=[freb])
            kb.op("dve", lambda h: h.tensor_tensor(out=tq[:], in0=EAi[:, 8, :], in1=Aim[:], op=ALU.mult), reads=[EAib, Aimb], writes=[tqb])
            kb.op("dve", lambda h: h.tensor_tensor(out=fre[:], in0=fre[:], in1=tq[:], op=ALU.add), reads=[freb, tqb], writes=[freb])
            kb.op("dve", lambda h: h.tensor_tensor(out=fre[:], in0=fre[:], in1=den[:], op=ALU.mult), reads=[freb, denb], writes=[freb])
            kb.op("dve", lambda h: h.tensor_tensor(out=fim[:], in0=EAi[:, 8, :], in1=Are[:], op=ALU.mult), reads=[EAib, Areb], writes=[fimb])
            kb.op("dve", lambda h: h.tensor_tensor(out=tq[:], in0=nr[:], in1=Aim[:], op=ALU.mult), reads=[nrb, Aimb], writes=[tqb])
            kb.op("dve", lambda h: h.tensor_tensor(out=fim[:], in0=fim[:], in1=tq[:], op=ALU.subtract), reads=[fimb, tqb], writes=[fimb])
            kb.op("dve", lambda h: h.tensor_tensor(out=fim[:], in0=fim[:], in1=den[:], op=ALU.mult), reads=[fimb, denb], writes=[fimb])
            Bbr, Bbrb = tl("Bbr", [128, G, 16]); Bbi, Bbib = tl("Bbi", [128, G, 16]); tb1, tb1b = tl("tb1", [128, G, 16])
            frb = fre[:].unsqueeze(2).broadcast_to([128, G, 16])
            fib = fim[:].unsqueeze(2).broadcast_to([128, G, 16])
            kb.op("dve", lambda h: h.tensor_tensor(out=Bbr[:], in0=Bre[:], in1=frb, op=ALU.mult), reads=[Breb, freb], writes=[Bbrb])
            kb.op("dve", lambda h: h.tensor_tensor(out=tb1[:], in0=Bim[:], in1=fib, op=ALU.mult), reads=[Bimb, fimb], writes=[tb1b])
            kb.op("dve", lambda h: h.tensor_tensor(out=Bbr[:], in0=Bbr[:], in1=tb1[:], op=ALU.subtract), reads=[Bbrb, tb1b], writes=[Bbrb])
            kb.op("dve", lambda h: h.tensor_tensor(out=Bbi[:], in0=Bim[:], in1=frb, op=ALU.mult), reads=[Bimb, freb], writes=[Bbib])
            kb.op("dve", lambda h: h.tensor_tensor(out=tb1[:], in0=Bre[:], in1=fib, op=ALU.mult), reads=[Breb, fimb], writes=[tb1b])
            kb.op("dve", lambda h: h.tensor_tensor(out=Bbi[:], in0=Bbi[:], in1=tb1[:], op=ALU.add), reads=[Bbib, tb1b], writes=[Bbib])
            fam = {}
            for nm in ("BTr", "BTi", "COr", "COi", "P1r", "P1i", "P2rF", "P2iF", "P2rB", "P2iB"):
                fam[nm] = tl(nm, [128, GBP, 128])
            for nm in ("P2rF", "P2iF", "P2rB", "P2iB"):
                kb.op("dve", lambda h: h.memset(fam[nm][0][:], 0.0), writes=[fam[nm][1]])
            tmp = [tl("tmp%d" % i, [128, GBP, 128]) for i in range(2)]
            mgs, mgsb = tl("mgs", [128, GBP, 128], BF16)
            bins, binsb = tl("bins", [128, GBP, 2, 128], BF16)
            cos_, cosb_ = tl("cos", [128, GBP, 4, 128], BF16)
            kb.op("dve", lambda h: h.memset(cos_[:], 0.0), writes=[cosb_])
            m1, m1b = tl("m1", [128, 128]); m2, m2b = tl("m2", [128, 128])

            def cprod(part, g0, k0, ks, Yr, Yrb, Yi, Yib, outr, outi, neg_im):
                en = "dve"
                o_r, o_rb = fam[outr]
                o_i, o_ib = fam[outi]
                (ta, tab), (tc, tcb) = tmp
                dims_e = [[1, GBP], [ks * G, 8], [0, 16]]
                Xr = self.sview(EAr, part, 64, (k0 + 7) * G + g0, dims_e)
                Xi = self.sview(EAi, part, 64, (k0 + 7) * G + g0, dims_e)
                dims_y = [[16, GBP], [0, 8], [1, 16]]
                yr = self.sview(Yr, part, 64, g0 * 16, dims_y)
                yi = self.sview(Yi, part, 64, g0 * 16, dims_y)
                d4 = [[128, GBP], [16, 8], [1, 16]]
                v = lambda t: self.sview(t, part, 64, 0, d4)
                kb.op(en, lambda h: h.tensor_tensor(out=v(ta), in0=Xr, in1=yr, op=ALU.mult), reads=[EArb, Yrb], writes=[tab])
                kb.op(en, lambda h: h.tensor_tensor(out=v(tc), in0=Xi, in1=yi, op=ALU.mult), reads=[EAib, Yib], writes=[tcb])
                kb.op(en, lambda h: h.tensor_tensor(out=v(o_r), in0=v(ta), in1=v(tc), op=ALU.subtract), reads=[tab, tcb], writes=[o_rb])
                kb.op(en, lambda h: h.tensor_tensor(out=v(ta), in0=Xr, in1=yi, op=ALU.mult), reads=[EArb, Yib], writes=[tab])
                kb.op(en, lambda h: h.tensor_tensor(out=v(tc), in0=Xi, in1=yr, op=ALU.mult), reads=[EAib, Yrb], writes=[tcb])
                kb.op(en, lambda h: h.tensor_tensor(out=v(o_i), in0=v(ta), in1=v(tc), op=ALU.add), reads=[tab, tcb], writes=[o_ib])
                if neg_im:
                    kb.op(en, lambda h: h.tensor_scalar_mul(out=v(o_i), in0=v(o_i), scalar1=-1.0), reads=[o_ib], writes=[o_ib])

            for bi in range(G // GBP):
                g0 = bi * GBP
                for part, d in ((0, 0), (64, 1)):
                    kB = (7, -1) if d == 0 else (0, 1)
                    kC = (1, 1) if d == 0 else (8, -1)
                    k1 = (0, 1) if d == 0 else (0, -1)
                    k2 = (0, -1) if d == 0 else (0, 1)
                    sfx = "F" if d == 0 else "B"
                    cprod(part, g0, kB[0], kB[1], Bbr, Bbrb, Bbi, Bbib, "BTr", "BTi", False)
                    cprod(part, g0, kC[0], kC[1], Cre, Creb, Cim, Cimb, "COr", "COi", True)
                    cprod(part, g0, k1[0], k1[1], Cre, Creb, Cim, Cimb, "P1r", "P1i", False)
                    cprod(part, g0, k2[0], k2[1], Bbr, Bbrb, Bbi, Bbib, "P2r" + sfx, "P2i" + sfx, True)
                BTr, BTrb = fam["BTr"]; BTi, BTib = fam["BTi"]
                P1r, P1rb = fam["P1r"]; P1i, P1ib = fam["P1i"]
                COr, COrb = fam["COr"]; COi, COib = fam["COi"]
                kb.op("act", lambda h: h.copy(cos_[0:64, :, 0, :], COr[0:64]), reads=[COrb], writes=[cosb_])
                kb.op("act", lambda h: h.copy(cos_[0:64, :, 1, :], COi[0:64]), reads=[COib], writes=[cosb_])
                kb.op("act", lambda h: h.copy(cos_[64:128, :, 2, :], COr[64:128]), reads=[COrb], writes=[cosb_])
                kb.op("act", lambda h: h.copy(cos_[64:128, :, 3, :], COi[64:128]), reads=[COib], writes=[cosb_])
                kb.dma(self.COUT[g0:g0 + GBP].rearrange("g p r n -> p g r n"), cos_[:], reads=[cosb_], writes=[self.dbuf("COUT")])
                for gi in range(GBP):
                    pa, pb = 2 + (gi % 2), 4 + (gi % 2)
                    kb.group("pe", [lambda h: h.transpose(self.ps[pa][:, 0:128], BTr[:, gi, :], identf[:]),
                                    lambda h: h.transpose(self.ps[pa][:, 128:256], BTi[:, gi, :], identf[:])],
                             reads=[BTrb, BTib, identfb], writes=[self.psb[pa]])
                    kb.op("act", lambda h: h.copy(bins[:, gi, :, :], self.ps[pa][:, 0:256].rearrange("p (r n) -> p r n", r=2)),
                          reads=[self.psb[pa]], writes=[binsb])
                    fns = []
                    rds = [P1rb, P1ib]
                    for d, sfx in ((0, "F"), (1, "B")):
                        p2r, p2rb = fam["P2r" + sfx]
                        p2i, p2ib = fam["P2i" + sfx]
                        rds += [p2rb, p2ib]
                        fns.append(lambda h, d=d, p2r=p2r: h.matmul(self.ps[pb][:, 128 * d:128 * d + 128], p2r[:, gi, :], P1r[:, gi, :], start=True, stop=False))
                        fns.append(lambda h, d=d, p2i=p2i: h.matmul(self.ps[pb][:, 128 * d:128 * d + 128], p2i[:, gi, :], P1i[:, gi, :], start=False, stop=True))
                    kb.group("pe", fns, reads=rds, writes=[self.psb[pb]])
                    kb.op("dve", lambda h: h.tensor_tensor(out=m1[:], in0=self.ps[pb][:, 0:128], in1=mmk[:, 0, :], op=ALU.mult),
                          reads=[self.psb[pb], mmkb], writes=[m1b])
                    kb.op("dve", lambda h: h.tensor_tensor(out=m2[:], in0=self.ps[pb][:, 128:256], in1=mmk[:, 1, :], op=ALU.mult),
                          reads=[self.psb[pb], mmkb], writes=[m2b])
                    kb.op("dve", lambda h: h.tensor_tensor(out=m1[:], in0=m1[:], in1=m2[:], op=ALU.add), reads=[m1b, m2b], writes=[m1b])
                    kb.op("dve", lambda h: h.scalar_tensor_tensor(out=mgs[:, gi, :], in0=identf[:], scalar=Dsk[:, g0 + gi:g0 + gi + 1], in1=m1[:],
                                                                   op0=ALU.mult, op1=ALU.add), reads=[identfb, Dskb, m1b], writes=[mgsb])
                kb.dma(self.BIN[g0:g0 + GBP].rearrange("g p r n -> p g r n"), bins[:], reads=[binsb], writes=[self.dbuf("BIN")])
                kb.dma(self.MG[g0:g0 + GBP].rearrange("g p n -> p g n"), mgs[:], reads=[mgsb], writes=[self.dbuf("MG")])
        kb.barrier()

    def phase_ssm(self):
        kb = self.kb
        Tn, NCH = self.T, self.NCH
        g = self.gtiles
        mu, mub = g["mu"]
        fl, flb = g["flag"]
        NX = NCH + 1
        with contextlib.ExitStack() as es:
            def tl(name, shape, dt=F32):
                return self.sb(es, "s_" + name, shape, dt), Buf(name)
            mg, mgb = tl("mg", [128, GB, 128], BF16)
            bn, bnb = tl("bin", [128, GB, 2, 128], BF16)
            co, cob = tl("cout", [128, GB, 4, 128], BF16)
            U8 = [tl("u8_%d" % s, [128, GB, NCH], BF16) for s in range(NSEG)]
            XS, XSb = tl("XS", [128, 2, GB, NX])
            XP = {k: tl("XP" + k, [128, 2, GB, NX], BF16) for k in "ABC"}
            MUA, MUAb = tl("MUA", [128, 2, GB])
            MUB, MUBb = tl("MUB", [128, 2, GB])
            T1, T1b = tl("T1", [128, 2, GB])
            T2, T2b = tl("T2", [128, 2, GB])
            FIN, FINb = tl("FIN", [128, 2, GB])
            LB = 16
            NBK = NCH // LB
            W1, W1b = tl("W1", [128, 2, GB, NBK])
            W2, W2b = tl("W2", [128, 2, GB, NBK])
            W3, W3b = tl("W3", [128, GB, NBK, LB - 1])
            MLA, MLAb = tl("MLA", [128, 2, GB])
            MLB, MLBb = tl("MLB", [128, 2, GB])
            mup, mupb = g["mup"]
            y8 = [tl("y8_%d" % i, [128, NCH], BF16) for i in range(2)]
            sq = [tl("sq%d" % i, [128, NCH]) for i in range(2)]
            uu = [tl("uu%d" % i, [128, NCH]) for i in range(2)]
            cnt = {"ps": 0, "ev": 0}

            def xcol(t, k):
                return t[:, :, :, k]

            def run_pass(sf, sbw, init_from, key, g0):
                xp, xpb = XP[key]
                for gi in range(GB):
                    pi = cnt["ps"] % 3
                    cnt["ps"] += 1
                    uf, ufb = U8[sf]
                    ub, ubb = U8[sbw]
                    fns = []
                    for r in range(2):
                        fns.append(lambda h, r=r: h.matmul(self.ps[pi][0:64, r * NCH:(r + 1) * NCH], bn[:, gi, r, 0:64], uf[:, gi, :], start=True, stop=True))
                        fns.append(lambda h, r=r: h.matmul(self.ps[pi][64:128, r * NCH:(r + 1) * NCH], bn[:, gi, r, 64:128], rev_last(ub[:, gi, :], NCH), start=True, stop=True))
                    kb.group("pe", fns, reads=[bnb, ufb, ubb], writes=[self.psb[pi]])
                    kb.op("act", lambda h: h.copy(XS[:, :, gi, 1:NX], self.ps[pi][:, 0:2 * NCH].rearrange("p (r n) -> p r n", r=2)),
                          reads=[self.psb[pi]], writes=[XSb])
                if init_from is None:
                    kb.op("dve", lambda h: h.memset(xcol(XS, 0), 0.0), writes=[XSb])
                else:
                    kb.op("dve", lambda h: h.tensor_scalar_mul(out=xcol(XS, 0), in0=FIN[:], scalar1=fl[:, 0:1]), reads=[FINb, flb], writes=[XSb])
                base = XS[:]
                pst = base.ap[0][0]
                RS = GB * NX

                def xv(off, dims):
                    return AP(base.tensor, base.offset + off, [[pst, 128]] + [list(d) for d in dims])
                mua_b = AP(MUA[:].tensor, MUA[:].offset, [[MUA[:].ap[0][0], 128], [GB, 2], [1, GB], [0, NBK]])
                mub_b = AP(MUB[:].tensor, MUB[:].offset, [[MUB[:].ap[0][0], 128], [GB, 2], [1, GB], [0, NBK]])
                for j in range(1, LB):
                    prev = xv(j, [[RS, 2], [NX, GB], [LB, NBK]])
                    prev_sw = xv(RS + j, [[-RS, 2], [NX, GB], [LB, NBK]])
                    cur = xv(1 + j, [[RS, 2], [NX, GB], [LB, NBK]])
                    kb.op("dve", lambda h: h.tensor_tensor(out=W1[:], in0=mua_b, in1=prev, op=ALU.mult), reads=[MUAb, XSb], writes=[W1b])
                    kb.op("dve", lambda h: h.tensor_tensor(out=W2[:], in0=mub_b, in1=prev_sw, op=ALU.mult), reads=[MUBb, XSb], writes=[W2b])
                    kb.op("dve", lambda h: h.tensor_tensor(out=W1[:], in0=W1[:], in1=W2[:], op=ALU.add), reads=[W1b, W2b], writes=[W1b])
                    kb.op("dve", lambda h: h.tensor_tensor(out=cur, in0=cur, in1=W1[:], op=ALU.add), reads=[W1b, XSb], writes=[XSb])
                for b in range(NBK):
                    pc = xv(b * LB, [[RS, 2], [NX, GB]])
                    pc_sw = xv(RS + b * LB, [[-RS, 2], [NX, GB]])
                    cur = xv((b + 1) * LB, [[RS, 2], [NX, GB]])
                    kb.op("dve", lambda h: h.tensor_tensor(out=T1[:], in0=MLA[:], in1=pc, op=ALU.mult), reads=[MLAb, XSb], writes=[T1b])
                    kb.op("dve", lambda h: h.tensor_tensor(out=T2[:], in0=MLB[:], in1=pc_sw, op=ALU.mult), reads=[MLBb, XSb], writes=[T2b])
                    kb.op("dve", lambda h: h.tensor_tensor(out=T1[:], in0=T1[:], in1=T2[:], op=ALU.add), reads=[T1b, T2b], writes=[T1b])
                    kb.op("dve", lambda h: h.tensor_tensor(out=cur, in0=cur, in1=T1[:], op=ALU.add), reads=[T1b, XSb], writes=[XSb])
                mp = mup[:]
                mps = mp.ap[0][0]

                def pv(r):
                    return AP(mp.tensor, mp.offset + (r * G + g0) * 16, [[mps, 128], [16, GB], [0, NBK], [1, LB - 1]])
                Cr = xv(0, [[NX, GB], [LB, NBK], [0, LB - 1]])
                Ci = xv(RS, [[NX, GB], [LB, NBK], [0, LB - 1]])
                Xr = xv(1, [[NX, GB], [LB, NBK], [1, LB - 1]])
                Xi = xv(RS + 1, [[NX, GB], [LB, NBK], [1, LB - 1]])
                for (pa, ca, tgt, op) in ((pv(0), Cr, Xr, ALU.add), (pv(1), Ci, Xr, ALU.subtract), (pv(0), Ci, Xi, ALU.add), (pv(1), Cr, Xi, ALU.add)):
                    kb.op("dve", lambda h: h.tensor_tensor(out=W3[:], in0=pa, in1=ca, op=ALU.mult), reads=[mupb, XSb], writes=[W3b])
                    kb.op("dve", lambda h: h.tensor_tensor(out=tgt, in0=tgt, in1=W3[:], op=op), reads=[W3b, XSb], writes=[XSb])
                kb.op("dve", lambda h: h.tensor_copy(FIN[:], xcol(XS, NCH)), reads=[XSb], writes=[FINb])
                kb.op("act", lambda h: h.copy(xp[:], XS[:]), reads=[XSb], writes=[xpb])

            def emit_y(s, kf, kbk, g0):
                xf, xfb = XP[kf]
                xb, xbb = XP[kbk]
                us, usb = U8[s]
                for gi in range(GB):
                    pi = cnt["ps"] % 3
                    cnt["ps"] += 1
                    e = cnt["ev"] % 2
                    cnt["ev"] += 1
                    o = self.ps[pi][:, 0:NCH]
                    fns = [lambda h: h.matmul(o, mg[:, gi, :], us[:, gi, :], start=True, stop=False),
                           lambda h: h.matmul(o, co[:, gi, 0, :], xf[:, 0, gi, 0:NCH], start=False, stop=False),
                           lambda h: h.matmul(o, co[:, gi, 1, :], xf[:, 1, gi, 0:NCH], start=False, stop=False),
                           lambda h: h.matmul(o, co[:, gi, 2, :], rev_last(xb[:, 0, gi, 0:NCH], NCH), start=False, stop=False),
                           lambda h: h.matmul(o, co[:, gi, 3, :], rev_last(xb[:, 1, gi, 0:NCH], NCH), start=False, stop=True)]
                    kb.group("pe", fns, reads=[mgb, cob, usb, xfb, xbb], writes=[self.psb[pi]])
                    (sq_t, sq_b), (uu_t, uu_b), (y_t, y_b) = sq[e], uu[e], y8[e]
                    kb.op("act", lambda h: h.activation(out=sq_t[:], in_=o, func=AF.Square), reads=[self.psb[pi]], writes=[sq_b])
                    kb.op("dve", lambda h: h.tensor_scalar(out=sq_t[:], in0=sq_t[:], scalar1=0.044715, scalar2=1.0, op0=ALU.mult, op1=ALU.add),
                          reads=[sq_b], writes=[sq_b])
                    kb.op("dve", lambda h: h.tensor_tensor(out=uu_t[:], in0=sq_t[:], in1=o, op=ALU.mult), reads=[sq_b, self.psb[pi]], writes=[uu_b])
                    kb.op("act", lambda h: h.activation(out=uu_t[:], in_=uu_t[:], func=AF.Sigmoid, scale=1.5957691216057308), reads=[uu_b], writes=[uu_b])
                    kb.op("dve", lambda h: h.tensor_tensor(out=y_t[:], in0=uu_t[:], in1=o, op=ALU.mult), reads=[uu_b, self.psb[pi]], writes=[y_b])
                    dst = dap(self.YT, s * 1024 * Tn + (g0 + gi) * 16 * Tn, [[NCH, 8], [Tn, 16], [1, NCH]])
                    kb.dma(dst, y_t[:], reads=[y_b], writes=[self.dbuf(("YT", s))])

            for bi in range(G // GB):
                g0 = bi * GB
                kb.dma(mg[:], self.MG[g0:g0 + GB].rearrange("g p n -> p g n"), reads=[self.dbuf("MG")], writes=[mgb])
                kb.dma(bn[:], self.BIN[g0:g0 + GB].rearrange("g p r n -> p g r n"), reads=[self.dbuf("BIN")], writes=[bnb])
                kb.dma(co[:], self.COUT[g0:g0 + GB].rearrange("g p r n -> p g r n"), reads=[self.dbuf("COUT")], writes=[cob])
                for s in range(NSEG):
                    ut, utb = U8[s]
                    for gi in range(GB):
                        src = dap(self.UT, s * 1024 * Tn + (g0 + gi) * 16 * Tn, [[NCH, 8], [Tn, 16], [1, NCH]])
                        kb.dma(ut[:, gi, :], src, reads=[self.dbuf(("E", s, (g0 + gi) // 8))], writes=[utb])
                kb.op("pool", lambda h: h.tensor_copy(MUA[:, 0, :], mu[:, 0, g0:g0 + GB]), reads=[mub], writes=[MUAb])
                kb.op("pool", lambda h: h.tensor_copy(MUA[:, 1, :], mu[:, 0, g0:g0 + GB]), reads=[mub], writes=[MUAb])
                kb.op("pool", lambda h: h.tensor_scalar_mul(out=MUB[:, 0, :], in0=mu[:, 1, g0:g0 + GB], scalar1=-1.0), reads=[mub], writes=[MUBb])
                kb.op("pool", lambda h: h.tensor_copy(MUB[:, 1, :], mu[:, 1, g0:g0 + GB]), reads=[mub], writes=[MUBb])
                kb.op("pool", lambda h: h.tensor_copy(MLA[:, 0, :], mup[:, 0, g0:g0 + GB, LB - 1]), reads=[mupb], writes=[MLAb])
                kb.op("pool", lambda h: h.tensor_copy(MLA[:, 1, :], mup[:, 0, g0:g0 + GB, LB - 1]), reads=[mupb], writes=[MLAb])
                kb.op("pool", lambda h: h.tensor_scalar_mul(out=MLB[:, 0, :], in0=mup[:, 1, g0:g0 + GB, LB - 1], scalar1=-1.0), reads=[mupb], writes=[MLBb])
                kb.op("pool", lambda h: h.tensor_copy(MLB[:, 1, :], mup[:, 1, g0:g0 + GB, LB - 1]), reads=[mupb], writes=[MLBb])
                run_pass(0, 1, None, "A", g0)
                run_pass(1, 0, "A", "B", g0)
                emit_y(0, "A", "B", g0)
                emit_y(1, "B", "A", g0)
                run_pass(2, 2, None, "C", g0)
                emit_y(2, "C", "C", g0)
        kb.barrier()

    def phase_glu(self):
        kb = self.kb
        Tn, NCH = self.T, self.NCH
        g = self.gtiles
        bglu, bglub = g["bglu"]
        with contextlib.ExitStack() as es:
            wg = self.sb(es, "g_w", [128, 8, 1024], BF16)
            wgb = Buf()
            wst = [self.sb(es, "g_wst%d" % i, [128, 1024], F32) for i in range(2)]
            wstb = [Buf(), Buf()]
            for kc in range(8):
                kb.dma(wst[kc % 2][:], self.w_glu[kc * 128:(kc + 1) * 128, :], writes=[wstb[kc % 2]])
                kb.op("pool", lambda h: h.tensor_copy(wg[:, kc, :], wst[kc % 2][:]), reads=[wstb[kc % 2]], writes=[wgb])
            yt = [self.sb(es, "g_yt%d" % i, [128, 8, TT], BF16) for i in range(2)]
            at = [self.sb(es, "g_at%d" % i, [128, 8, TT], BF16) for i in range(2)]
            ytb = [Buf(), Buf()]
            atb = [Buf(), Buf()]
            zst = [self.sb(es, "g_z%d" % i, [128, Tn], BF16) for i in range(8)]
            zstb = [Buf() for _ in range(8)]
            sg = [self.sb(es, "g_sg%d" % i, [128, TT], F32) for i in range(2)]
            sgb = [Buf(), Buf()]
            JT = TT // NCH
            n = 0
            it = 0
            for s in range(NSEG):
                for ct in range(Tn // TT):
                    sl = it % 2
                    it += 1
                    kb.dma(yt[sl][:], self.YT[s, :, ct * TT:(ct + 1) * TT].rearrange("(k p) n -> p k n", p=128), writes=[ytb[sl]])
                    kb.dma(at[sl][:], self.AG[s, :, ct * TT:(ct + 1) * TT].rearrange("(k p) n -> p k n", p=128), writes=[atb[sl]])
                    for fo in range(8):
                        pi = n % 3
                        e = n % 2
                        n += 1
                        fns = [(lambda h, kc=kc: h.matmul(self.ps[pi][:, 0:TT], wg[:, kc, fo * 128:(fo + 1) * 128], yt[sl][:, kc, :],
                                                          start=(kc == 0), stop=(kc == 7))) for kc in range(8)]
                        kb.group("pe", fns, reads=[wgb, ytb[sl]], writes=[self.psb[pi]])
                        kb.op("act", lambda h: h.activation(out=sg[e][:], in_=self.ps[pi][:, 0:TT], func=AF.Sigmoid, bias=bglu[:, fo:fo + 1]),
                              reads=[self.psb[pi], bglub], writes=[sgb[e]])
                        kb.op("dve", lambda h: h.tensor_tensor(out=sg[e][:], in0=sg[e][:], in1=yt[sl][:, fo, :], op=ALU.mult),
                              reads=[sgb[e], ytb[sl]], writes=[sgb[e]])
                        outv = zst[fo][:].rearrange("p (m j) -> p j m", j=8)[:, ct * JT:(ct + 1) * JT, :]
                        kb.op("dve", lambda h: h.tensor_tensor(out=outv, in0=sg[e][:].rearrange("p (j m) -> p j m", j=JT),
                                                               in1=at[sl][:, fo, :].rearrange("p (j m) -> p j m", j=JT), op=ALU.mult),
                              reads=[sgb[e], atb[sl]], writes=[zstb[fo]])
                for fo in range(8):
                    kb.dma(self.ZT[s, fo * 128:(fo + 1) * 128, :], zst[fo][:], reads=[zstb[fo]])
        kb.barrier()

    def attn_setup(self, es, pfx):
        st = {
            "PT": [self.sb(es, pfx + "PT%d" % i, [128, TT], BF16) for i in range(5)],
            "PTb": [Buf() for _ in range(5)],
            "RL": [self.sb(es, pfx + "RL%d" % i, [128, TT], F32) for i in range(2)],
            "RLb": [Buf(), Buf()],
            "AC": [self.sb(es, pfx + "AC%d" % i, [128, TT], BF16) for i in range(2)],
            "ACb": [Buf(), Buf()],
            "acn": 0,
            "n": 0, "u": 0,
        }
        return st

    def attn_unit(self, st, q_ap, qb, blocks, scale, gate_ap, gateb, out_ap, outb, after=None):
        kb = self.kb
        ones, onesb = self.gtiles["ones"]
        u = st["u"] % 2
        st["u"] += 1
        pO, pL = 4 + u, 6 + u
        nb = len(blocks)
        LOOK = 3
        slots = {}

        def emit_s(i):
            k_ap, kbuf, v_ap, vbuf, b_ap, bbuf, m_ap, mbuf = blocks[i]
            pS = st["n"] % 4
            sl = st["n"] % 5
            st["n"] += 1
            slots[i] = sl
            PT, PTb = st["PT"][sl], st["PTb"][sl]
            kb.group("pe", [lambda h: h.matmul(self.ps[pS][:, 0:TT], k_ap, q_ap, start=True, stop=True)], reads=[kbuf, qb], writes=[self.psb[pS]])
            kb.op("act", lambda h: h.activation(out=PT[:], in_=self.ps[pS][:, 0:TT], func=AF.Exp, scale=scale, bias=b_ap),
                  reads=[self.psb[pS], bbuf], writes=[PTb])
            if m_ap is not None:
                en = "pool" if (i % 3 == 2) else "dve"
                kb.op(en, lambda h: h.tensor_tensor(out=PT[:], in0=PT[:], in1=m_ap, op=ALU.mult), reads=[PTb, mbuf], writes=[PTb])

        pair_l = all(b[6] is None for b in blocks)
        pendl = []

        def flush_l():
            while pendl:
                pendl.pop(0)()

        def emit_pv(i):
            k_ap, kbuf, v_ap, vbuf, b_ap, bbuf, m_ap, mbuf = blocks[i]
            sl = slots[i]
            PT, PTb = st["PT"][sl], st["PTb"][sl]
            if not pair_l:
                kb.group("pe", [lambda h: h.matmul(self.ps[pO][:, 0:TT], v_ap, PT[:], start=(i == 0), stop=(i == nb - 1)),
                                lambda h: h.matmul(self.ps[pL][:, 0:TT], ones[:], PT[:], start=(i == 0), stop=(i == nb - 1))],
                         reads=[vbuf, PTb, onesb], writes=[self.psb[pO], self.psb[pL]])
                return
            flush_l()
            kb.group("pe", [lambda h: h.matmul(self.ps[pO][:, 0:TT], v_ap, PT[:], start=(i == 0), stop=(i == nb - 1))],
                     reads=[vbuf, PTb], writes=[self.psb[pO]])
            if i % 2 == 1:
                a = st["acn"] % 2
                st["acn"] += 1
                AC, ACb = st["AC"][a], st["ACb"][a]
                P0, P0b = st["PT"][slots[i - 1]], st["PTb"][slots[i - 1]]
                kb.op("dve", lambda h: h.tensor_tensor(out=AC[:], in0=P0[:], in1=PT[:], op=ALU.add), reads=[P0b, PTb], writes=[ACb])
                pendl.append(lambda: kb.group("pe", [lambda h: h.matmul(self.ps[pL][:, 0:TT], ones[:], AC[:], start=(i == 1), stop=(i == nb - 1))],
                                              reads=[ACb, onesb], writes=[self.psb[pL]]))
            elif i == nb - 1:
                pendl.append(lambda: kb.group("pe", [lambda h: h.matmul(self.ps[pL][:, 0:TT], ones[:], PT[:], start=(i == 0), stop=True)],
                                              reads=[PTb, onesb], writes=[self.psb[pL]]))

        for i in range(nb + LOOK):
            if i < nb:
                emit_s(i)
            if i >= LOOK:
                emit_pv(i - LOOK)
            fin_prev = st.get("fin")
            if fin_prev is not None:
                if nb >= 8:
                    if 2 <= i < 6:
                        fin_prev(i - 2)
                    if i == 6:
                        fin_prev(4)
                        st.pop("fin")
                elif i == nb - 1:
                    for c in range(5):
                        fin_prev(c)
                    st.pop("fin")
        flush_l()
        RL, RLb = st["RL"][u], st["RLb"][u]

        def fin(c):
            if c < 4:
                cs_ = slice(c * 128, (c + 1) * 128)
                kb.op("dve", lambda h: h.reciprocal(RL[:, cs_], self.ps[pL][:, cs_]), reads=[self.psb[pL]], writes=[RLb])
                kb.op("dve", lambda h: h.tensor_tensor(out=RL[:, cs_], in0=RL[:, cs_], in1=self.ps[pO][:, cs_], op=ALU.mult), reads=[RLb, self.psb[pO]], writes=[RLb])
            else:
                kb.op("dve", lambda h: h.tensor_tensor(out=out_ap, in0=RL[:], in1=gate_ap, op=ALU.mult), reads=[RLb, gateb], writes=[outb])
                if after is not None:
                    after()
        st["fin"] = fin

    def run_units(self, units):
        n = len(units)
        if n:
            units[0][0]()
        for i in range(n):
            if i + 1 < n:
                units[i + 1][0]()
            units[i][1]()
            units[i][2]()

    def phase_gqa(self):
        kb = self.kb
        Tn, NB, NTT = self.T, self.NB, self.NTT
        g = self.gtiles
        misc, miscb = g["misc"]
        scale = 128.0 ** -0.5
        with contextlib.ExitStack() as es:
            st = self.attn_setup(es, "a_")
            kt = [self.sb(es, "a_kt%d" % i, [128, 2, Tn], BF16) for i in range(2)]
            vt = [self.sb(es, "a_vt%d" % i, [128, 2, NB, 128], BF16) for i in range(2)]
            ktb, vtb = [Buf(), Buf()], [Buf(), Buf()]
            qt_ = [self.sb(es, "a_q%d" % i, [128, TT], BF16) for i in range(3)]
            gt_ = [self.sb(es, "a_g%d" % i, [128, TT], BF16) for i in range(3)]
            zt_ = [self.sb(es, "a_z%d" % i, [128, TT], BF16) for i in range(3)]
            qtb, gtb, ztb = [Buf() for _ in range(3)], [Buf() for _ in range(3)], [Buf() for _ in range(3)]
            units = []
            it = 0
            hn = 0
            for segs in ((0, 1), (2,)):
                for kh in range(2):
                    hs = hn % 2
                    hn += 1
                    first = True
                    for li, s in enumerate(segs):
                        for h in range(4 * kh, 4 * kh + 4):
                            for tq in range(NTT):
                                sl3 = it % 3
                                sl2 = it % 3
                                it += 1

                                def load(first=first, hs=hs, segs=segs, kh=kh, s=s, h=h, tq=tq, sl3=sl3):
                                    if first:
                                        for lj, s2 in enumerate(segs):
                                            kb.dma(kt[hs][:, lj, :], self.KT[s2, kh], writes=[ktb[hs]])
                                            kb.dma(vt[hs][:, lj, :, :], dap(self.VV, s2 * Tn * 256 + kh * 128, [[256, 128], [128 * 256, NB], [1, 128]]), writes=[vtb[hs]])
                                    kb.dma(qt_[sl3][:], self.QT[s, h, :, tq * TT:(tq + 1) * TT], writes=[qtb[sl3]])
                                    kb.dma(gt_[sl3][:], self.BG[s, h * 128:(h + 1) * 128, tq * TT:(tq + 1) * TT], writes=[gtb[sl3]])

                                def compute(hs=hs, segs=segs, li=li, sl3=sl3, sl2=sl2, s=s, h=h, tq=tq):
                                    blocks = []
                                    for lj in range(len(segs)):
                                        b_ap = misc[:, 6:7] if lj == li else misc[:, 7:8]
                                        for b in range(NB):
                                            blocks.append((kt[hs][:, lj, b * 128:(b + 1) * 128], ktb[hs], vt[hs][:, lj, b, :], vtb[hs], b_ap, miscb, None, None))
                                    self.attn_unit(st, qt_[sl3][:], qtb[sl3], blocks, scale, gt_[sl3][:], gtb[sl3], zt_[sl2][:], ztb[sl2],
                                                   after=lambda: kb.dma(self.ZT[s, 1024 + h * 128:1024 + (h + 1) * 128, tq * TT:(tq + 1) * TT], zt_[sl2][:], reads=[ztb[sl2]]))
                                units.append((load, compute, lambda: None))
                                first = False
            self.run_units(units)
            for c in range(5):
                st["fin"](c)
            st.pop("fin")
        kb.barrier()

    def phase_dil(self):
        kb = self.kb
        Tn, NB, NTT = self.T, self.NB, self.NTT
        g = self.gtiles
        misc, miscb = g["misc"]
        fl, flb = g["flag"]
        scale = 128.0 ** -0.5
        with contextlib.ExitStack() as es:
            st = self.attn_setup(es, "d_")
            dm = self.sb(es, "d_mask", [128, 20, TT], BF16)
            dmb = Buf()
            kb.dma(dm[:], self.c_dmask.rearrange("i p n -> p i n"), writes=[dmb])
            kt = [self.sb(es, "d_kt%d" % i, [128, 2, Tn], BF16) for i in range(2)]
            vt = [self.sb(es, "d_vt%d" % i, [128, 2, NB, 128], BF16) for i in range(2)]
            ktb, vtb = [Buf(), Buf()], [Buf(), Buf()]
            qt_ = [self.sb(es, "d_q%d" % i, [128, TT], BF16) for i in range(3)]
            gt_ = [self.sb(es, "d_g%d" % i, [128, TT], BF16) for i in range(3)]
            zt_ = [self.sb(es, "d_z%d" % i, [128, TT], BF16) for i in range(3)]
            qtb, gtb, ztb = [Buf() for _ in range(3)], [Buf() for _ in range(3)], [Buf() for _ in range(3)]
            units = []
            it = 0
            hn = 0
            for segs in ((0, 1), (2,)):
                span = len(segs) * Tn
                for n in range(16):
                    hs = hn % 2
                    hn += 1
                    first = True
                    for li, s in enumerate(segs):
                        for tq in range(NTT):
                            sl3 = it % 3
                            sl2 = it % 3
                            it += 1

                            def load(first=first, hs=hs, segs=segs, n=n, s=s, tq=tq, sl3=sl3):
                                if first:
                                    for lj, s2 in enumerate(segs):
                                        kb.dma(kt[hs][:, lj, :], self.K2T[s2, n], writes=[ktb[hs]])
                                        kb.dma(vt[hs][:, lj, :, :], dap(self.V2, s2 * Tn * D + n * 128, [[D, 128], [128 * D, NB], [1, 128]]), writes=[vtb[hs]])
                                kb.dma(qt_[sl3][:], self.Q2T[s, n, :, tq * TT:(tq + 1) * TT], writes=[qtb[sl3]])
                                kb.dma(gt_[sl3][:], self.G2T[s, n * 128:(n + 1) * 128, tq * TT:(tq + 1) * TT], writes=[gtb[sl3]])

                            def compute(hs=hs, li=li, tq=tq, span=span, sl3=sl3, sl2=sl2, s=s, n=n):
                                Q0 = li * Tn + tq * TT
                                blocks = []
                                for mi in range(20):
                                    ks = Q0 - 1024 + mi * 128
                                    if ks < 0 or ks >= span:
                                        continue
                                    lj, b = ks // Tn, (ks % Tn) // 128
                                    b_ap = misc[:, 2:3] if lj == li else fl[:, 1:2]
                                    bb = miscb if lj == li else flb
                                    blocks.append((kt[hs][:, lj, b * 128:(b + 1) * 128], ktb[hs], vt[hs][:, lj, b, :], vtb[hs], b_ap, bb, dm[:, mi, :], dmb))
                                self.attn_unit(st, qt_[sl3][:], qtb[sl3], blocks, scale, gt_[sl3][:], gtb[sl3], zt_[sl2][:], ztb[sl2],
                                               after=lambda: kb.dma(self.ZT[s, n * 128:(n + 1) * 128, tq * TT:(tq + 1) * TT], zt_[sl2][:], reads=[ztb[sl2]]))
                            units.append((load, compute, lambda: None))
                            first = False
            self.run_units(units)
            for c in range(5):
                st["fin"](c)
            st.pop("fin")
        kb.barrier()

    def phase_out(self, layer):
        kb = self.kb
        Tn, NB = self.T, self.NB
        g = self.gtiles
        misc, miscb = g["misc"]
        W = self.w_out_e if layer == 0 else self.w_out_o
        with contextlib.ExitStack() as es:
            wo = self.sb(es, "o_w", [128, KC, D], BF16)
            wob = Buf()
            xr_ = [self.sb(es, "o_xr%d" % i, [128, D], F32) for i in range(2)]
            xrb = [Buf(), Buf()]
            for kc in range(KC):
                kb.dma(xr_[kc % 2][:], W[kc * 128:(kc + 1) * 128, :], writes=[xrb[kc % 2]])
                kb.op("pool", lambda h: h.tensor_copy(wo[:, kc, :], xr_[kc % 2][:]), reads=[xrb[kc % 2]], writes=[wob])
            lg = self.sb(es, "o_lg", [128, D], F32)
            lb = self.sb(es, "o_lb", [128, D], F32)
            lgb = Buf()
            kb.dma(lg[:], dap(self.ln_g, layer * D, [[0, 128], [1, D]]), writes=[lgb])
            kb.dma(lb[:], dap(self.ln_b, layer * D, [[0, 128], [1, D]]), writes=[lgb])
            zb_ = [self.sb(es, "o_zb%d" % i, [128, KC, TT], BF16) for i in range(2)]
            zbb = [Buf(), Buf()]
            v = self.sb(es, "o_v", [128, D], F32)
            vb = Buf()
            xo = [self.sb(es, "o_xo%d" % i, [128, D], F32) for i in range(2)]
            xob = [Buf(), Buf()]
            stt_ = self.sb(es, "o_st", [128, 8], F32)
            stb = Buf()
            xbf = [self.sb(es, "o_xbf%d" % i, [128, D], BF16) for i in range(3)]
            xbfb = [Buf(), Buf(), Buf()]
            xTb_ = [self.sb(es, "o_xT%d" % i, [128, KC, TT], BF16) for i in range(2)]
            xTbb = [Buf(), Buf()]
            units = []
            it = 0
            hcnt = [0]
            src = self.x if layer == 0 else self.X1
            for s in range(NSEG):
                for b in range(NB):
                    sl = it % 2
                    it += 1

                    zs = (it - 1) // 4 % 2
                    bq = b % 4

                    s3 = (it - 1) % 3

                    def load(s=s, b=b, sl=sl):
                        kb.dma(xr_[sl][:], src[s, b * 128:(b + 1) * 128, :], writes=[xrb[sl]])

                    def loadz(s=s, b=b, zs=zs):
                        kb.dma(zb_[zs][:], self.ZT[s, :, b * 128:b * 128 + TT].rearrange("(k p) n -> p k n", p=128), writes=[zbb[zs]])

                    def compute(s=s, b=b, sl=sl, zs=zs, bq=bq, s3=s3):
                        if layer == 1:
                            halves = [[4 * sl + i for i in range(4)]]
                        else:
                            halves = []
                            for hf in range(2):
                                pp = (2 * hcnt[0]) % 6
                                hcnt[0] += 1
                                halves.append([pp, pp + 1])
                        nt0 = 0
                        for banks in halves:
                            fns = []
                            for kc in range(KC):
                                for j, pbk in enumerate(banks):
                                    nt = nt0 + j
                                    fns.append(lambda h, kc=kc, nt=nt, pbk=pbk: h.matmul(self.ps[pbk][:, 0:512], zb_[zs][:, kc, bq * 128:(bq + 1) * 128], wo[:, kc, nt * 512:(nt + 1) * 512],
                                                                                     start=(kc == 0), stop=(kc == KC - 1)))
                            kb.group("pe", fns, reads=[zbb[zs], wob], writes=[self.psb[p_] for p_ in banks])
                            for j, pbk in enumerate(banks):
                                nt = nt0 + j
                                kb.op("dve", lambda h: h.scalar_tensor_tensor(out=v[:, nt * 512:(nt + 1) * 512], in0=xr_[sl][:, nt * 512:(nt + 1) * 512], scalar=ALPHA,
                                                                               in1=self.ps[pbk][:, 0:512], op0=ALU.mult, op1=ALU.add),
                                      reads=[xrb[sl], self.psb[pbk]], writes=[vb])
                            nt0 += len(banks)
                        kb.op("act", lambda h: h.activation(out=xbf[s3][:], in_=v[:], func=AF.Copy, accum_out=stt_[:, 0:1]), reads=[vb], writes=[xbfb[s3], stb])
                        kb.op("act", lambda h: h.activation(out=xbf[s3][:], in_=v[:], func=AF.Square, accum_out=stt_[:, 1:2]), reads=[vb], writes=[xbfb[s3], stb])
                        kb.op("dve", lambda h: h.tensor_scalar_mul(out=stt_[:, 2:3], in0=stt_[:, 0:1], scalar1=1.0 / D), reads=[stb], writes=[stb])
                        kb.op("dve", lambda h: h.tensor_tensor(out=stt_[:, 3:4], in0=stt_[:, 2:3], in1=stt_[:, 2:3], op=ALU.mult), reads=[stb], writes=[stb])
                        kb.op("dve", lambda h: h.scalar_tensor_tensor(out=stt_[:, 4:5], in0=stt_[:, 1:2], scalar=1.0 / D, in1=stt_[:, 3:4], op0=ALU.mult, op1=ALU.subtract),
                              reads=[stb], writes=[stb])
                        kb.op("act", lambda h: h.activation(out=stt_[:, 5:6], in_=stt_[:, 4:5], func=AF.Sqrt, bias=misc[:, 1:2]), reads=[stb, miscb], writes=[stb])
                        kb.op("dve", lambda h: h.reciprocal(stt_[:, 6:7], stt_[:, 5:6]), reads=[stb], writes=[stb])
                        xo_t, xo_b = xo[sl], xob[sl]
                        kb.op("dve", lambda h: h.scalar_tensor_tensor(out=xo_t[:], in0=v[:], scalar=stt_[:, 2:3], in1=lg[:], op0=ALU.subtract, op1=ALU.mult),
                              reads=[vb, stb, lgb], writes=[xo_b])
                        kb.op("dve", lambda h: h.scalar_tensor_tensor(out=xo_t[:], in0=xo_t[:], scalar=stt_[:, 6:7], in1=lb[:], op0=ALU.mult, op1=ALU.add),
                              reads=[xo_b, stb, lgb], writes=[xo_b])
                        if layer == 0:
                            kb.op("pool", lambda h: h.tensor_copy(xbf[s3][:], xo_t[:]), reads=[xo_b], writes=[xbfb[s3]])

                    def late(s=s, b=b, s3=s3, zs=zs, bq=bq):
                        if layer == 0:
                            self.transpose_block(xbf[s3], xbfb[s3], xTb_[zs], xTbb[zs], bq, "act" if b % 2 == 0 else "dve")
                            if bq == 3:
                                kb.dma(self.X1T[s, :, (b - 3) * 128:(b + 1) * 128].rearrange("(k p) n -> p k n", p=128), xTb_[zs][:], reads=[xTbb[zs]])

                    def store(s=s, b=b, sl=sl):
                        xo_t, xo_b = xo[sl], xob[sl]
                        if layer == 0:
                            kb.dma(self.X1[s, b * 128:(b + 1) * 128, :], xo_t[:], reads=[xo_b])
                        else:
                            kb.dma(self.y[s, b * 128:(b + 1) * 128, :], xo_t[:], reads=[xo_b])
                    units.append((load, compute, store, late, loadz))
            n = len(units)
            units[0][4]()
            units[0][0]()
            for i in range(n):
                if i + 1 < n:
                    units[i + 1][0]()
                if i % 4 == 0 and i + 4 < n:
                    units[i + 4][4]()
                units[i][1]()
                if i > 1:
                    units[i - 2][3]()
                units[i][2]()
            if n > 1:
                units[n - 2][3]()
            units[n - 1][3]()
        kb.barrier()

    def phase_odd_in(self, s):
        kb = self.kb
        Tn, NTT, NB = self.T, self.NTT, self.NB
        g = self.gtiles
        rotO, rotOb = g["rotO"]
        with contextlib.ExitStack() as es:
            xT = self.sb(es, "f_xT", [128, KC, Tn], BF16)
            xTb = [Buf() for _ in range(NB)]
            for tq in range(NTT):
                kb.dma(xT[:, :, tq * TT:(tq + 1) * TT], self.X1T[s, :, tq * TT:(tq + 1) * TT].rearrange("(k p) n -> p k n", p=128), writes=xTb[4 * tq:4 * tq + 4])
            cs = self.sb(es, "f_cs", [128, 2, Tn], F32)
            csb = Buf()
            kb.dma(cs[:, 0, :], self.cosO[s], writes=[csb])
            kb.dma(cs[:, 1, :], self.sinO[s], writes=[csb])
            stg = [self.sb(es, "f_stg%d" % i, [128, Tn], BF16) for i in range(2)]
            stgb = [Buf(), Buf()]
            vst = [self.sb(es, "f_vst%d" % i, [128, NB, 256], BF16) for i in range(2)]
            vstb = [Buf(), Buf()]
            qb_ = [self.sb(es, "f_qb%d" % i, [128, TT], BF16) for i in range(3)]
            t1 = [self.sb(es, "f_t1%d" % i, [128, TT], F32) for i in range(3)]
            t2 = [self.sb(es, "f_t2%d" % i, [128, TT], F32) for i in range(3)]
            tb = {k: [Buf(), Buf(), Buf()] for k in ("qb", "t1", "t2")}
            cnt = {"mm": 0, "ep": 0, "v": 0, "rc": 0}

            def mm_tile(wbf, wbfb, tt):
                pi = cnt["mm"] % 4
                cnt["mm"] += 1
                fns = [(lambda h, kc=kc: h.matmul(self.ps[pi][:, 0:TT], wbf[:, kc, 0:128], xT[:, kc, tt * TT:(tt + 1) * TT],
                                                  start=(kc == 0), stop=(kc == KC - 1))) for kc in range(KC)]
                kb.group("pe", fns, reads=[wbfb] + xTb[4 * tt:4 * tt + 4], writes=[self.psb[pi]])
                return pi

            def chunk_silu(fo, dst_ap):
                def run(wbf, wbfb):
                    run_due(flush=True)
                    sl = fo % 2
                    for tt in range(NTT):
                        pi = mm_tile(wbf, wbfb, tt)
                        kb.op("act", lambda h: h.activation(out=stg[sl][:, tt * TT:(tt + 1) * TT], in_=self.ps[pi][:, 0:TT], func=AF.Silu),
                              reads=[self.psb[pi]], writes=[stgb[sl]])
                    kb.dma(dst_ap, stg[sl][:], reads=[stgb[sl]])
                return run

            pend = []
            clock = {"k": 0}

            def run_due(flush=False):
                while pend and (flush or pend[0][0] <= clock["k"]):
                    pend.pop(0)[1]()

            def chunk_rope(fo, dst_ap):
                def run(wbf, wbfb):
                    sl = cnt["rc"] % 2
                    cnt["rc"] += 1
                    for tt in range(NTT):
                        k = clock["k"]
                        pi = mm_tile(wbf, wbfb, tt)
                        e = k % 3
                        pr = 4 + (k % 2)
                        kb.op("act", lambda h: h.copy(qb_[e][:], self.ps[pi][:, 0:TT]), reads=[self.psb[pi]], writes=[tb["qb"][e]])

                        def stage_b(pi=pi, e=e, pr=pr, tt=tt, sl=sl):
                            kb.group("pe", [lambda h: h.matmul(self.ps[pr][:, 0:TT], rotO[:], qb_[e][:], start=True, stop=True)],
                                     reads=[tb["qb"][e], rotOb], writes=[self.psb[pr]])
                            kb.op("dve", lambda h: h.tensor_tensor(out=t1[e][:], in0=self.ps[pi][:, 0:TT], in1=cs[:, 0, tt * TT:(tt + 1) * TT], op=ALU.mult),
                                  reads=[self.psb[pi], csb], writes=[tb["t1"][e]])
                            kb.op("dve", lambda h: h.tensor_tensor(out=t2[e][:], in0=self.ps[pr][:, 0:TT], in1=cs[:, 1, tt * TT:(tt + 1) * TT], op=ALU.mult),
                                  reads=[self.psb[pr], csb], writes=[tb["t2"][e]])
                            kb.op("dve", lambda h: h.tensor_tensor(out=stg[sl][:, tt * TT:(tt + 1) * TT], in0=t1[e][:], in1=t2[e][:], op=ALU.add),
                                  reads=[tb["t1"][e], tb["t2"][e]], writes=[stgb[sl]])
                        pend.append((k + 1, stage_b))
                        if tt == NTT - 1:
                            pend.append((k + 1, lambda sl=sl: kb.dma(dst_ap, stg[sl][:], reads=[stgb[sl]])))
                        clock["k"] += 1
                        run_due()
                return run

            def chunk_v(vc):
                def run(wbf, wbfb):
                    run_due(flush=True)
                    sl = vc % 2
                    for b in range(NB):
                        pi = cnt["mm"] % 4
                        cnt["mm"] += 1
                        fns = [(lambda h, kc=kc: h.matmul(self.ps[pi][:, 0:256], xT[:, kc, b * 128:(b + 1) * 128], wbf[:, kc, 0:256],
                                                          start=(kc == 0), stop=(kc == KC - 1))) for kc in range(KC)]
                        kb.group("pe", fns, reads=[wbfb, xTb[b]], writes=[self.psb[pi]])
                        kb.op("act", lambda h: h.copy(vst[sl][:, b, :], self.ps[pi][:, 0:256]), reads=[self.psb[pi]], writes=[vstb[sl]])
                    kb.dma(dap(self.V2, s * Tn * D + vc * 256, [[D, 128], [128 * D, NB], [1, 256]]), vst[sl][:], reads=[vstb[sl]])
                return run

            specs = []
            for fo in range(16):
                specs.append((fo * 128, 128, chunk_rope(fo, self.Q2T[s, fo])))
            for fo in range(16):
                specs.append((2048 + fo * 128, 128, chunk_rope(fo, self.K2T[s, fo])))
            for vc in range(8):
                specs.append((4096 + vc * 256, 256, chunk_v(vc)))
            for fo in range(16):
                specs.append((6144 + fo * 128, 128, chunk_silu(fo, self.G2T[s, fo * 128:(fo + 1) * 128, :])))
            import os
            sel = os.environ.get("ODD_SEL")
            if sel:
                a, b = [int(v) for v in sel.split(":")]
                specs = specs[a:b]
            self.linear(es, self.w_in_o, ODD_IN, xT, xTb, specs, "f_")
            run_due(flush=True)
        kb.barrier()

    def build(self):
        nc = self.nc

        def ph(name, fn, *a):
            with nc.named_scope(name):
                fn(*a)
        ph("consts", self.load_consts)
        ph("ssm_pre", self.phase_ssm_pre)
        for s in range(NSEG):
            ph("even_in%d" % s, self.phase_even_in, s)
        ph("ssm", self.phase_ssm)
        ph("glu", self.phase_glu)
        ph("gqa", self.phase_gqa)
        ph("out0", self.phase_out, 0)
        for s in range(NSEG):
            ph("odd_in%d" % s, self.phase_odd_in, s)
        ph("dil", self.phase_dil)
        ph("out1", self.phase_out, 1)
        self.kb.finish()
        return self.nc


def _rot_tables(Tn, pos0, kind):
    f32 = np.float32
    pos = (np.arange(Tn, dtype=f32) + f32(pos0))
    cos = np.ones((128, Tn), f32)
    sin = np.zeros((128, Tn), f32)
    if kind == "E":
        n = 64
        inv = (f32(10000.0) ** (-(np.arange(0, n, 2, dtype=f32) / f32(n)))).astype(f32)
        row = np.floor(pos / f32(64)).astype(f32)
        col = (pos - row * f32(64)).astype(f32)
        for half, pp in ((0, row), (1, col)):
            ang = (pp[None, :] * inv[:, None]).astype(f32)
            c, s_ = np.cos(ang).astype(f32), np.sin(ang).astype(f32)
            cos[half * 64:half * 64 + 32] = c
            cos[half * 64 + 32:half * 64 + 64] = c
            sin[half * 64:half * 64 + 32] = s_
            sin[half * 64 + 32:half * 64 + 64] = s_
    else:
        n = 32
        inv = (f32(500000.0) ** (-(np.arange(0, n, 2, dtype=f32) / f32(n)))).astype(f32)
        ang = (pos[None, :] * inv[:, None]).astype(f32)
        c, s_ = np.cos(ang).astype(f32), np.sin(ang).astype(f32)
        cos[0:16] = c
        cos[16:32] = c
        sin[0:16] = s_
        sin[16:32] = s_
    return cos, sin


def _rot_mats():
    bf = ml_dtypes.bfloat16
    rE = np.zeros((128, 128), np.float32)
    for m in range(128):
        if (m % 64) < 32:
            rE[m + 32, m] = -1.0
        else:
            rE[m - 32, m] = 1.0
    rO = np.zeros((128, 128), np.float32)
    for m in range(32):
        if m < 16:
            rO[m + 16, m] = -1.0
        else:
            rO[m - 16, m] = 1.0
    return rE.astype(bf), rO.astype(bf)


def _dil_masks():
    out = np.zeros((20, 128, 512), np.float32)
    k = np.arange(128)[:, None]
    q = np.arange(512)[None, :]
    for i in range(20):
        dl = (i * 128 - 1024) + k - q
        a = np.abs(dl)
        m = (a <= 64).astype(np.float32)
        m += ((a <= 256) & (dl % 4 == 0)).astype(np.float32)
        m += ((a <= 1024) & (dl % 16 == 0)).astype(np.float32)
        out[i] = m
    return out.astype(ml_dtypes.bfloat16)


def _mmasks():
    jc = np.arange(128) // 16
    mf = (jc[:, None] <= jc[None, :]).astype(np.float32)
    mb = (jc[:, None] >= jc[None, :]).astype(np.float32)
    return np.stack([mf, mb], 0)


def make_core_inputs(inputs, Tn=T):
    xp = np.asarray(inputs["x_prompt"])
    xs = np.asarray(inputs["x_sample"])
    rE, rO = _rot_mats()
    shared = {
        "w_in_e": np.ascontiguousarray(inputs["even_w_in"][0]), "w_out_e": np.ascontiguousarray(inputs["even_w_out"][0]),
        "w_glu": np.ascontiguousarray(inputs["ssm_glu_w"][0]), "b_glu": np.ascontiguousarray(inputs["ssm_glu_b"][0]),
        "w_in_o": np.ascontiguousarray(inputs["odd_w_in"][0]), "w_out_o": np.ascontiguousarray(inputs["odd_w_out"][0]),
        "ln_g": np.ascontiguousarray(inputs["ln_g"]), "ln_b": np.ascontiguousarray(inputs["ln_b"]),
        "a_re": np.ascontiguousarray(inputs["ssm_a_re"][0]), "a_im": np.ascontiguousarray(inputs["ssm_a_im"][0]),
        "log_dt": np.ascontiguousarray(inputs["ssm_log_dt"][0]),
        "b_re": np.ascontiguousarray(inputs["ssm_b_re"][0]), "b_im": np.ascontiguousarray(inputs["ssm_b_im"][0]),
        "c_re": np.ascontiguousarray(inputs["ssm_c_re"][0]), "c_im": np.ascontiguousarray(inputs["ssm_c_im"][0]),
        "d_skip": np.ascontiguousarray(inputs["ssm_d"][0]),
        "gq": np.ascontiguousarray(inputs["attn_q_norm"][0]), "gk": np.ascontiguousarray(inputs["attn_k_norm"][0]),
        "identf": np.eye(128, dtype=np.float32), "identb": np.eye(128, dtype=np.float32).astype(ml_dtypes.bfloat16),
        "rotE": rE, "rotO": rO, "dmask": _dil_masks(), "mmask": _mmasks(),
    }
    shared = {k: np.asarray(v, dtype=v.dtype if v.dtype != np.float64 else np.float32) for k, v in shared.items()}
    tabs = {}
    for linked in (0, 1):
        p0 = [0, Tn * linked, 0]
        cE, sE, cO, sO = [], [], [], []
        for sg in range(NSEG):
            c, s_ = _rot_tables(Tn, p0[sg], "E")
            cE.append(c); sE.append(s_)
            c, s_ = _rot_tables(Tn, p0[sg], "O")
            cO.append(c); sO.append(s_)
        fl = np.zeros((128, 2), np.float32)
        fl[:, 0] = float(linked)
        fl[:, 1] = 0.0 if linked else NEG
        tabs[linked] = {"cosE": np.stack(cE), "sinE": np.stack(sE), "cosO": np.stack(cO), "sinO": np.stack(sO), "flag": fl}
    maps = []
    for c in range(8):
        if c < 4:
            xx = np.stack([xp[c, 0:Tn], xp[c, Tn:2 * Tn], xs[c, 0:Tn]], 0)
            linked = 1
        else:
            b0 = 4 + 3 * (c - 4)
            xx = np.stack([xs[b0, 0:Tn], xs[b0 + 1, 0:Tn], xs[b0 + 2, 0:Tn]], 0)
            linked = 0
        m = dict(shared)
        m.update(tabs[linked])
        m["x"] = np.ascontiguousarray(xx, dtype=np.float32)
        maps.append(m)
    return maps


_PROG_CACHE = {}


def kernel(**inputs):
    maps = make_core_inputs(inputs, T)
    if "nc" not in _PROG_CACHE:
        _PROG_CACHE["nc"] = Prog(T_=T).build()
    nc = _PROG_CACHE["nc"]
    res = run_bass_kernel_spmd(nc, maps, core_ids=list(range(8)))
    yp = np.empty((4, 2 * T, D), np.float32)
    ys = np.empty((16, T, D), np.float32)
    for c in range(8):
        y = np.asarray(res.results[c]["y"])
        if c < 4:
            yp[c, 0:T] = y[0]
            yp[c, T:2 * T] = y[1]
            ys[c] = y[2]
        else:
            b0 = 4 + 3 * (c - 4)
            ys[b0], ys[b0 + 1], ys[b0 + 2] = y[0], y[1], y[2]
    return (yp, ys)
```

```python
import contextlib
import math
import numpy as np
import ml_dtypes
import concourse.bass as bass
import concourse.mybir as mybir
from concourse.bass_utils import run_bass_kernel_spmd
from concourse.ap import AP

F32 = mybir.dt.float32
BF16 = mybir.dt.bfloat16
AF = mybir.ActivationFunctionType
ALU = mybir.AluOpType

D = 2048
T = 2048
NSEG = 3
KC = 16
TT = 512
EVEN_IN = 4608
ODD_IN = 8192
G = 64
GB = 16
GBP = 8
ALPHA = 4 ** 0.25
LN_EPS = 1e-5
QK_EPS = 1e-6
NEG = -30000.0
SEM_WRAP = 24000


class Tok:
    __slots__ = ("sid", "sem", "val")

    def __init__(self, sid, sem, val):
        self.sid, self.sem, self.val = sid, sem, val


class Buf:
    __slots__ = ("name", "w", "r", "excl")

    def __init__(self, name="", excl=False):
        self.name = name
        self.w = None
        self.r = {}
        self.excl = excl


class Eng:
    def __init__(self, kb, name, h):
        self.kb, self.name, self.h = kb, name, h
        self.sem = None
        self.sid = None
        self.cnt = 0
        self.seen = {}
        self.last = None


class KB:
    def __init__(self, nc):
        self.nc = nc
        self.es = contextlib.ExitStack()
        self.nsem = 0
        self.e = {
            "pe": Eng(self, "pe", nc.tensor),
            "act": Eng(self, "act", nc.scalar),
            "dve": Eng(self, "dve", nc.vector),
            "pool": Eng(self, "pool", nc.gpsimd),
            "sp": Eng(self, "sp", nc.sync),
        }
        self.dsems = []
        self.dnext = 0
        self.NDS = 40
        self.all_dma_toks = {}

    def new_sem(self):
        s = self.es.enter_context(self.nc.semaphore("s%d" % self.nsem))
        self.nsem += 1
        return (self.nsem, s)

    def wait(self, en, tok):
        if tok is None:
            return
        e = self.e[en]
        if e.seen.get(tok.sid, 0) >= tok.val:
            return
        e.h.wait_ge(tok.sem, tok.val)
        e.seen[tok.sid] = tok.val

    def signal(self, en, inst):
        e = self.e[en]
        if e.sem is None or e.cnt >= SEM_WRAP:
            e.sid, e.sem = self.new_sem()
            e.cnt = 0
        inst.then_inc(e.sem, 1)
        e.cnt += 1
        t = Tok(e.sid, e.sem, e.cnt)
        e.last = t
        return t

    def _deps(self, en, reads, writes):
        for b in reads:
            if b.w is not None and not (en == "pe" and b.w.sid == self.e["pe"].sid):
                self.wait(en, b.w)
            if b.excl:
                for t in b.r.values():
                    if t.sid != self.e[en].sid:
                        self.wait(en, t)
        for b in writes:
            if b.w is not None and not (en == "pe" and b.w.sid == self.e["pe"].sid):
                self.wait(en, b.w)
            for t in b.r.values():
                if not (en == "pe" and t.sid == self.e["pe"].sid):
                    self.wait(en, t)

    def _mark(self, tok, reads, writes):
        for b in reads:
            b.r[tok.sid] = tok
        for b in writes:
            b.w = tok
            b.r = {}

    def op(self, en, fn, reads=(), writes=()):
        self._deps(en, reads, writes)
        inst = fn(self.e[en].h)
        tok = self.signal(en, inst)
        self._mark(tok, reads, writes)
        return tok

    def group(self, en, fns, reads=(), writes=()):
        self._deps(en, reads, writes)
        inst = None
        for fn in fns:
            inst = fn(self.e[en].h)
        tok = self.signal(en, inst)
        self._mark(tok, reads, writes)
        return tok

    def dma(self, out, in_, reads=(), writes=(), q="sp", slow=False):
        self._deps(q, reads, writes)
        if len(self.dsems) < self.NDS:
            sid, sem = self.new_sem()
            ent = [sid, sem, 0, None]
            self.dsems.append(ent)
        else:
            ent = self.dsems[self.dnext % self.NDS]
        self.dnext += 1
        if ent[3] is not None:
            self.wait(q, ent[3])
        if slow:
            self.e[q].h.dma_start(out=out, in_=in_, allow_slow_non_contiguous=True).then_inc(ent[1], 16)
        else:
            self.e[q].h.dma_start(out=out, in_=in_).then_inc(ent[1], 16)
        ent[2] += 16
        tok = Tok(ent[0], ent[1], ent[2])
        ent[3] = tok
        self.all_dma_toks[ent[0]] = tok
        self._mark(tok, reads, writes)
        return tok

    def barrier(self):
        toks = [e.last for e in self.e.values() if e.last is not None] + list(self.all_dma_toks.values())
        for en in self.e:
            for t in toks:
                if en == "pe" and t.sid == self.e["pe"].sid:
                    continue
                self.wait(en, t)

    def finish(self):
        toks = [e.last for e in self.e.values() if e.last is not None] + list(self.all_dma_toks.values())
        for t in toks:
            self.wait("sp", t)


def dap(base, off, dims):
    return AP(base.tensor, base.offset + off, [list(d) for d in dims])


def rev_last(ap, n):
    dims = [list(d) for d in ap.ap]
    assert dims[-1][0] == 1
    dims[-1] = [-1, n]
    return AP(ap.tensor, ap.offset + (n - 1), dims)


class Prog:
    def __init__(self, T_=T, dbg=()):
        self.T = T_
        self.NTT = T_ // TT
        self.NB = T_ // 128
        self.NCH = T_ // 8
        self.dbg = set(dbg)
        nc = bass.Bass("TRN2", target_bir_lowering=False)
        self.nc = nc
        self.kb = KB(nc)
        Tn = T_

        def din(name, shape, dt=F32):
            return nc.dram_tensor(name, list(shape), dt, kind="ExternalInput").ap()

        def dscr(name, shape, dt=BF16):
            kind = "ExternalOutput" if name in self.dbg else "Internal"
            return nc.dram_tensor(name, list(shape), dt, kind=kind).ap()

        self.x = din("x", [NSEG, Tn, D])
        self.y = nc.dram_tensor("y", [NSEG, Tn, D], F32, kind="ExternalOutput").ap()
        self.w_in_e = din("w_in_e", [D, EVEN_IN])
        self.w_out_e = din("w_out_e", [D, D])
        self.w_glu = din("w_glu", [1024, 1024])
        self.b_glu = din("b_glu", [1024])
        self.w_in_o = din("w_in_o", [D, ODD_IN])
        self.w_out_o = din("w_out_o", [D, D])
        self.ln_g = din("ln_g", [2, D])
        self.ln_b = din("ln_b", [2, D])
        self.a_re = din("a_re", [2, G, 64])
        self.a_im = din("a_im", [2, G, 64])
        self.log_dt = din("log_dt", [2, G])
        self.b_re = din("b_re", [G, 64, 16])
        self.b_im = din("b_im", [G, 64, 16])
        self.c_re = din("c_re", [2, G, 16, 64])
        self.c_im = din("c_im", [2, G, 16, 64])
        self.d_skip = din("d_skip", [1024])
        self.gq = din("gq", [128])
        self.gk = din("gk", [128])
        self.cosE = din("cosE", [NSEG, 128, Tn])
        self.sinE = din("sinE", [NSEG, 128, Tn])
        self.cosO = din("cosO", [NSEG, 128, Tn])
        self.sinO = din("sinO", [NSEG, 128, Tn])
        self.c_identf = din("identf", [128, 128])
        self.c_identb = din("identb", [128, 128], BF16)
        self.c_rotE = din("rotE", [128, 128], BF16)
        self.c_rotO = din("rotO", [128, 128], BF16)
        self.c_dmask = din("dmask", [20, 128, 512], BF16)
        self.c_mmask = din("mmask", [2, 128, 128])
        self.c_flag = din("flag", [128, 2])
        self.UT = dscr("UT", [NSEG, 1024, Tn])
        self.AG = dscr("AG", [NSEG, 1024, Tn])
        self.QT = dscr("QT", [NSEG, 8, 128, Tn])
        self.KT = dscr("KT", [NSEG, 2, 128, Tn])
        self.VV = dscr("VV", [NSEG, Tn, 256])
        self.BG = dscr("BG", [NSEG, 1024, Tn])
        self.YT = dscr("YT", [NSEG, 1024, Tn])
        self.ZT = dscr("ZT", [NSEG, D, Tn])
        self.X1 = dscr("X1", [NSEG, Tn, D], F32)
        self.X1T = dscr("X1T", [NSEG, D, Tn])
        self.Q2T = dscr("Q2T", [NSEG, 16, 128, Tn])
        self.K2T = dscr("K2T", [NSEG, 16, 128, Tn])
        self.V2 = dscr("V2", [NSEG, Tn, D])
        self.G2T = dscr("G2T", [NSEG, D, Tn])
        self.MG = dscr("MG", [G, 128, 128])
        self.BIN = dscr("BIN", [G, 128, 2, 128])
        self.COUT = dscr("COUT", [G, 128, 4, 128])
        self.scrbuf = {}
        self.ges = contextlib.ExitStack()
        self.ps = [self.ges.enter_context(nc.psum_tensor("ps%d" % i, [128, 512], F32)) for i in range(8)]
        self.psb = [Buf("ps%d" % i, excl=True) for i in range(8)]
        self.gtiles = {}

    def sb(self, es, name, shape, dt=F32):
        self._nsb = getattr(self, "_nsb", 0) + 1
        return es.enter_context(self.nc.sbuf_tensor("%s_%d" % (name, self._nsb), list(shape), dt))

    def dbuf(self, key):
        b = self.scrbuf.get(key)
        if b is None:
            b = Buf(str(key))
            self.scrbuf[key] = b
        return b

    def load_consts(self):
        kb, nc, es = self.kb, self.nc, self.ges
        g = self.gtiles

        def ld(name, src, shape, dt, slow=False):
            t = self.sb(es, "c_" + name, shape, dt)
            b = Buf(name)
            kb.dma(t[:], src, writes=[b], slow=slow)
            g[name] = (t, b)

        ld("identf", self.c_identf[:, :], [128, 128], F32)
        ld("identb", self.c_identb[:, :], [128, 128], BF16)
        ld("rotE", self.c_rotE[:, :], [128, 128], BF16)
        ld("rotO", self.c_rotO[:, :], [128, 128], BF16)
        ld("flag", self.c_flag[:, :], [128, 2], F32)
        ld("gq", dap(self.gq, 0, [[1, 128], [1, 1]]), [128, 1], F32)
        ld("gk", dap(self.gk, 0, [[1, 128], [1, 1]]), [128, 1], F32)
        ld("gqb", dap(self.gq, 0, [[0, 128], [1, 128]]), [128, 128], F32)
        ld("gkb", dap(self.gk, 0, [[0, 128], [1, 128]]), [128, 128], F32)
        ld("bglu", dap(self.b_glu, 0, [[1, 128], [128, 8]]), [128, 8], F32, slow=True)
        g["mu"] = (self.sb(es, "c_mu", [128, 2, G], F32), Buf("mu"))
        g["mup"] = (self.sb(es, "c_mup", [128, 2, G, 16], F32), Buf("mup"))
        t = self.sb(es, "c_misc", [128, 8], F32)
        b = Buf("misc")
        g["misc"] = (t, b)
        kb.op("dve", lambda h: h.memset(t[:, 0:1], QK_EPS), writes=[b])
        kb.op("dve", lambda h: h.memset(t[:, 1:2], LN_EPS), writes=[b])
        kb.op("dve", lambda h: h.memset(t[:, 2:3], 0.0), writes=[b])
        kb.op("dve", lambda h: h.memset(t[:, 3:4], math.pi / 2), writes=[b])
        o1 = self.sb(es, "c_ones", [128, 128], BF16)
        o2 = self.sb(es, "c_ones128", [128, 128], BF16)
        b1 = Buf("ones")
        kb.op("dve", lambda h: h.memset(o1[:], 1.0), writes=[b1])
        kb.op("dve", lambda h: h.memset(o2[:], 1.0 / 128), writes=[b1])
        g["ones"] = (o1, b1)
        g["ones128"] = (o2, b1)
        mq = t[:, 4:5]
        mk = t[:, 5:6]
        gqb, gqbb = g["gqb"]
        gkb, gkbb = g["gkb"]
        kb.op("dve", lambda h: h.tensor_reduce(out=mq, in_=gqb[:], axis=mybir.AxisListType.X, op=ALU.max,
                                                apply_absolute_value=True), reads=[gqbb], writes=[b])
        kb.op("dve", lambda h: h.tensor_reduce(out=mk, in_=gkb[:], axis=mybir.AxisListType.X, op=ALU.max,
                                                apply_absolute_value=True), reads=[gkbb], writes=[b])
        kb.op("dve", lambda h: h.scalar_tensor_tensor(out=t[:, 6:7], in0=mq, scalar=-math.sqrt(128.0), in1=mk,
                                                       op0=ALU.mult, op1=ALU.mult), reads=[b], writes=[b])
        fl, flb = g["flag"]
        kb.op("dve", lambda h: h.tensor_tensor(out=t[:, 7:8], in0=t[:, 6:7], in1=fl[:, 1:2], op=ALU.add),
              reads=[b, flb], writes=[b])

    def load_xT(self, es, src, xT, xTb, pfx):
        kb = self.kb
        xin = [self.sb(es, pfx + "xin%d" % i, [128, D], F32) for i in range(2)]
        xbf = [self.sb(es, pfx + "xbf%d" % i, [128, D], BF16) for i in range(2)]
        xinb = [Buf(), Buf()]
        xbfb = [Buf(), Buf()]
        ident, identb = self.gtiles["identb"]
        for b in range(self.NB):
            sl = b % 2
            kb.dma(xin[sl][:], src[b * 128:(b + 1) * 128, :], writes=[xinb[sl]])
            kb.op("pool", lambda h: h.tensor_copy(xbf[sl][:], xin[sl][:]), reads=[xinb[sl]], writes=[xbfb[sl]])
            self.transpose_block(xbf[sl], xbfb[sl], xT, xTb[b], b, "act" if b % 2 == 0 else "dve")

    def transpose_block(self, xbf, xbfb, xT, xTblk, b, evac_eng):
        kb = self.kb
        ident, identb = self.gtiles["identb"]
        for q in range(4):
            pi = 6 + (q % 2)
            psv = self.ps[pi][:].bitcast(BF16)
            fns = []
            for i in range(4):
                kc = 4 * q + i
                fns.append(lambda h, i=i, kc=kc: h.transpose(psv[:, i * 128:(i + 1) * 128], xbf[:, kc * 128:(kc + 1) * 128], ident[:]))
            kb.group("pe", fns, reads=[xbfb, identb], writes=[self.psb[pi]])
            outv = xT[:, 4 * q:4 * q + 4, b * 128:(b + 1) * 128]
            inv = psv[:, 0:512].rearrange("p (a n) -> p a n", a=4)
            if evac_eng == "act":
                kb.op("act", lambda h: h.copy(outv, inv), reads=[self.psb[pi]], writes=[xTblk])
            else:
                kb.op("dve", lambda h: h.tensor_copy(outv, inv), reads=[self.psb[pi]], writes=[xTblk])

    def linear(self, es, W, ncols, xT, xTb, specs, pfx, maxw=256):
        kb = self.kb
        wst = [self.sb(es, pfx + "wst%d" % i, [128, KC, maxw], F32) for i in range(2)]
        wbf = [self.sb(es, pfx + "wbf%d" % i, [128, KC, maxw], BF16) for i in range(2)]
        wstb = [Buf(), Buf()]
        wbfb = [Buf(), Buf()]

        def fetch(i):
            c0, wd, _ = specs[i]
            sl = i % 2
            src = dap(W, c0, [[ncols, 128], [128 * ncols, KC], [1, wd]])
            kb.dma(wst[sl][:, :, 0:wd], src, writes=[wstb[sl]])
            kb.op("pool", lambda h: h.tensor_copy(wbf[sl][:, :, 0:wd], wst[sl][:, :, 0:wd]), reads=[wstb[sl]], writes=[wbfb[sl]])

        fetch(0)
        for i in range(len(specs)):
            if i + 1 < len(specs):
                fetch(i + 1)
            specs[i][2](wbf[i % 2], wbfb[i % 2])

    def phase_even_in(self, s):
        kb, nc = self.kb, self.nc
        Tn, NTT, NB = self.T, self.NTT, self.NB
        g = self.gtiles
        with contextlib.ExitStack() as es:
            xT = self.sb(es, "e_xT", [128, KC, Tn], BF16)
            xTb = [Buf("xT%d" % b) for b in range(NB)]
            self.load_xT(es, self.x[s], xT, xTb, "e_")
            cs = self.sb(es, "e_cs", [128, 2, Tn], F32)
            csb = Buf("cs")
            kb.dma(cs[:, 0, :], self.cosE[s], writes=[csb])
            kb.dma(cs[:, 1, :], self.sinE[s], writes=[csb])
            stg = [self.sb(es, "e_stg%d" % i, [128, Tn], BF16) for i in range(2)]
            stgb = [Buf(), Buf()]
            vst = self.sb(es, "e_vst", [128, NB, 256], BF16)
            vstb = Buf()
            sqb = [self.sb(es, "e_sq%d" % i, [128, TT], BF16) for i in range(3)]
            sd = [self.sb(es, "e_sd%d" % i, [128, TT], F32) for i in range(3)]
            qn = [self.sb(es, "e_qn%d" % i, [128, TT], F32) for i in range(3)]
            qnb = [self.sb(es, "e_qnb%d" % i, [128, TT], BF16) for i in range(3)]
            t1 = [self.sb(es, "e_t1%d" % i, [128, TT], F32) for i in range(3)]
            t2 = [self.sb(es, "e_t2%d" % i, [128, TT], F32) for i in range(3)]
            tb = {k: [Buf(), Buf(), Buf()] for k in ("sq", "sd", "qn", "qnb", "t1", "t2")}
            misc, miscb = g["misc"]
            ones128, onesb = g["ones128"]
            rotE, rotEb = g["rotE"]
            cnt = {"mm": 0, "ep": 0}

            def mm_tile(wbf, wbfb, tt):
                pi = cnt["mm"] % 4
                cnt["mm"] += 1
                fns = [(lambda h, kc=kc: h.matmul(self.ps[pi][:, 0:TT], wbf[:, kc, 0:128], xT[:, kc, tt * TT:(tt + 1) * TT],
                                                  start=(kc == 0), stop=(kc == KC - 1))) for kc in range(KC)]
                kb.group("pe", fns, reads=[wbfb] + xTb[4 * tt:4 * tt + 4], writes=[self.psb[pi]])
                return pi

            def chunk_simple(fo, kind, dst_ap):
                def run(wbf, wbfb):
                    run_due(flush=True)
                    sl = fo % 2
                    for tt in range(NTT):
                        pi = mm_tile(wbf, wbfb, tt)
                        if kind in ("u", "ag"):
                            outv = stg[sl][:].rearrange("p (j m) -> p m j", j=8)[:, 64 * tt:64 * tt + 64, :]
                            inv = self.ps[pi][:, 0:TT].rearrange("p (m j) -> p m j", j=8)
                        else:
                            outv = stg[sl][:, tt * TT:(tt + 1) * TT]
                            inv = self.ps[pi][:, 0:TT]
                        fn = AF.Copy if kind == "u" else AF.Silu
                        kb.op("act", lambda h: h.activation(out=outv, in_=inv, func=fn), reads=[self.psb[pi]], writes=[stgb[sl]])
                    kb.dma(dst_ap, stg[sl][:], reads=[stgb[sl]], writes=[self.dbuf(("E", s, fo))])
                return run

            pend = []
            clock = {"k": 0}

            def run_due(flush=False):
                while pend and (flush or pend[0][0] <= clock["k"]):
                    pend.pop(0)[1]()

            def chunk_qk(fo, gname, dst_ap):
                gt, gtb = g[gname]

                def run(wbf, wbfb):
                    sl = fo % 2
                    for tt in range(NTT):
                        k = clock["k"]
                        pi = mm_tile(wbf, wbfb, tt)
                        e = k % 3
                        pm, pr = 4, 5
                        kb.op("act", lambda h: h.activation(out=sqb[e][:], in_=self.ps[pi][:, 0:TT], func=AF.Square),
                              reads=[self.psb[pi]], writes=[tb["sq"][e]])

                        def stage_b(pi=pi, e=e):
                            kb.group("pe", [lambda h: h.matmul(self.ps[pm][:, 0:TT], ones128[:], sqb[e][:], start=True, stop=True)],
                                     reads=[tb["sq"][e], onesb], writes=[self.psb[pm]])
                            kb.op("act", lambda h: h.activation(out=sd[e][:], in_=self.ps[pm][:, 0:TT], func=AF.Ln, bias=misc[:, 0:1]),
                                  reads=[self.psb[pm], miscb], writes=[tb["sd"][e]])
                            kb.op("act", lambda h: h.activation(out=sd[e][:], in_=sd[e][:], func=AF.Exp, scale=-0.5), reads=[tb["sd"][e]], writes=[tb["sd"][e]])
                            kb.op("dve", lambda h: h.scalar_tensor_tensor(out=qn[e][:], in0=self.ps[pi][:, 0:TT], scalar=gt[:, 0:1], in1=sd[e][:],
                                                                           op0=ALU.mult, op1=ALU.mult),
                                  reads=[self.psb[pi], tb["sd"][e], gtb], writes=[tb["qn"][e]])
                            kb.op("act", lambda h: h.copy(qnb[e][:], qn[e][:]), reads=[tb["qn"][e]], writes=[tb["qnb"][e]])

                        def stage_c(e=e, tt=tt, sl=sl):
                            kb.group("pe", [lambda h: h.matmul(self.ps[pr][:, 0:TT], rotE[:], qnb[e][:], start=True, stop=True)],
                                     reads=[tb["qnb"][e], rotEb], writes=[self.psb[pr]])
                            kb.op("dve", lambda h: h.tensor_tensor(out=t1[e][:], in0=qn[e][:], in1=cs[:, 0, tt * TT:(tt + 1) * TT], op=ALU.mult),
                                  reads=[tb["qn"][e], csb], writes=[tb["t1"][e]])
                            kb.op("dve", lambda h: h.tensor_tensor(out=t2[e][:], in0=self.ps[pr][:, 0:TT], in1=cs[:, 1, tt * TT:(tt + 1) * TT], op=ALU.mult),
                                  reads=[self.psb[pr], csb], writes=[tb["t2"][e]])
                            kb.op("dve", lambda h: h.tensor_tensor(out=stg[sl][:, tt * TT:(tt + 1) * TT], in0=t1[e][:], in1=t2[e][:], op=ALU.add),
                                  reads=[tb["t1"][e], tb["t2"][e]], writes=[stgb[sl]])
                        pend.append((k + 1, stage_b))
                        pend.append((k + 2, stage_c))
                        if tt == NTT - 1:
                            pend.append((k + 2, lambda sl=sl: kb.dma(dst_ap, stg[sl][:], reads=[stgb[sl]], writes=[self.dbuf(("E", s, fo))])))
                        pend.sort(key=lambda t: t[0])
                        clock["k"] += 1
                        run_due()
                return run

            def chunk_v(half):
                def run(wbf, wbfb):
                    run_due(flush=True)
                    for b in range(NB):
                        pi = cnt["mm"] % 4
                        cnt["mm"] += 1
                        fns = [(lambda h, kc=kc: h.matmul(self.ps[pi][:, 0:128], xT[:, kc, b * 128:(b + 1) * 128], wbf[:, kc, 0:128],
                                                          start=(kc == 0), stop=(kc == KC - 1))) for kc in range(KC)]
                        kb.group("pe", fns, reads=[wbfb, xTb[b]], writes=[self.psb[pi]])
                        kb.op("act", lambda h: h.copy(vst[:, b, half * 128:(half + 1) * 128], self.ps[pi][:, 0:128]), reads=[self.psb[pi]], writes=[vstb])
                    if half == 1:
                        kb.dma(self.VV[s].rearrange("(b p) e -> p b e", p=128), vst[:], reads=[vstb], writes=[self.dbuf(("E", s, "v"))])
                return run

            specs = []
            for fo in range(36):
                c0 = fo * 128
                if fo < 8:
                    specs.append((c0, 128, chunk_simple(fo, "u", self.UT[s, fo * 128:(fo + 1) * 128, :])))
                elif fo < 16:
                    specs.append((c0, 128, chunk_simple(fo, "ag", self.AG[s, (fo - 8) * 128:(fo - 7) * 128, :])))
                elif fo < 24:
                    specs.append((c0, 128, chunk_qk(fo, "gq", self.QT[s, fo - 16])))
                elif fo < 26:
                    specs.append((c0, 128, chunk_qk(fo, "gk", self.KT[s, fo - 24])))
                elif fo < 28:
                    specs.append((c0, 128, chunk_v(fo - 26)))
                else:
                    specs.append((c0, 128, chunk_simple(fo, "bg", self.BG[s, (fo - 28) * 128:(fo - 27) * 128, :])))
            self.linear(es, self.w_in_e, EVEN_IN, xT, xTb, specs, "e_", maxw=128)
            run_due(flush=True)
        kb.barrier()

    def sview(self, tile, part0, nparts, off, dims):
        base = tile[:]
        pst = base.ap[0][0]
        return AP(base.tensor, base.offset + part0 * pst + off, [[pst, nparts]] + [list(d) for d in dims])

    def phase_ssm_pre(self):
        kb, nc = self.kb, self.nc
        g = self.gtiles
        identf, identfb = g["identf"]
        misc, miscb = g["misc"]
        MAGIC = 12582912.0
        TWO_PI = 2.0 * math.pi
        with contextlib.ExitStack() as es:
            def tl(name, shape, dt=F32):
                return self.sb(es, "p_" + name, shape, dt), Buf(name)
            Are, Areb = tl("Are", [128, G]); Aim, Aimb = tl("Aim", [128, G]); LDT, LDTb = tl("LDT", [128, G])
            Bre, Breb = tl("Bre", [128, G, 16]); Bim, Bimb = tl("Bim", [128, G, 16])
            Cre, Creb = tl("Cre", [128, G, 16]); Cim, Cimb = tl("Cim", [128, G, 16])
            Dsk, Dskb = tl("Dsk", [128, G])
            mmk, mmkb = tl("mmk", [128, 2, 128])
            kb.dma(mmk[:], self.c_mmask.rearrange("a p n -> p a n"), writes=[mmkb])
            for d in range(2):
                kb.dma(Are[64 * d:64 * d + 64, :], dap(self.a_re, d * G * 64, [[1, 64], [64, G]]), writes=[Areb], slow=True)
                kb.dma(Aim[64 * d:64 * d + 64, :], dap(self.a_im, d * G * 64, [[1, 64], [64, G]]), writes=[Aimb], slow=True)
                kb.dma(LDT[64 * d:64 * d + 64, :], dap(self.log_dt, d * G, [[0, 64], [1, G]]), writes=[LDTb])
                kb.dma(Bre[64 * d:64 * d + 64, :, :], dap(self.b_re, 0, [[16, 64], [1024, G], [1, 16]]), writes=[Breb])
                kb.dma(Bim[64 * d:64 * d + 64, :, :], dap(self.b_im, 0, [[16, 64], [1024, G], [1, 16]]), writes=[Bimb])
            for j in range(8):
                kb.dma(Dsk[16 * j:16 * j + 16, :], dap(self.d_skip, 0, [[1, 16], [16, G]]), writes=[Dskb], slow=True)
            ct = [tl("ct%d" % i, [128, 128]) for i in range(2)]
            n = 0
            for src, dst, dstb in ((self.c_re, Cre, Creb), (self.c_im, Cim, Cimb)):
                for blk in range(8):
                    c_t, c_b = ct[n % 2]
                    pi = n % 2
                    n += 1
                    kb.dma(c_t[:], dap(src, blk * 8 * 16 * 64, [[64, 128], [G * 16 * 64, 2], [1, 64]]), writes=[c_b])
                    kb.group("pe", [lambda h: h.transpose(self.ps[pi][:, 0:128], c_t[:], identf[:])], reads=[c_b, identfb], writes=[self.psb[pi]])
                    kb.op("act", lambda h: h.copy(dst[:, blk * 8:(blk + 1) * 8, :].rearrange("p a c -> p (a c)"), self.ps[pi][:, 0:128]),
                          reads=[self.psb[pi]], writes=[dstb])
            dt_, dtb = tl("dt", [128, G]); ard, ardb = tl("ard", [128, G]); aid, aidb = tl("aid", [128, G])
            kb.op("act", lambda h: h.activation(out=dt_[:], in_=LDT[:], func=AF.Exp), reads=[LDTb], writes=[dtb])
            kb.op("dve", lambda h: h.tensor_tensor(out=ard[:], in0=Are[:], in1=dt_[:], op=ALU.mult), reads=[Areb, dtb], writes=[ardb])
            kb.op("dve", lambda h: h.tensor_tensor(out=aid[:], in0=Aim[:], in1=dt_[:], op=ALU.mult), reads=[Aimb, dtb], writes=[aidb])
            EAr, EArb = tl("EAr", [128, 16, G]); EAi, EAib = tl("EAi", [128, 16, G])
            mag, magb = tl("mag", [128, 16, G]); ang, angb = tl("ang", [128, 16, G]); rr, rrb = tl("rr", [128, 16, G])
            sn, snb = tl("sn", [128, 16, G]); cs_, csb_ = tl("cs", [128, 16, G])
            for k in range(-7, 9):
                i = k + 7
                kb.op("act", lambda h: h.activation(out=mag[:, i, :], in_=ard[:], func=AF.Exp, scale=float(k)), reads=[ardb], writes=[magb])
                kb.op("dve", lambda h: h.tensor_scalar(out=rr[:, i, :], in0=aid[:], scalar1=float(k) / TWO_PI, scalar2=MAGIC, op0=ALU.mult, op1=ALU.add),
                      reads=[aidb], writes=[rrb])
                kb.op("dve", lambda h: h.tensor_scalar(out=rr[:, i, :], in0=rr[:, i, :], scalar1=-MAGIC, scalar2=-TWO_PI, op0=ALU.add, op1=ALU.mult),
                      reads=[rrb], writes=[rrb])
                kb.op("dve", lambda h: h.scalar_tensor_tensor(out=ang[:, i, :], in0=aid[:], scalar=float(k), in1=rr[:, i, :], op0=ALU.mult, op1=ALU.add),
                      reads=[aidb, rrb], writes=[angb])
            kb.op("dve", lambda h: h.tensor_scalar(out=ang[:], in0=ang[:], scalar1=math.pi, scalar2=-math.pi, op0=ALU.min, op1=ALU.max),
                  reads=[angb], writes=[angb])
            kb.op("act", lambda h: h.activation(out=sn[:], in_=ang[:], func=AF.Sin), reads=[angb], writes=[snb])
            kb.op("act", lambda h: h.activation(out=rr[:], in_=ang[:], func=AF.Abs), reads=[angb], writes=[rrb])
            kb.op("act", lambda h: h.activation(out=cs_[:], in_=rr[:], func=AF.Sin, scale=-1.0, bias=misc[:, 3:4]), reads=[rrb, miscb], writes=[csb_])
            kb.op("dve", lambda h: h.tensor_tensor(out=EAr[:], in0=mag[:], in1=cs_[:], op=ALU.mult), reads=[magb, csb_], writes=[EArb])
            kb.op("dve", lambda h: h.tensor_tensor(out=EAi[:], in0=mag[:], in1=sn[:], op=ALU.mult), reads=[magb, snb], writes=[EAib])
            mu_t, mu_b = g["mu"]
            kb.op("dve", lambda h: h.tensor_copy(mu_t[:, 0, :], EAr[:, 15, :]), reads=[EArb], writes=[mu_b])
            kb.op("dve", lambda h: h.tensor_copy(mu_t[:, 1, :], EAi[:, 15, :]), reads=[EAib], writes=[mu_b])
            mup_t, mup_b = g["mup"]
            pw1, pw1b = tl("pw1", [128, G]); pw2, pw2b = tl("pw2", [128, G])
            kb.op("dve", lambda h: h.tensor_copy(mup_t[:, 0, :, 0], EAr[:, 15, :]), reads=[EArb], writes=[mup_b])
            kb.op("dve", lambda h: h.tensor_copy(mup_t[:, 1, :, 0], EAi[:, 15, :]), reads=[EAib], writes=[mup_b])
            for k in range(1, 16):
                pr, pi_ = mup_t[:, 0, :, k - 1], mup_t[:, 1, :, k - 1]
                kb.op("dve", lambda h: h.tensor_tensor(out=pw1[:], in0=pr, in1=mu_t[:, 0, :], op=ALU.mult), reads=[mup_b, mu_b], writes=[pw1b])
                kb.op("dve", lambda h: h.tensor_tensor(out=pw2[:], in0=pi_, in1=mu_t[:, 1, :], op=ALU.mult), reads=[mup_b, mu_b], writes=[pw2b])
                kb.op("dve", lambda h: h.tensor_tensor(out=mup_t[:, 0, :, k], in0=pw1[:], in1=pw2[:], op=ALU.subtract), reads=[pw1b, pw2b], writes=[mup_b])
                kb.op("dve", lambda h: h.tensor_tensor(out=pw1[:], in0=pr, in1=mu_t[:, 1, :], op=ALU.mult), reads=[mup_b, mu_b], writes=[pw1b])
                kb.op("dve", lambda h: h.tensor_tensor(out=pw2[:], in0=pi_, in1=mu_t[:, 0, :], op=ALU.mult), reads=[mup_b, mu_b], writes=[pw2b])
                kb.op("dve", lambda h: h.tensor_tensor(out=mup_t[:, 1, :, k], in0=pw1[:], in1=pw2[:], op=ALU.add), reads=[pw1b, pw2b], writes=[mup_b])
            nr, nrb = tl("nr", [128, G]); den, denb = tl("den", [128, G]); tq, tqb = tl("tq", [128, G])
            fre, freb = tl("fre", [128, G]); fim, fimb = tl("fim", [128, G])
            kb.op("dve", lambda h: h.tensor_scalar_add(out=nr[:], in0=EAr[:, 8, :], scalar1=-1.0), reads=[EArb], writes=[nrb])
            kb.op("dve", lambda h: h.tensor_tensor(out=den[:], in0=Are[:], in1=Are[:], op=ALU.mult), reads=[Areb], writes=[denb])
            kb.op("dve", lambda h: h.tensor_tensor(out=tq[:], in0=Aim[:], in1=Aim[:], op=ALU.mult), reads=[Aimb], writes=[tqb])
            kb.op("dve", lambda h: h.tensor_tensor(out=den[:], in0=den[:], in1=tq[:], op=ALU.add), reads=[denb, tqb], writes=[denb])
            kb.op("dve", lambda h: h.reciprocal(den[:], den[:]), reads=[denb], writes=[denb])
            kb.op("dve", lambda h: h.tensor_tensor(out=fre[:], in0=nr[:], in1=Are[:], op=ALU.mult), reads=[nrb, Areb], writes=[freb])
            kb.op("dve", lambda h: h.tensor_tensor(out=tq[:], in0=EAi[:, 8, :], in1=Aim[:], op=ALU.mult), reads=[EAib, Aimb], writes=[tqb])
            kb.op("dve", lambda h: h.tensor_tensor(out=fre[:], in0=fre[:], in1=tq[:], op=ALU.add), reads=[freb, tqb], writes=[freb])
            kb.op("dve", lambda h: h.tensor_tensor(out=fre[:], in0=fre[:], in1=den[:], op=ALU.mult), reads=[freb, denb], writes=[freb])
            kb.op("dve", lambda h: h.tensor_tensor(out=fim[:], in0=EAi[:, 8, :], in1=Are[:], op=ALU.mult), reads=[EAib, Areb], writes=[fimb])
            kb.op("dve", lambda h: h.tensor_tensor(out=tq[:], in0=nr[:], in1=Aim[:], op=ALU.mult), reads=[nrb, Aimb], writes=[tqb])
            kb.op("dve", lambda h: h.tensor_tensor(out=fim[:], in0=fim[:], in1=tq[:], op=ALU.subtract), reads=[fimb, tqb], writes=[fimb])
            kb.op("dve", lambda h: h.tensor_tensor(out=fim[:], in0=fim[:], in1=den[:], op=ALU.mult), reads=[fimb, denb], writes=[fimb])
            Bbr, Bbrb = tl("Bbr", [128, G, 16]); Bbi, Bbib = tl("Bbi", [128, G, 16]); tb1, tb1b = tl("tb1", [128, G, 16])
            frb = fre[:].unsqueeze(2).broadcast_to([128, G, 16])
            fib = fim[:].unsqueeze(2).broadcast_to([128, G, 16])
            kb.op("dve", lambda h: h.tensor_tensor(out=Bbr[:], in0=Bre[:], in1=frb, op=ALU.mult), reads=[Breb, freb], writes=[Bbrb])
            kb.op("dve", lambda h: h.tensor_tensor(out=tb1[:], in0=Bim[:], in1=fib, op=ALU.mult), reads=[Bimb, fimb], writes=[tb1b])
            kb.op("dve", lambda h: h.tensor_tensor(out=Bbr[:], in0=Bbr[:], in1=tb1[:], op=ALU.subtract), reads=[Bbrb, tb1b], writes=[Bbrb])
            kb.op("dve", lambda h: h.tensor_tensor(out=Bbi[:], in0=Bim[:], in1=frb, op=ALU.mult), reads=[Bimb, freb], writes=[Bbib])
            kb.op("dve", lambda h: h.tensor_tensor(out=tb1[:], in0=Bre[:], in1=fib, op=ALU.mult), reads=[Breb, fimb], writes=[tb1b])
            kb.op("dve", lambda h: h.tensor_tensor(out=Bbi[:], in0=Bbi[:], in1=tb1[:], op=ALU.add), reads=[Bbib, tb1b], writes=[Bbib])
            fam = {}
            for nm in ("BTr", "BTi", "COr", "COi", "P1r", "P1i", "P2rF", "P2iF", "P2rB", "P2iB"):
                fam[nm] = tl(nm, [128, GBP, 128])
            for nm in ("P2rF", "P2iF", "P2rB", "P2iB"):
                kb.op("dve", lambda h: h.memset(fam[nm][0][:], 0.0), writes=[fam[nm][1]])
            tmp = [tl("tmp%d" % i, [128, GBP, 128]) for i in range(2)]
            mgs, mgsb = tl("mgs", [128, GBP, 128], BF16)
            bins, binsb = tl("bins", [128, GBP, 2, 128], BF16)
            cos_, cosb_ = tl("cos", [128, GBP, 4, 128], BF16)
            kb.op("dve", lambda h: h.memset(cos_[:], 0.0), writes=[cosb_])
            m1, m1b = tl("m1", [128, 128]); m2, m2b = tl("m2", [128, 128])

            def cprod(part, g0, k0, ks, Yr, Yrb, Yi, Yib, outr, outi, neg_im):
                en = "dve"
                o_r, o_rb = fam[outr]
                o_i, o_ib = fam[outi]
                (ta, tab), (tc, tcb) = tmp
                dims_e = [[1, GBP], [ks * G, 8], [0, 16]]
                Xr = self.sview(EAr, part, 64, (k0 + 7) * G + g0, dims_e)
                Xi = self.sview(EAi, part, 64, (k0 + 7) * G + g0, dims_e)
                dims_y = [[16, GBP], [0, 8], [1, 16]]
                yr = self.sview(Yr, part, 64, g0 * 16, dims_y)
                yi = self.sview(Yi, part, 64, g0 * 16, dims_y)
                d4 = [[128, GBP], [16, 8], [1, 16]]
                v = lambda t: self.sview(t, part, 64, 0, d4)
                kb.op(en, lambda h: h.tensor_tensor(out=v(ta), in0=Xr, in1=yr, op=ALU.mult), reads=[EArb, Yrb], writes=[tab])
                kb.op(en, lambda h: h.tensor_tensor(out=v(tc), in0=Xi, in1=yi, op=ALU.mult), reads=[EAib, Yib], writes=[tcb])
                kb.op(en, lambda h: h.tensor_tensor(out=v(o_r), in0=v(ta), in1=v(tc), op=ALU.subtract), reads=[tab, tcb], writes=[o_rb])
                kb.op(en, lambda h: h.tensor_tensor(out=v(ta), in0=Xr, in1=yi, op=ALU.mult), reads=[EArb, Yib], writes=[tab])
                kb.op(en, lambda h: h.tensor_tensor(out=v(tc), in0=Xi, in1=yr, op=ALU.mult), reads=[EAib, Yrb], writes=[tcb])
                kb.op(en, lambda h: h.tensor_tensor(out=v(o_i), in0=v(ta), in1=v(tc), op=ALU.add), reads=[tab, tcb], writes=[o_ib])
                if neg_im:
                    kb.op(en, lambda h: h.tensor_scalar_mul(out=v(o_i), in0=v(o_i), scalar1=-1.0), reads=[o_ib], writes=[o_ib])

            for bi in range(G // GBP):
                g0 = bi * GBP
                for part, d in ((0, 0), (64, 1)):
                    kB = (7, -1) if d == 0 else (0, 1)
                    kC = (1, 1) if d == 0 else (8, -1)
                    k1 = (0, 1) if d == 0 else (0, -1)
                    k2 = (0, -1) if d == 0 else (0, 1)
                    sfx = "F" if d == 0 else "B"
                    cprod(part, g0, kB[0], kB[1], Bbr, Bbrb, Bbi, Bbib, "BTr", "BTi", False)
                    cprod(part, g0, kC[0], kC[1], Cre, Creb, Cim, Cimb, "COr", "COi", True)
                    cprod(part, g0, k1[0], k1[1], Cre, Creb, Cim, Cimb, "P1r", "P1i", False)
                    cprod(part, g0, k2[0], k2[1], Bbr, Bbrb, Bbi, Bbib, "P2r" + sfx, "P2i" + sfx, True)
                BTr, BTrb = fam["BTr"]; BTi, BTib = fam["BTi"]
                P1r, P1rb = fam["P1r"]; P1i, P1ib = fam["P1i"]
                COr, COrb = fam["COr"]; COi, COib = fam["COi"]
                kb.op("act", lambda h: h.copy(cos_[0:64, :, 0, :], COr[0:64]), reads=[COrb], writes=[cosb_])
                kb.op("act", lambda h: h.copy(cos_[0:64, :, 1, :], COi[0:64]), reads=[COib], writes=[cosb_])
                kb.op("act", lambda h: h.copy(cos_[64:128, :, 2, :], COr[64:128]), reads=[COrb], writes=[cosb_])
                kb.op("act", lambda h: h.copy(cos_[64:128, :, 3, :], COi[64:128]), reads=[COib], writes=[cosb_])
                kb.dma(self.COUT[g0:g0 + GBP].rearrange("g p r n -> p g r n"), cos_[:], reads=[cosb_], writes=[self.dbuf("COUT")])
                for gi in range(GBP):
                    pa, pb = 2 + (gi % 2), 4 + (gi % 2)
                    kb.group("pe", [lambda h: h.transpose(self.ps[pa][:, 0:128], BTr[:, gi, :], identf[:]),
                                    lambda h: h.transpose(self.ps[pa][:, 128:256], BTi[:, gi, :], identf[:])],
                             reads=[BTrb, BTib, identfb], writes=[self.psb[pa]])
                    kb.op("act", lambda h: h.copy(bins[:, gi, :, :], self.ps[pa][:, 0:256].rearrange("p (r n) -> p r n", r=2)),
                          reads=[self.psb[pa]], writes=[binsb])
                    fns = []
                    rds = [P1rb, P1ib]
                    for d, sfx in ((0, "F"), (1, "B")):
                        p2r, p2rb = fam["P2r" + sfx]
                        p2i, p2ib = fam["P2i" + sfx]
                        rds += [p2rb, p2ib]
                        fns.append(lambda h, d=d, p2r=p2r: h.matmul(self.ps[pb][:, 128 * d:128 * d + 128], p2r[:, gi, :], P1r[:, gi, :], start=True, stop=False))
                        fns.append(lambda h, d=d, p2i=p2i: h.matmul(self.ps[pb][:, 128 * d:128 * d + 128], p2i[:, gi, :], P1i[:, gi, :], start=False, stop=True))
                    kb.group("pe", fns, reads=rds, writes=[self.psb[pb]])
                    kb.op("dve", lambda h: h.tensor_tensor(out=m1[:], in0=self.ps[pb][:, 0:128], in1=mmk[:, 0, :], op=ALU.mult),
                          reads=[self.psb[pb], mmkb], writes=[m1b])
                    kb.op("dve", lambda h: h.tensor_tensor(out=m2[:], in0=self.ps[pb][:, 128:256], in1=mmk[:, 1, :], op=ALU.mult),
                          reads=[self.psb[pb], mmkb], writes=[m2b])
                    kb.op("dve", lambda h: h.tensor_tensor(out=m1[:], in0=m1[:], in1=m2[:], op=ALU.add), reads=[m1b, m2b], writes=[m1b])
                    kb.op("dve", lambda h: h.scalar_tensor_tensor(out=mgs[:, gi, :], in0=identf[:], scalar=Dsk[:, g0 + gi:g0 + gi + 1], in1=m1[:],
                                                                   op0=ALU.mult, op1=ALU.add), reads=[identfb, Dskb, m1b], writes=[mgsb])
                kb.dma(self.BIN[g0:g0 + GBP].rearrange("g p r n -> p g r n"), bins[:], reads=[binsb], writes=[self.dbuf("BIN")])
                kb.dma(self.MG[g0:g0 + GBP].rearrange("g p n -> p g n"), mgs[:], reads=[mgsb], writes=[self.dbuf("MG")])
        kb.barrier()

    def phase_ssm(self):
        kb = self.kb
        Tn, NCH = self.T, self.NCH
        g = self.gtiles
        mu, mub = g["mu"]
        fl, flb = g["flag"]
        NX = NCH + 1
        with contextlib.ExitStack() as es:
            def tl(name, shape, dt=F32):
                return self.sb(es, "s_" + name, shape, dt), Buf(name)
            mg, mgb = tl("mg", [128, GB, 128], BF16)
            bn, bnb = tl("bin", [128, GB, 2, 128], BF16)
            co, cob = tl("cout", [128, GB, 4, 128], BF16)
            U8 = [tl("u8_%d" % s, [128, GB, NCH], BF16) for s in range(NSEG)]
            XS, XSb = tl("XS", [128, 2, GB, NX])
            XP = {k: tl("XP" + k, [128, 2, GB, NX], BF16) for k in "ABC"}
            MUA, MUAb = tl("MUA", [128, 2, GB])
            MUB, MUBb = tl("MUB", [128, 2, GB])
            T1, T1b = tl("T1", [128, 2, GB])
            T2, T2b = tl("T2", [128, 2, GB])
            FIN, FINb = tl("FIN", [128, 2, GB])
            LB = 16
            NBK = NCH // LB
            W1, W1b = tl("W1", [128, 2, GB, NBK])
            W2, W2b = tl("W2", [128, 2, GB, NBK])
            W3, W3b = tl("W3", [128, GB, NBK, LB - 1])
            MLA, MLAb = tl("MLA", [128, 2, GB])
            MLB, MLBb = tl("MLB", [128, 2, GB])
            mup, mupb = g["mup"]
            y8 = [tl("y8_%d" % i, [128, NCH], BF16) for i in range(2)]
            sq = [tl("sq%d" % i, [128, NCH]) for i in range(2)]
            uu = [tl("uu%d" % i, [128, NCH]) for i in range(2)]
            cnt = {"ps": 0, "ev": 0}

            def xcol(t, k):
                return t[:, :, :, k]

            def run_pass(sf, sbw, init_from, key, g0):
                xp, xpb = XP[key]
                for gi in range(GB):
                    pi = cnt["ps"] % 3
                    cnt["ps"] += 1
                    uf, ufb = U8[sf]
                    ub, ubb = U8[sbw]
                    fns = []
                    for r in range(2):
                        fns.append(lambda h, r=r: h.matmul(self.ps[pi][0:64, r * NCH:(r + 1) * NCH], bn[:, gi, r, 0:64], uf[:, gi, :], start=True, stop=True))
                        fns.append(lambda h, r=r: h.matmul(self.ps[pi][64:128, r * NCH:(r + 1) * NCH], bn[:, gi, r, 64:128], rev_last(ub[:, gi, :], NCH), start=True, stop=True))
                    kb.group("pe", fns, reads=[bnb, ufb, ubb], writes=[self.psb[pi]])
                    kb.op("act", lambda h: h.copy(XS[:, :, gi, 1:NX], self.ps[pi][:, 0:2 * NCH].rearrange("p (r n) -> p r n", r=2)),
                          reads=[self.psb[pi]], writes=[XSb])
                if init_from is None:
                    kb.op("dve", lambda h: h.memset(xcol(XS, 0), 0.0), writes=[XSb])
                else:
                    kb.op("dve", lambda h: h.tensor_scalar_mul(out=xcol(XS, 0), in0=FIN[:], scalar1=fl[:, 0:1]), reads=[FINb, flb], writes=[XSb])
                base = XS[:]
                pst = base.ap[0][0]
                RS = GB * NX

                def xv(off, dims):
                    return AP(base.tensor, base.offset + off, [[pst, 128]] + [list(d) for d in dims])
                mua_b = AP(MUA[:].tensor, MUA[:].offset, [[MUA[:].ap[0][0], 128], [GB, 2], [1, GB], [0, NBK]])
                mub_b = AP(MUB[:].tensor, MUB[:].offset, [[MUB[:].ap[0][0], 128], [GB, 2], [1, GB], [0, NBK]])
                for j in range(1, LB):
                    prev = xv(j, [[RS, 2], [NX, GB], [LB, NBK]])
                    prev_sw = xv(RS + j, [[-RS, 2], [NX, GB], [LB, NBK]])
                    cur = xv(1 + j, [[RS, 2], [NX, GB], [LB, NBK]])
                    kb.op("dve", lambda h: h.tensor_tensor(out=W1[:], in0=mua_b, in1=prev, op=ALU.mult), reads=[MUAb, XSb], writes=[W1b])
                    kb.op("dve", lambda h: h.tensor_tensor(out=W2[:], in0=mub_b, in1=prev_sw, op=ALU.mult), reads=[MUBb, XSb], writes=[W2b])
                    kb.op("dve", lambda h: h.tensor_tensor(out=W1[:], in0=W1[:], in1=W2[:], op=ALU.add), reads=[W1b, W2b], writes=[W1b])
                    kb.op("dve", lambda h: h.tensor_tensor(out=cur, in0=cur, in1=W1[:], op=ALU.add), reads=[W1b, XSb], writes=[XSb])
                for b in range(NBK):
                    pc = xv(b * LB, [[RS, 2], [NX, GB]])
                    pc_sw = xv(RS + b * LB, [[-RS, 2], [NX, GB]])
                    cur = xv((b + 1) * LB, [[RS, 2], [NX, GB]])
                    kb.op("dve", lambda h: h.tensor_tensor(out=T1[:], in0=MLA[:], in1=pc, op=ALU.mult), reads=[MLAb, XSb], writes=[T1b])
                    kb.op("dve", lambda h: h.tensor_tensor(out=T2[:], in0=MLB[:], in1=pc_sw, op=ALU.mult), reads=[MLBb, XSb], writes=[T2b])
                    kb.op("dve", lambda h: h.tensor_tensor(out=T1[:], in0=T1[:], in1=T2[:], op=ALU.add), reads=[T1b, T2b], writes=[T1b])
                    kb.op("dve", lambda h: h.tensor_tensor(out=cur, in0=cur, in1=T1[:], op=ALU.add), reads=[T1b, XSb], writes=[XSb])
                mp = mup[:]
                mps = mp.ap[0][0]

                def pv(r):
                    return AP(mp.tensor, mp.offset + (r * G + g0) * 16, [[mps, 128], [16, GB], [0, NBK], [1, LB - 1]])
                Cr = xv(0, [[NX, GB], [LB, NBK], [0, LB - 1]])
                Ci = xv(RS, [[NX, GB], [LB, NBK], [0, LB - 1]])
                Xr = xv(1, [[NX, GB], [LB, NBK], [1, LB - 1]])
                Xi = xv(RS + 1, [[NX, GB], [LB, NBK], [1, LB - 1]])
                for (pa, ca, tgt, op) in ((pv(0), Cr, Xr, ALU.add), (pv(1), Ci, Xr, ALU.subtract), (pv(0), Ci, Xi, ALU.add), (pv(1), Cr, Xi, ALU.add)):
                    kb.op("dve", lambda h: h.tensor_tensor(out=W3[:], in0=pa, in1=ca, op=ALU.mult), reads=[mupb, XSb], writes=[W3b])
                    kb.op("dve", lambda h: h.tensor_tensor(out=tgt, in0=tgt, in1=W3[:], op=op), reads=[W3b, XSb], writes=[XSb])
                kb.op("dve", lambda h: h.tensor_copy(FIN[:], xcol(XS, NCH)), reads=[XSb], writes=[FINb])
                kb.op("act", lambda h: h.copy(xp[:], XS[:]), reads=[XSb], writes=[xpb])

            def emit_y(s, kf, kbk, g0):
                xf, xfb = XP[kf]
                xb, xbb = XP[kbk]
                us, usb = U8[s]
                for gi in range(GB):
                    pi = cnt["ps"] % 3
                    cnt["ps"] += 1
                    e = cnt["ev"] % 2
                    cnt["ev"] += 1
                    o = self.ps[pi][:, 0:NCH]
                    fns = [lambda h: h.matmul(o, mg[:, gi, :], us[:, gi, :], start=True, stop=False),
                           lambda h: h.matmul(o, co[:, gi, 0, :], xf[:, 0, gi, 0:NCH], start=False, stop=False),
                           lambda h: h.matmul(o, co[:, gi, 1, :], xf[:, 1, gi, 0:NCH], start=False, stop=False),
                           lambda h: h.matmul(o, co[:, gi, 2, :], rev_last(xb[:, 0, gi, 0:NCH], NCH), start=False, stop=False),
                           lambda h: h.matmul(o, co[:, gi, 3, :], rev_last(xb[:, 1, gi, 0:NCH], NCH), start=False, stop=True)]
                    kb.group("pe", fns, reads=[mgb, cob, usb, xfb, xbb], writes=[self.psb[pi]])
                    (sq_t, sq_b), (uu_t, uu_b), (y_t, y_b) = sq[e], uu[e], y8[e]
                    kb.op("act", lambda h: h.activation(out=sq_t[:], in_=o, func=AF.Square), reads=[self.psb[pi]], writes=[sq_b])
                    kb.op("dve", lambda h: h.tensor_scalar(out=sq_t[:], in0=sq_t[:], scalar1=0.044715, scalar2=1.0, op0=ALU.mult, op1=ALU.add),
                          reads=[sq_b], writes=[sq_b])
                    kb.op("dve", lambda h: h.tensor_tensor(out=uu_t[:], in0=sq_t[:], in1=o, op=ALU.mult), reads=[sq_b, self.psb[pi]], writes=[uu_b])
                    kb.op("act", lambda h: h.activation(out=uu_t[:], in_=uu_t[:], func=AF.Sigmoid, scale=1.5957691216057308), reads=[uu_b], writes=[uu_b])
                    kb.op("dve", lambda h: h.tensor_tensor(out=y_t[:], in0=uu_t[:], in1=o, op=ALU.mult), reads=[uu_b, self.psb[pi]], writes=[y_b])
                    dst = dap(self.YT, s * 1024 * Tn + (g0 + gi) * 16 * Tn, [[NCH, 8], [Tn, 16], [1, NCH]])
                    kb.dma(dst, y_t[:], reads=[y_b], writes=[self.dbuf(("YT", s))])

            for bi in range(G // GB):
                g0 = bi * GB
                kb.dma(mg[:], self.MG[g0:g0 + GB].rearrange("g p n -> p g n"), reads=[self.dbuf("MG")], writes=[mgb])
                kb.dma(bn[:], self.BIN[g0:g0 + GB].rearrange("g p r n -> p g r n"), reads=[self.dbuf("BIN")], writes=[bnb])
                kb.dma(co[:], self.COUT[g0:g0 + GB].rearrange("g p r n -> p g r n"), reads=[self.dbuf("COUT")], writes=[cob])
                for s in range(NSEG):
                    ut, utb = U8[s]
                    for gi in range(GB):
                        src = dap(self.UT, s * 1024 * Tn + (g0 + gi) * 16 * Tn, [[NCH, 8], [Tn, 16], [1, NCH]])
                        kb.dma(ut[:, gi, :], src, reads=[self.dbuf(("E", s, (g0 + gi) // 8))], writes=[utb])
                kb.op("pool", lambda h: h.tensor_copy(MUA[:, 0, :], mu[:, 0, g0:g0 + GB]), reads=[mub], writes=[MUAb])
                kb.op("pool", lambda h: h.tensor_copy(MUA[:, 1, :], mu[:, 0, g0:g0 + GB]), reads=[mub], writes=[MUAb])
                kb.op("pool", lambda h: h.tensor_scalar_mul(out=MUB[:, 0, :], in0=mu[:, 1, g0:g0 + GB], scalar1=-1.0), reads=[mub], writes=[MUBb])
                kb.op("pool", lambda h: h.tensor_copy(MUB[:, 1, :], mu[:, 1, g0:g0 + GB]), reads=[mub], writes=[MUBb])
                kb.op("pool", lambda h: h.tensor_copy(MLA[:, 0, :], mup[:, 0, g0:g0 + GB, LB - 1]), reads=[mupb], writes=[MLAb])
                kb.op("pool", lambda h: h.tensor_copy(MLA[:, 1, :], mup[:, 0, g0:g0 + GB, LB - 1]), reads=[mupb], writes=[MLAb])
                kb.op("pool", lambda h: h.tensor_scalar_mul(out=MLB[:, 0, :], in0=mup[:, 1, g0:g0 + GB, LB - 1], scalar1=-1.0), reads=[mupb], writes=[MLBb])
                kb.op("pool", lambda h: h.tensor_copy(MLB[:, 1, :], mup[:, 1, g0:g0 + GB, LB - 1]), reads=[mupb], writes=[MLBb])
                run_pass(0, 1, None, "A", g0)
                run_pass(1, 0, "A", "B", g0)
                emit_y(0, "A", "B", g0)
                emit_y(1, "B", "A", g0)
                run_pass(2, 2, None, "C", g0)
                emit_y(2, "C", "C", g0)
        kb.barrier()

    def phase_glu(self):
        kb = self.kb
        Tn, NCH = self.T, self.NCH
        g = self.gtiles
        bglu, bglub = g["bglu"]
        with contextlib.ExitStack() as es:
            wg = self.sb(es, "g_w", [128, 8, 1024], BF16)
            wgb = Buf()
            wst = [self.sb(es, "g_wst%d" % i, [128, 1024], F32) for i in range(2)]
            wstb = [Buf(), Buf()]
            for kc in range(8):
                kb.dma(wst[kc % 2][:], self.w_glu[kc * 128:(kc + 1) * 128, :], writes=[wstb[kc % 2]])
                kb.op("pool", lambda h: h.tensor_copy(wg[:, kc, :], wst[kc % 2][:]), reads=[wstb[kc % 2]], writes=[wgb])
            yt = [self.sb(es, "g_yt%d" % i, [128, 8, TT], BF16) for i in range(2)]
            at = [self.sb(es, "g_at%d" % i, [128, 8, TT], BF16) for i in range(2)]
            ytb = [Buf(), Buf()]
            atb = [Buf(), Buf()]
            zst = [self.sb(es, "g_z%d" % i, [128, Tn], BF16) for i in range(8)]
            zstb = [Buf() for _ in range(8)]
            sg = [self.sb(es, "g_sg%d" % i, [128, TT], F32) for i in range(2)]
            sgb = [Buf(), Buf()]
            JT = TT // NCH
            n = 0
            it = 0
            for s in range(NSEG):
                for ct in range(Tn // TT):
                    sl = it % 2
                    it += 1
                    kb.dma(yt[sl][:], self.YT[s, :, ct * TT:(ct + 1) * TT].rearrange("(k p) n -> p k n", p=128), writes=[ytb[sl]])
                    kb.dma(at[sl][:], self.AG[s, :, ct * TT:(ct + 1) * TT].rearrange("(k p) n -> p k n", p=128), writes=[atb[sl]])
                    for fo in range(8):
                        pi = n % 3
                        e = n % 2
                        n += 1
                        fns = [(lambda h, kc=kc: h.matmul(self.ps[pi][:, 0:TT], wg[:, kc, fo * 128:(fo + 1) * 128], yt[sl][:, kc, :],
                                                          start=(kc == 0), stop=(kc == 7))) for kc in range(8)]
                        kb.group("pe", fns, reads=[wgb, ytb[sl]], writes=[self.psb[pi]])
                        kb.op("act", lambda h: h.activation(out=sg[e][:], in_=self.ps[pi][:, 0:TT], func=AF.Sigmoid, bias=bglu[:, fo:fo + 1]),
                              reads=[self.psb[pi], bglub], writes=[sgb[e]])
                        kb.op("dve", lambda h: h.tensor_tensor(out=sg[e][:], in0=sg[e][:], in1=yt[sl][:, fo, :], op=ALU.mult),
                              reads=[sgb[e], ytb[sl]], writes=[sgb[e]])
                        outv = zst[fo][:].rearrange("p (m j) -> p j m", j=8)[:, ct * JT:(ct + 1) * JT, :]
                        kb.op("dve", lambda h: h.tensor_tensor(out=outv, in0=sg[e][:].rearrange("p (j m) -> p j m", j=JT),
                                                               in1=at[sl][:, fo, :].rearrange("p (j m) -> p j m", j=JT), op=ALU.mult),
                              reads=[sgb[e], atb[sl]], writes=[zstb[fo]])
                for fo in range(8):
                    kb.dma(self.ZT[s, fo * 128:(fo + 1) * 128, :], zst[fo][:], reads=[zstb[fo]])
        kb.barrier()

    def attn_setup(self, es, pfx):
        st = {
            "PT": [self.sb(es, pfx + "PT%d" % i, [128, TT], BF16) for i in range(4)],
            "PTb": [Buf() for _ in range(4)],
            "RL": [self.sb(es, pfx + "RL%d" % i, [128, TT], F32) for i in range(2)],
            "RLb": [Buf(), Buf()],
            "n": 0, "u": 0,
        }
        return st

    def attn_unit(self, st, q_ap, qb, blocks, scale, gate_ap, gateb, out_ap, outb, after=None):
        kb = self.kb
        ones, onesb = self.gtiles["ones"]
        u = st["u"] % 2
        st["u"] += 1
        pO, pL = 4 + u, 6 + u
        nb = len(blocks)
        LOOK = 3
        slots = {}

        def rng(blk):
            return (blk[8], blk[9]) if len(blk) > 8 else (0, TT)

        def emit_s(i):
            k_ap, kbuf, v_ap, vbuf, b_ap, bbuf, m_ap, mbuf = blocks[i][:8]
            c0, c1 = rng(blocks[i])
            pS = st["n"] % 4
            sl = st["n"] % 4
            st["n"] += 1
            slots[i] = sl
            PT, PTb = st["PT"][sl], st["PTb"][sl]
            kb.group("pe", [lambda h: h.matmul(self.ps[pS][:, c0:c1], k_ap, q_ap[:, c0:c1], start=True, stop=True)], reads=[kbuf, qb], writes=[self.psb[pS]])
            kb.op("act", lambda h: h.activation(out=PT[:, c0:c1], in_=self.ps[pS][:, c0:c1], func=AF.Exp, scale=scale, bias=b_ap),
                  reads=[self.psb[pS], bbuf], writes=[PTb])
            if m_ap is not None:
                en = "pool" if (i % 3 == 2) else "dve"
                kb.op(en, lambda h: h.tensor_tensor(out=PT[:, c0:c1], in0=PT[:, c0:c1], in1=m_ap[:, c0:c1], op=ALU.mult), reads=[PTb, mbuf], writes=[PTb])

        def emit_pv(i):
            k_ap, kbuf, v_ap, vbuf, b_ap, bbuf, m_ap, mbuf = blocks[i][:8]
            c0, c1 = rng(blocks[i])
            sl = slots[i]
            PT, PTb = st["PT"][sl], st["PTb"][sl]
            kb.group("pe", [lambda h: h.matmul(self.ps[pO][:, c0:c1], v_ap, PT[:, c0:c1], start=(i == 0), stop=(i == nb - 1)),
                            lambda h: h.matmul(self.ps[pL][:, c0:c1], ones[:], PT[:, c0:c1], start=(i == 0), stop=(i == nb - 1))],
                     reads=[vbuf, PTb, onesb], writes=[self.psb[pO], self.psb[pL]])

        for i in range(nb + LOOK):
            if i < nb:
                emit_s(i)
            if i >= LOOK:
                emit_pv(i - LOOK)
            fin_prev = st.get("fin")
            if fin_prev is not None:
                if nb >= 8:
                    if 2 <= i < 6:
                        fin_prev(i - 2)
                    if i == 6:
                        fin_prev(4)
                        st.pop("fin")
                elif i == nb - 1:
                    for c in range(5):
                        fin_prev(c)
                    st.pop("fin")
        RL, RLb = st["RL"][u], st["RLb"][u]

        def fin(c):
            if c < 4:
                cs_ = slice(c * 128, (c + 1) * 128)
                kb.op("dve", lambda h: h.reciprocal(RL[:, cs_], self.ps[pL][:, cs_]), reads=[self.psb[pL]], writes=[RLb])
                kb.op("dve", lambda h: h.tensor_tensor(out=RL[:, cs_], in0=RL[:, cs_], in1=self.ps[pO][:, cs_], op=ALU.mult), reads=[RLb, self.psb[pO]], writes=[RLb])
            else:
                kb.op("dve", lambda h: h.tensor_tensor(out=out_ap, in0=RL[:], in1=gate_ap, op=ALU.mult), reads=[RLb, gateb], writes=[outb])
                if after is not None:
                    after()
        st["fin"] = fin

    def run_units(self, units):
        n = len(units)
        if n:
            units[0][0]()
        for i in range(n):
            if i + 1 < n:
                units[i + 1][0]()
            units[i][1]()
            units[i][2]()

    def phase_gqa(self):
        kb = self.kb
        Tn, NB, NTT = self.T, self.NB, self.NTT
        g = self.gtiles
        misc, miscb = g["misc"]
        scale = 128.0 ** -0.5
        with contextlib.ExitStack() as es:
            st = self.attn_setup(es, "a_")
            kt = [self.sb(es, "a_kt%d" % i, [128, 2, Tn], BF16) for i in range(2)]
            vt = [self.sb(es, "a_vt%d" % i, [128, 2, NB, 128], BF16) for i in range(2)]
            ktb, vtb = [Buf(), Buf()], [Buf(), Buf()]
            qt_ = [self.sb(es, "a_q%d" % i, [128, TT], BF16) for i in range(3)]
            gt_ = [self.sb(es, "a_g%d" % i, [128, TT], BF16) for i in range(3)]
            zt_ = [self.sb(es, "a_z%d" % i, [128, TT], BF16) for i in range(3)]
            qtb, gtb, ztb = [Buf() for _ in range(3)], [Buf() for _ in range(3)], [Buf() for _ in range(3)]
            units = []
            it = 0
            hn = 0
            for segs in ((0, 1), (2,)):
                for kh in range(2):
                    hs = hn % 2
                    hn += 1
                    first = True
                    for li, s in enumerate(segs):
                        for h in range(4 * kh, 4 * kh + 4):
                            for tq in range(NTT):
                                sl3 = it % 3
                                sl2 = it % 3
                                it += 1

                                def load(first=first, hs=hs, segs=segs, kh=kh, s=s, h=h, tq=tq, sl3=sl3):
                                    if first:
                                        for lj, s2 in enumerate(segs):
                                            kb.dma(kt[hs][:, lj, :], self.KT[s2, kh], writes=[ktb[hs]])
                                            kb.dma(vt[hs][:, lj, :, :], dap(self.VV, s2 * Tn * 256 + kh * 128, [[256, 128], [128 * 256, NB], [1, 128]]), writes=[vtb[hs]])
                                    kb.dma(qt_[sl3][:], self.QT[s, h, :, tq * TT:(tq + 1) * TT], writes=[qtb[sl3]])
                                    kb.dma(gt_[sl3][:], self.BG[s, h * 128:(h + 1) * 128, tq * TT:(tq + 1) * TT], writes=[gtb[sl3]])

                                def compute(hs=hs, segs=segs, li=li, sl3=sl3, sl2=sl2, s=s, h=h, tq=tq):
                                    blocks = []
                                    for lj in range(len(segs)):
                                        b_ap = misc[:, 6:7] if lj == li else misc[:, 7:8]
                                        for b in range(NB):
                                            blocks.append((kt[hs][:, lj, b * 128:(b + 1) * 128], ktb[hs], vt[hs][:, lj, b, :], vtb[hs], b_ap, miscb, None, None))
                                    self.attn_unit(st, qt_[sl3][:], qtb[sl3], blocks, scale, gt_[sl3][:], gtb[sl3], zt_[sl2][:], ztb[sl2],
                                                   after=lambda: kb.dma(self.ZT[s, 1024 + h * 128:1024 + (h + 1) * 128, tq * TT:(tq + 1) * TT], zt_[sl2][:], reads=[ztb[sl2]]))
                                units.append((load, compute, lambda: None))
                                first = False
            self.run_units(units)
            for c in range(5):
                st["fin"](c)
            st.pop("fin")
        kb.barrier()

    def phase_dil(self):
        kb = self.kb
        Tn, NB, NTT = self.T, self.NB, self.NTT
        g = self.gtiles
        misc, miscb = g["misc"]
        fl, flb = g["flag"]
        scale = 128.0 ** -0.5
        with contextlib.ExitStack() as es:
            st = self.attn_setup(es, "d_")
            dm = self.sb(es, "d_mask", [128, 20, TT], BF16)
            dmb = Buf()
            kb.dma(dm[:], self.c_dmask.rearrange("i p n -> p i n"), writes=[dmb])
            kt = [self.sb(es, "d_kt%d" % i, [128, 2, Tn], BF16) for i in range(2)]
            vt = [self.sb(es, "d_vt%d" % i, [128, 2, NB, 128], BF16) for i in range(2)]
            ktb, vtb = [Buf(), Buf()], [Buf(), Buf()]
            qt_ = [self.sb(es, "d_q%d" % i, [128, TT], BF16) for i in range(3)]
            gt_ = [self.sb(es, "d_g%d" % i, [128, TT], BF16) for i in range(3)]
            zt_ = [self.sb(es, "d_z%d" % i, [128, TT], BF16) for i in range(3)]
            qtb, gtb, ztb = [Buf() for _ in range(3)], [Buf() for _ in range(3)], [Buf() for _ in range(3)]
            units = []
            it = 0
            hn = 0
            for segs in ((0, 1), (2,)):
                span = len(segs) * Tn
                for n in range(16):
                    hs = hn % 2
                    hn += 1
                    first = True
                    for li, s in enumerate(segs):
                        for tq in range(NTT):
                            sl3 = it % 3
                            sl2 = it % 3
                            it += 1

                            def load(first=first, hs=hs, segs=segs, n=n, s=s, tq=tq, sl3=sl3):
                                if first:
                                    for lj, s2 in enumerate(segs):
                                        kb.dma(kt[hs][:, lj, :], self.K2T[s2, n], writes=[ktb[hs]])
                                        kb.dma(vt[hs][:, lj, :, :], dap(self.V2, s2 * Tn * D + n * 128, [[D, 128], [128 * D, NB], [1, 128]]), writes=[vtb[hs]])
                                kb.dma(qt_[sl3][:], self.Q2T[s, n, :, tq * TT:(tq + 1) * TT], writes=[qtb[sl3]])
                                kb.dma(gt_[sl3][:], self.G2T[s, n * 128:(n + 1) * 128, tq * TT:(tq + 1) * TT], writes=[gtb[sl3]])

                            def compute(hs=hs, li=li, tq=tq, span=span, sl3=sl3, sl2=sl2, s=s, n=n):
                                Q0 = li * Tn + tq * TT
                                blocks = []
                                for mi in range(20):
                                    ks = Q0 - 1024 + mi * 128
                                    if ks < 0 or ks >= span:
                                        continue
                                    lj, b = ks // Tn, (ks % Tn) // 128
                                    b_ap = misc[:, 2:3] if lj == li else fl[:, 1:2]
                                    bb = miscb if lj == li else flb
                                    dd = mi * 128 - 1024
                                    c0, c1 = max(0, dd - 1024), min(TT, dd + 1152)
                                    blocks.append((kt[hs][:, lj, b * 128:(b + 1) * 128], ktb[hs], vt[hs][:, lj, b, :], vtb[hs], b_ap, bb, dm[:, mi, :], dmb, c0, c1))
                                full = [bk for bk in blocks if bk[9] - bk[8] == TT]
                                part = [bk for bk in blocks if bk[9] - bk[8] != TT]
                                if len(full) >= 2:
                                    blocks = [full[0]] + part + full[1:]
                                else:
                                    blocks = [bk[:8] for bk in blocks]
                                self.attn_unit(st, qt_[sl3][:], qtb[sl3], blocks, scale, gt_[sl3][:], gtb[sl3], zt_[sl2][:], ztb[sl2],
                                               after=lambda: kb.dma(self.ZT[s, n * 128:(n + 1) * 128, tq * TT:(tq + 1) * TT], zt_[sl2][:], reads=[ztb[sl2]]))
                            units.append((load, compute, lambda: None))
                            first = False
            self.run_units(units)
            for c in range(5):
                st["fin"](c)
            st.pop("fin")
        kb.barrier()

    def phase_out(self, layer):
        kb = self.kb
        Tn, NB = self.T, self.NB
        g = self.gtiles
        misc, miscb = g["misc"]
        W = self.w_out_e if layer == 0 else self.w_out_o
        with contextlib.ExitStack() as es:
            wo = self.sb(es, "o_w", [128, KC, D], BF16)
            wob = Buf()
            xr_ = [self.sb(es, "o_xr%d" % i, [128, D], F32) for i in range(2)]
            xrb = [Buf(), Buf()]
            for kc in range(KC):
                kb.dma(xr_[kc % 2][:], W[kc * 128:(kc + 1) * 128, :], writes=[xrb[kc % 2]])
                kb.op("pool", lambda h: h.tensor_copy(wo[:, kc, :], xr_[kc % 2][:]), reads=[xrb[kc % 2]], writes=[wob])
            lg = self.sb(es, "o_lg", [128, D], F32)
            lb = self.sb(es, "o_lb", [128, D], F32)
            lgb = Buf()
            kb.dma(lg[:], dap(self.ln_g, layer * D, [[0, 128], [1, D]]), writes=[lgb])
            kb.dma(lb[:], dap(self.ln_b, layer * D, [[0, 128], [1, D]]), writes=[lgb])
            zb_ = [self.sb(es, "o_zb%d" % i, [128, KC, TT], BF16) for i in range(2)]
            zbb = [Buf(), Buf()]
            v = self.sb(es, "o_v", [128, D], F32)
            vb = Buf()
            xo = [self.sb(es, "o_xo%d" % i, [128, D], F32) for i in range(2)]
            xob = [Buf(), Buf()]
            stt_ = self.sb(es, "o_st", [128, 8], F32)
            stb = Buf()
            xbf = [self.sb(es, "o_xbf%d" % i, [128, D], BF16) for i in range(3)]
            xbfb = [Buf(), Buf(), Buf()]
            xTb_ = [self.sb(es, "o_xT%d" % i, [128, KC, TT], BF16) for i in range(2)]
            xTbb = [Buf(), Buf()]
            units = []
            it = 0
            hcnt = [0]
            src = self.x if layer == 0 else self.X1
            for s in range(NSEG):
                for b in range(NB):
                    sl = it % 2
                    it += 1

                    zs = (it - 1) // 4 % 2
                    bq = b % 4

                    s3 = (it - 1) % 3

                    def load(s=s, b=b, sl=sl):
                        kb.dma(xr_[sl][:], src[s, b * 128:(b + 1) * 128, :], writes=[xrb[sl]])

                    def loadz(s=s, b=b, zs=zs):
                        kb.dma(zb_[zs][:], self.ZT[s, :, b * 128:b * 128 + TT].rearrange("(k p) n -> p k n", p=128), writes=[zbb[zs]])

                    def compute(s=s, b=b, sl=sl, zs=zs, bq=bq, s3=s3):
                        if layer == 1:
                            halves = [[4 * sl + i for i in range(4)]]
                        else:
                            halves = []
                            for hf in range(2):
                                pp = (2 * hcnt[0]) % 6
                                hcnt[0] += 1
                                halves.append([pp, pp + 1])
                        nt0 = 0
                        for banks in halves:
                            fns = []
                            for kc in range(KC):
                                for j, pbk in enumerate(banks):
                                    nt = nt0 + j
                                    fns.append(lambda h, kc=kc, nt=nt, pbk=pbk: h.matmul(self.ps[pbk][:, 0:512], zb_[zs][:, kc, bq * 128:(bq + 1) * 128], wo[:, kc, nt * 512:(nt + 1) * 512],
                                                                                     start=(kc == 0), stop=(kc == KC - 1)))
                            kb.group("pe", fns, reads=[zbb[zs], wob], writes=[self.psb[p_] for p_ in banks])
                            for j, pbk in enumerate(banks):
                                nt = nt0 + j
                                kb.op("dve", lambda h: h.scalar_tensor_tensor(out=v[:, nt * 512:(nt + 1) * 512], in0=xr_[sl][:, nt * 512:(nt + 1) * 512], scalar=ALPHA,
                                                                               in1=self.ps[pbk][:, 0:512], op0=ALU.mult, op1=ALU.add),
                                      reads=[xrb[sl], self.psb[pbk]], writes=[vb])
                            nt0 += len(banks)
                        kb.op("act", lambda h: h.activation(out=xbf[s3][:], in_=v[:], func=AF.Copy, accum_out=stt_[:, 0:1]), reads=[vb], writes=[xbfb[s3], stb])
                        kb.op("act", lambda h: h.activation(out=xbf[s3][:], in_=v[:], func=AF.Square, accum_out=stt_[:, 1:2]), reads=[vb], writes=[xbfb[s3], stb])
                        kb.op("dve", lambda h: h.tensor_scalar_mul(out=stt_[:, 2:3], in0=stt_[:, 0:1], scalar1=1.0 / D), reads=[stb], writes=[stb])
                        kb.op("dve", lambda h: h.tensor_tensor(out=stt_[:, 3:4], in0=stt_[:, 2:3], in1=stt_[:, 2:3], op=ALU.mult), reads=[stb], writes=[stb])
                        kb.op("dve", lambda h: h.scalar_tensor_tensor(out=stt_[:, 4:5], in0=stt_[:, 1:2], scalar=1.0 / D, in1=stt_[:, 3:4], op0=ALU.mult, op1=ALU.subtract),
                              reads=[stb], writes=[stb])
                        kb.op("act", lambda h: h.activation(out=stt_[:, 5:6], in_=stt_[:, 4:5], func=AF.Sqrt, bias=misc[:, 1:2]), reads=[stb, miscb], writes=[stb])
                        kb.op("dve", lambda h: h.reciprocal(stt_[:, 6:7], stt_[:, 5:6]), reads=[stb], writes=[stb])
                        xo_t, xo_b = xo[sl], xob[sl]
                        kb.op("dve", lambda h: h.scalar_tensor_tensor(out=xo_t[:], in0=v[:], scalar=stt_[:, 2:3], in1=lg[:], op0=ALU.subtract, op1=ALU.mult),
                              reads=[vb, stb, lgb], writes=[xo_b])
                        kb.op("dve", lambda h: h.scalar_tensor_tensor(out=xo_t[:], in0=xo_t[:], scalar=stt_[:, 6:7], in1=lb[:], op0=ALU.mult, op1=ALU.add),
                              reads=[xo_b, stb, lgb], writes=[xo_b])
                        if layer == 0:
                            kb.op("pool", lambda h: h.tensor_copy(xbf[s3][:], xo_t[:]), reads=[xo_b], writes=[xbfb[s3]])

                    def late(s=s, b=b, s3=s3, zs=zs, bq=bq):
                        if layer == 0:
                            self.transpose_block(xbf[s3], xbfb[s3], xTb_[zs], xTbb[zs], bq, "act" if b % 2 == 0 else "dve")
                            if bq == 3:
                                kb.dma(self.X1T[s, :, (b - 3) * 128:(b + 1) * 128].rearrange("(k p) n -> p k n", p=128), xTb_[zs][:], reads=[xTbb[zs]])

                    def store(s=s, b=b, sl=sl):
                        xo_t, xo_b = xo[sl], xob[sl]
                        if layer == 0:
                            kb.dma(self.X1[s, b * 128:(b + 1) * 128, :], xo_t[:], reads=[xo_b])
                        else:
                            kb.dma(self.y[s, b * 128:(b + 1) * 128, :], xo_t[:], reads=[xo_b])
                    units.append((load, compute, store, late, loadz))
            n = len(units)
            units[0][4]()
            units[0][0]()
            for i in range(n):
                if i + 1 < n:
                    units[i + 1][0]()
                if i % 4 == 0 and i + 4 < n:
                    units[i + 4][4]()
                units[i][1]()
                if i > 1:
                    units[i - 2][3]()
                units[i][2]()
            if n > 1:
                units[n - 2][3]()
            units[n - 1][3]()
        kb.barrier()

    def phase_odd_in(self, s):
        kb = self.kb
        Tn, NTT, NB = self.T, self.NTT, self.NB
        g = self.gtiles
        rotO, rotOb = g["rotO"]
        with contextlib.ExitStack() as es:
            xT = self.sb(es, "f_xT", [128, KC, Tn], BF16)
            xTb = [Buf() for _ in range(NB)]
            for tq in range(NTT):
                kb.dma(xT[:, :, tq * TT:(tq + 1) * TT], self.X1T[s, :, tq * TT:(tq + 1) * TT].rearrange("(k p) n -> p k n", p=128), writes=xTb[4 * tq:4 * tq + 4])
            cs = self.sb(es, "f_cs", [128, 2, Tn], F32)
            csb = Buf()
            kb.dma(cs[:, 0, :], self.cosO[s], writes=[csb])
            kb.dma(cs[:, 1, :], self.sinO[s], writes=[csb])
            stg = [self.sb(es, "f_stg%d" % i, [128, Tn], BF16) for i in range(2)]
            stgb = [Buf(), Buf()]
            vst = [self.sb(es, "f_vst%d" % i, [128, NB, 256], BF16) for i in range(2)]
            vstb = [Buf(), Buf()]
            qb_ = [self.sb(es, "f_qb%d" % i, [128, TT], BF16) for i in range(3)]
            t1 = [self.sb(es, "f_t1%d" % i, [128, TT], F32) for i in range(3)]
            t2 = [self.sb(es, "f_t2%d" % i, [128, TT], F32) for i in range(3)]
            tb = {k: [Buf(), Buf(), Buf()] for k in ("qb", "t1", "t2")}
            cnt = {"mm": 0, "ep": 0, "v": 0, "rc": 0}

            def mm_tile(wbf, wbfb, tt):
                pi = cnt["mm"] % 4
                cnt["mm"] += 1
                fns = [(lambda h, kc=kc: h.matmul(self.ps[pi][:, 0:TT], wbf[:, kc, 0:128], xT[:, kc, tt * TT:(tt + 1) * TT],
                                                  start=(kc == 0), stop=(kc == KC - 1))) for kc in range(KC)]
                kb.group("pe", fns, reads=[wbfb] + xTb[4 * tt:4 * tt + 4], writes=[self.psb[pi]])
                return pi

            def chunk_silu(fo, dst_ap):
                def run(wbf, wbfb):
                    run_due(flush=True)
                    sl = fo % 2
                    for tt in range(NTT):
                        pi = mm_tile(wbf, wbfb, tt)
                        kb.op("act", lambda h: h.activation(out=stg[sl][:, tt * TT:(tt + 1) * TT], in_=self.ps[pi][:, 0:TT], func=AF.Silu),
                              reads=[self.psb[pi]], writes=[stgb[sl]])
                    kb.dma(dst_ap, stg[sl][:], reads=[stgb[sl]])
                return run

            pend = []
            clock = {"k": 0}

            def run_due(flush=False):
                while pend and (flush or pend[0][0] <= clock["k"]):
                    pend.pop(0)[1]()

            def chunk_rope(fo, dst_ap):
                def run(wbf, wbfb):
                    sl = cnt["rc"] % 2
                    cnt["rc"] += 1
                    for tt in range(NTT):
                        k = clock["k"]
                        pi = mm_tile(wbf, wbfb, tt)
                        e = k % 3
                        pr = 4 + (k % 2)
                        kb.op("act", lambda h: h.copy(qb_[e][:], self.ps[pi][:, 0:TT]), reads=[self.psb[pi]], writes=[tb["qb"][e]])

                        def stage_b(pi=pi, e=e, pr=pr, tt=tt, sl=sl):
                            kb.group("pe", [lambda h: h.matmul(self.ps[pr][:, 0:TT], rotO[:], qb_[e][:], start=True, stop=True)],
                                     reads=[tb["qb"][e], rotOb], writes=[self.psb[pr]])
                            kb.op("dve", lambda h: h.tensor_tensor(out=t1[e][:], in0=self.ps[pi][:, 0:TT], in1=cs[:, 0, tt * TT:(tt + 1) * TT], op=ALU.mult),
                                  reads=[self.psb[pi], csb], writes=[tb["t1"][e]])
                            kb.op("dve", lambda h: h.tensor_tensor(out=t2[e][:], in0=self.ps[pr][:, 0:TT], in1=cs[:, 1, tt * TT:(tt + 1) * TT], op=ALU.mult),
                                  reads=[self.psb[pr], csb], writes=[tb["t2"][e]])
                            kb.op("dve", lambda h: h.tensor_tensor(out=stg[sl][:, tt * TT:(tt + 1) * TT], in0=t1[e][:], in1=t2[e][:], op=ALU.add),
                                  reads=[tb["t1"][e], tb["t2"][e]], writes=[stgb[sl]])
                        pend.append((k + 1, stage_b))
                        if tt == NTT - 1:
                            pend.append((k + 1, lambda sl=sl: kb.dma(dst_ap, stg[sl][:], reads=[stgb[sl]])))
                        clock["k"] += 1
                        run_due()
                return run

            def chunk_v(vc):
                def run(wbf, wbfb):
                    run_due(flush=True)
                    sl = vc % 2
                    for b in range(NB):
                        pi = cnt["mm"] % 4
                        cnt["mm"] += 1
                        fns = [(lambda h, kc=kc: h.matmul(self.ps[pi][:, 0:256], xT[:, kc, b * 128:(b + 1) * 128], wbf[:, kc, 0:256],
                                                          start=(kc == 0), stop=(kc == KC - 1))) for kc in range(KC)]
                        kb.group("pe", fns, reads=[wbfb, xTb[b]], writes=[self.psb[pi]])
                        kb.op("act", lambda h: h.copy(vst[sl][:, b, :], self.ps[pi][:, 0:256]), reads=[self.psb[pi]], writes=[vstb[sl]])
                    kb.dma(dap(self.V2, s * Tn * D + vc * 256, [[D, 128], [128 * D, NB], [1, 256]]), vst[sl][:], reads=[vstb[sl]])
                return run

            specs = []
            for fo in range(16):
                specs.append((fo * 128, 128, chunk_rope(fo, self.Q2T[s, fo])))
            for fo in range(16):
                specs.append((2048 + fo * 128, 128, chunk_rope(fo, self.K2T[s, fo])))
            for vc in range(8):
                specs.append((4096 + vc * 256, 256, chunk_v(vc)))
            for fo in range(16):
                specs.append((6144 + fo * 128, 128, chunk_silu(fo, self.G2T[s, fo * 128:(fo + 1) * 128, :])))
            import os
            sel = os.environ.get("ODD_SEL")
            if sel:
                a, b = [int(v) for v in sel.split(":")]
                specs = specs[a:b]
            self.linear(es, self.w_in_o, ODD_IN, xT, xTb, specs, "f_")
            run_due(flush=True)
        kb.barrier()

    def build(self):
        nc = self.nc

        def ph(name, fn, *a):
            with nc.named_scope(name):
                fn(*a)
        ph("consts", self.load_consts)
        ph("ssm_pre", self.phase_ssm_pre)
        for s in range(NSEG):
            ph("even_in%d" % s, self.phase_even_in, s)
        ph("ssm", self.phase_ssm)
        ph("glu", self.phase_glu)
        ph("gqa", self.phase_gqa)
        ph("out0", self.phase_out, 0)
        for s in range(NSEG):
            ph("odd_in%d" % s, self.phase_odd_in, s)
        ph("dil", self.phase_dil)
        ph("out1", self.phase_out, 1)
        self.kb.finish()
        return self.nc


def _rot_tables(Tn, pos0, kind):
    f32 = np.float32
    pos = (np.arange(Tn, dtype=f32) + f32(pos0))
    cos = np.ones((128, Tn), f32)
    sin = np.zeros((128, Tn), f32)
    if kind == "E":
        n = 64
        inv = (f32(10000.0) ** (-(np.arange(0, n, 2, dtype=f32) / f32(n)))).astype(f32)
        row = np.floor(pos / f32(64)).astype(f32)
        col = (pos - row * f32(64)).astype(f32)
        for half, pp in ((0, row), (1, col)):
            ang = (pp[None, :] * inv[:, None]).astype(f32)
            c, s_ = np.cos(ang).astype(f32), np.sin(ang).astype(f32)
            cos[half * 64:half * 64 + 32] = c
            cos[half * 64 + 32:half * 64 + 64] = c
            sin[half * 64:half * 64 + 32] = s_
            sin[half * 64 + 32:half * 64 + 64] = s_
    else:
        n = 32
        inv = (f32(500000.0) ** (-(np.arange(0, n, 2, dtype=f32) / f32(n)))).astype(f32)
        ang = (pos[None, :] * inv[:, None]).astype(f32)
        c, s_ = np.cos(ang).astype(f32), np.sin(ang).astype(f32)
        cos[0:16] = c
        cos[16:32] = c
        sin[0:16] = s_
        sin[16:32] = s_
    return cos, sin


def _rot_mats():
    bf = ml_dtypes.bfloat16
    rE = np.zeros((128, 128), np.float32)
    for m in range(128):
        if (m % 64) < 32:
            rE[m + 32, m] = -1.0
        else:
            rE[m - 32, m] = 1.0
    rO = np.zeros((128, 128), np.float32)
    for m in range(32):
        if m < 16:
            rO[m + 16, m] = -1.0
        else:
            rO[m - 16, m] = 1.0
    return rE.astype(bf), rO.astype(bf)


def _dil_masks():
    out = np.zeros((20, 128, 512), np.float32)
    k = np.arange(128)[:, None]
    q = np.arange(512)[None, :]
    for i in range(20):
        dl = (i * 128 - 1024) + k - q
        a = np.abs(dl)
        m = (a <= 64).astype(np.float32)
        m += ((a <= 256) & (dl % 4 == 0)).astype(np.float32)
        m += ((a <= 1024) & (dl % 16 == 0)).astype(np.float32)
        out[i] = m
    return out.astype(ml_dtypes.bfloat16)


def _mmasks():
    jc = np.arange(128) // 16
    mf = (jc[:, None] <= jc[None, :]).astype(np.float32)
    mb = (jc[:, None] >= jc[None, :]).astype(np.float32)
    return np.stack([mf, mb], 0)


def make_core_inputs(inputs, Tn=T):
    xp = np.asarray(inputs["x_prompt"])
    xs = np.asarray(inputs["x_sample"])
    rE, rO = _rot_mats()
    shared = {
        "w_in_e": np.ascontiguousarray(inputs["even_w_in"][0]), "w_out_e": np.ascontiguousarray(inputs["even_w_out"][0]),
        "w_glu": np.ascontiguousarray(inputs["ssm_glu_w"][0]), "b_glu": np.ascontiguousarray(inputs["ssm_glu_b"][0]),
        "w_in_o": np.ascontiguousarray(inputs["odd_w_in"][0]), "w_out_o": np.ascontiguousarray(inputs["odd_w_out"][0]),
        "ln_g": np.ascontiguousarray(inputs["ln_g"]), "ln_b": np.ascontiguousarray(inputs["ln_b"]),
        "a_re": np.ascontiguousarray(inputs["ssm_a_re"][0]), "a_im": np.ascontiguousarray(inputs["ssm_a_im"][0]),
        "log_dt": np.ascontiguousarray(inputs["ssm_log_dt"][0]),
        "b_re": np.ascontiguousarray(inputs["ssm_b_re"][0]), "b_im": np.ascontiguousarray(inputs["ssm_b_im"][0]),
        "c_re": np.ascontiguousarray(inputs["ssm_c_re"][0]), "c_im": np.ascontiguousarray(inputs["ssm_c_im"][0]),
        "d_skip": np.ascontiguousarray(inputs["ssm_d"][0]),
        "gq": np.ascontiguousarray(inputs["attn_q_norm"][0]), "gk": np.ascontiguousarray(inputs["attn_k_norm"][0]),
        "identf": np.eye(128, dtype=np.float32), "identb": np.eye(128, dtype=np.float32).astype(ml_dtypes.bfloat16),
        "rotE": rE, "rotO": rO, "dmask": _dil_masks(), "mmask": _mmasks(),
    }
    shared = {k: np.asarray(v, dtype=v.dtype if v.dtype != np.float64 else np.float32) for k, v in shared.items()}
    tabs = {}
    for linked in (0, 1):
        p0 = [0, Tn * linked, 0]
        cE, sE, cO, sO = [], [], [], []
        for sg in range(NSEG):
            c, s_ = _rot_tables(Tn, p0[sg], "E")
            cE.append(c); sE.append(s_)
            c, s_ = _rot_tables(Tn, p0[sg], "O")
            cO.append(c); sO.append(s_)
        fl = np.zeros((128, 2), np.float32)
        fl[:, 0] = float(linked)
        fl[:, 1] = 0.0 if linked else NEG
        tabs[linked] = {"cosE": np.stack(cE), "sinE": np.stack(sE), "cosO": np.stack(cO), "sinO": np.stack(sO), "flag": fl}
    maps = []
    for c in range(8):
        if c < 4:
            xx = np.stack([xp[c, 0:Tn], xp[c, Tn:2 * Tn], xs[c, 0:Tn]], 0)
            linked = 1
        else:
            b0 = 4 + 3 * (c - 4)
            xx = np.stack([xs[b0, 0:Tn], xs[b0 + 1, 0:Tn], xs[b0 + 2, 0:Tn]], 0)
            linked = 0
        m = dict(shared)
        m.update(tabs[linked])
        m["x"] = np.ascontiguousarray(xx, dtype=np.float32)
        maps.append(m)
    return maps


_PROG_CACHE = {}


def kernel(**inputs):
    maps = make_core_inputs(inputs, T)
    if "nc" not in _PROG_CACHE:
        _PROG_CACHE["nc"] = Prog(T_=T).build()
    nc = _PROG_CACHE["nc"]
    res = run_bass_kernel_spmd(nc, maps, core_ids=list(range(8)))
    yp = np.empty((4, 2 * T, D), np.float32)
    ys = np.empty((16, T, D), np.float32)
    for c in range(8):
        y = np.asarray(res.results[c]["y"])
        if c < 4:
            yp[c, 0:T] = y[0]
            yp[c, T:2 * T] = y[1]
            ys[c] = y[2]
        else:
            b0 = 4 + 3 * (c - 4)
            ys[b0], ys[b0 + 1], ys[b0 + 2] = y[0], y[1], y[2]
    return (yp, ys)
```
